# Optimizing a Trainium2 kernel written in Bass

```python
import jax
import jax.numpy as jnp
from jax import lax
import numpy as np

D_MODEL = 2048
BATCH = 4
SEQ = 4096
DEPTH = 2

GRID_W = 64
CTX_LEN = 256
N_MOD = 9
D_FF = 5632
RMS_EPS = 1e-6
ROPE_BASE = 10000.0

GLA_HEADS = 4
GLA_DK = D_MODEL // 16
GLA_DV = D_MODEL // 8
GLA_LOWRANK = 16
GLA_TAU = 16.0
GLA_CHUNK = 64
FNET_GROUPS = 4
FNET_GROUP_DIM = D_MODEL // 8
GLA_QK_W = GLA_HEADS * GLA_DK
GLA_V_W = GLA_HEADS * GLA_DV
FNET_W = FNET_GROUPS * FNET_GROUP_DIM
AB_SPLITS = (GLA_QK_W, 2 * GLA_QK_W, 2 * GLA_QK_W + GLA_V_W,
             2 * GLA_QK_W + GLA_V_W + GLA_LOWRANK, 2 * GLA_QK_W + GLA_V_W + 2 * GLA_LOWRANK,
             2 * GLA_QK_W + 2 * GLA_V_W + 2 * GLA_LOWRANK)
AB_IN_W = AB_SPLITS[-1] + FNET_W
AB_MIX_W = GLA_V_W + FNET_W

NA_HEADS = D_MODEL // 64
NA_HEAD_DIM = D_MODEL // NA_HEADS
NA_KH_MAX = 8
NA_KW = 16

kernel_name = 'hybrid_gla_fnet_natten_macaron_dit'

F32 = jnp.float32


def rms_norm(x, g):
    x32 = x.astype(F32)
    y = x32 * lax.rsqrt(jnp.mean(x32 * x32, axis=-1, keepdims=True) + RMS_EPS)
    return (y * g.astype(F32)).astype(x.dtype)


def modulate(h, shift, scale):
    return h * (1 + scale) + shift


def swiglu(h, w_gu, w_down):
    a, u = jnp.split(h @ w_gu, 2, axis=-1)
    return (jax.nn.silu(a) * u) @ w_down


def split_heads(z, n_heads):
    b, t, hd = z.shape
    return z.reshape(b, t, n_heads, hd // n_heads).transpose(0, 2, 1, 3)


def merge_heads(z):
    b, h, t, d = z.shape
    return z.transpose(0, 2, 1, 3).reshape(b, t, h * d)


def _rope_axis(x, pos):
    half = x.shape[-1] // 2
    inv_freq = ROPE_BASE ** (-jnp.arange(half, dtype=F32) / half)
    ang = pos.astype(F32)[:, None] * inv_freq[None, :]
    cos, sin = jnp.cos(ang), jnp.sin(ang)
    x1 = x[..., :half].astype(F32)
    x2 = x[..., half:].astype(F32)
    return jnp.concatenate([x1 * cos - x2 * sin, x1 * sin + x2 * cos], axis=-1).astype(x.dtype)


def axial_rope_2d(x):
    t = jnp.arange(x.shape[-2])
    row, col = t // GRID_W, t % GRID_W
    h = x.shape[-1] // 2
    return jnp.concatenate([_rope_axis(x[..., :h], row), _rope_axis(x[..., h:], col)], axis=-1)


def gla_chunked(q, k, v, g, s0):
    b, h, t, dk = q.shape
    dv = v.shape[-1]
    n = t // GLA_CHUNK

    def to_chunks(a):
        return a.reshape(b, h, n, GLA_CHUNK, a.shape[-1]).transpose(2, 0, 1, 3, 4).astype(F32)

    causal = jnp.tril(jnp.ones((GLA_CHUNK, GLA_CHUNK), dtype=bool))

    def step(s, inp):
        qi, ki, vi, gi = inp
        bc = jnp.cumsum(gi, axis=2)
        o_inter = jnp.einsum('bhtk,bhkv->bhtv', qi * jnp.exp(bc), s)
        diff = bc[:, :, :, None, :] - bc[:, :, None, :, :]
        decay = jnp.exp(jnp.where(causal[:, :, None], diff, -jnp.inf))
        att = jnp.einsum('bhtk,bhsk,bhtsk->bhts', qi, ki, decay)
        o = o_inter + jnp.einsum('bhts,bhsv->bhtv', att, vi)
        b_last = bc[:, :, -1:, :]
        s_new = (jnp.exp(b_last[:, :, 0, :])[..., None] * s
                 + jnp.einsum('bhsk,bhsv->bhkv', ki * jnp.exp(b_last - bc), vi))
        return s_new, o

    s_fin, oc = lax.scan(step, s0.astype(F32), (to_chunks(q), to_chunks(k), to_chunks(v), to_chunks(g)))
    return oc.transpose(1, 2, 0, 3, 4).reshape(b, h, t, dv), s_fin


def gla_bidirectional(q, k, v, g_f, g_b, s_f0, s_b0):
    rev = lambda a: jnp.flip(a, axis=2)
    o_f, s_f = gla_chunked(q, k, v, g_f, s_f0)
    o_b, s_b = gla_chunked(rev(q), rev(k), rev(v), rev(g_b), s_b0)
    return o_f + rev(o_b), s_f, s_b


def _gla_fnet_project(u, w_in, gw_f, gb_f, gw_b, gb_b, rotary):
    q, k, v, lr_f, lr_b, r, f = jnp.split(u @ w_in, AB_SPLITS, axis=-1)
    q = split_heads(q, GLA_HEADS) * (GLA_DK ** -0.5)
    k = split_heads(k, GLA_HEADS)
    if rotary:
        q, k = axial_rope_2d(q), axial_rope_2d(k)
    v = split_heads(v, GLA_HEADS)
    g_f = split_heads(jax.nn.log_sigmoid((lr_f @ gw_f + gb_f).astype(F32)) / GLA_TAU, GLA_HEADS)
    g_b = split_heads(jax.nn.log_sigmoid((lr_b @ gw_b + gb_b).astype(F32)) / GLA_TAU, GLA_HEADS)
    b, t, _ = f.shape
    fg = f.reshape(b, t, FNET_GROUPS, FNET_GROUP_DIM).astype(F32)
    f_mix = jnp.fft.fft2(fg, axes=(1, 3), norm='ortho').real.reshape(b, t, FNET_W).astype(u.dtype)
    return q, k, v, g_f, g_b, r, f_mix


def _gla_fnet_out(o, r, f_mix, g_norm, w_out):
    o = o * lax.rsqrt(jnp.mean(o * o, axis=-1, keepdims=True) + RMS_EPS)
    o = merge_heads(o) * g_norm.astype(F32) * jax.nn.silu(r.astype(F32))
    return jnp.concatenate([o.astype(f_mix.dtype), f_mix], axis=-1) @ w_out


def mix_gla_fnet(u, uc, need_ctx, w_in, gw_f, gb_f, gw_b, gb_b, g_norm, w_out):
    qc, kc, vc, gfc, gbc, rc, fc = _gla_fnet_project(uc, w_in, gw_f, gb_f, gw_b, gb_b, rotary=False)
    zeros = jnp.zeros((uc.shape[0], GLA_HEADS, GLA_DK, GLA_DV), F32)
    oc, s_f, s_b = gla_bidirectional(qc, kc, vc, gfc, gbc, zeros, zeros)
    q, k, v, gf, gb, r, f = _gla_fnet_project(u, w_in, gw_f, gb_f, gw_b, gb_b, rotary=True)
    o, _, _ = gla_bidirectional(q, k, v, gf, gb, s_f, s_b)
    y = _gla_fnet_out(o, r, f, g_norm, w_out)
    yc = _gla_fnet_out(oc, rc, fc, g_norm, w_out) if need_ctx else None
    return y, yc


def mix_neighbourhood(u, uc, need_ctx, w_qkv, rpb, w_out):
    b, t, _ = u.shape
    rows = t // GRID_W
    kh = min(NA_KH_MAX, rows)
    scale = NA_HEAD_DIM ** -0.5
    q, k, v = (split_heads(z, NA_HEADS) for z in jnp.split(u @ w_qkv, 3, axis=-1))
    q = q * scale
    kc, vc = (split_heads(z, NA_HEADS) for z in jnp.split(uc @ w_qkv[:, D_MODEL:], 2, axis=-1))
    grid = lambda z: z.reshape(b, NA_HEADS, rows, GRID_W, NA_HEAD_DIM)
    qg, kg, vg = grid(q), grid(k), grid(v)
    col = jnp.arange(GRID_W)
    col_start = jnp.clip(col - NA_KW // 2, 0, GRID_W - NA_KW)
    col_mask = (col[None, :] >= col_start[:, None]) & (col[None, :] < col_start[:, None] + NA_KW)
    dc_idx = jnp.clip(col[None, :] - col[:, None] + NA_KW - 1, 0, 2 * NA_KW - 2)
    rpb32 = rpb.astype(F32)
    n_loc = kh * GRID_W

    def row_block(r):
        r0 = jnp.clip(r - kh // 2, 0, rows - kh)
        kb = lax.dynamic_slice_in_dim(kg, r0, kh, axis=2)
        vb = lax.dynamic_slice_in_dim(vg, r0, kh, axis=2)
        qr = lax.dynamic_index_in_dim(qg, r, axis=2, keepdims=False)
        dr_idx = r0 + jnp.arange(kh) - r + NA_KH_MAX - 1
        bias = rpb32[:, dr_idx[:, None, None], dc_idx[None, :, :]].transpose(0, 2, 1, 3)
        s_loc = jnp.einsum('bhqd,bhikd->bhqik', qr, kb).astype(F32) + bias
        s_loc = jnp.where(col_mask[:, None, :], s_loc, -jnp.inf).reshape(b, NA_HEADS, GRID_W, n_loc)
        s_ctx = jnp.einsum('bhqd,bhcd->bhqc', qr, kc).astype(F32)
        p = jax.nn.softmax(jnp.concatenate([s_loc, s_ctx], axis=-1), axis=-1).astype(v.dtype)
        p_loc = p[..., :n_loc].reshape(b, NA_HEADS, GRID_W, kh, GRID_W)
        return (jnp.einsum('bhqik,bhikd->bhqd', p_loc, vb)
                + jnp.einsum('bhqc,bhcd->bhqd', p[..., n_loc:], vc))

    o = lax.map(row_block, jnp.arange(rows))
    o = o.transpose(1, 2, 0, 3, 4).reshape(b, NA_HEADS, t, NA_HEAD_DIM)
    y = merge_heads(o) @ w_out
    yc = None
    if need_ctx:
        qc = split_heads(uc @ w_qkv[:, :D_MODEL], NA_HEADS) * scale
        pc = jax.nn.softmax(jnp.einsum('bhqd,bhcd->bhqc', qc, kc).astype(F32), axis=-1).astype(vc.dtype)
        yc = merge_heads(jnp.einsum('bhqc,bhcd->bhqd', pc, vc)) @ w_out
    return y, yc


def trunk_layer(x, xc, c, c_ctx, common, mixer, last):
    ada_w, ada_b, n_ffn1, n_mix, n_ffn2, f1_gu, f1_down, f2_gu, f2_down = common
    m = jnp.split((jax.nn.silu(c) @ ada_w + ada_b)[:, None, :], N_MOD, axis=-1)
    mc = jnp.split((jax.nn.silu(c_ctx) @ ada_w + ada_b)[None, None, :], N_MOD, axis=-1)
    sub_in = lambda h, g, mm, j: modulate(rms_norm(h, g), mm[3 * j], mm[3 * j + 1])
    x = x + 0.5 * m[2] * swiglu(sub_in(x, n_ffn1, m, 0), f1_gu, f1_down)
    xc = xc + 0.5 * mc[2] * swiglu(sub_in(xc, n_ffn1, mc, 0), f1_gu, f1_down)
    y, yc = mixer(sub_in(x, n_mix, m, 1), sub_in(xc, n_mix, mc, 1), not last)
    x = x + m[5] * y
    x = x + 0.5 * m[8] * swiglu(sub_in(x, n_ffn2, m, 2), f2_gu, f2_down)
    if not last:
        xc = xc + mc[5] * yc
        xc = xc + 0.5 * mc[8] * swiglu(sub_in(xc, n_ffn2, mc, 2), f2_gu, f2_down)
    return x, xc


def setup_inputs(seed: int = 0) -> dict:
    key = jax.random.key(seed)
    keys = iter(jax.random.split(key, 48))
    D = D_MODEL

    def normal(shape, scale):
        return jax.random.normal(next(keys), shape, F32) * scale

    def gain(n):
        return 1.0 + 0.02 * jax.random.normal(next(keys), (n,), F32)

    def common(p):
        return {p + 'ada_w': normal((D, N_MOD * D), 0.5 * D ** -0.5),
                p + 'ada_b': normal((N_MOD * D,), 0.02),
                p + 'norm_ffn1': gain(D), p + 'norm_mix': gain(D), p + 'norm_ffn2': gain(D),
                p + 'ffn1_w_gu': normal((D, 2 * D_FF), D ** -0.5),
                p + 'ffn1_w_down': normal((D_FF, D), D_FF ** -0.5),
                p + 'ffn2_w_gu': normal((D, 2 * D_FF), D ** -0.5),
                p + 'ffn2_w_down': normal((D_FF, D), D_FF ** -0.5)}

    inputs = {'x': normal((BATCH, SEQ, D), 1.0),
              'c': normal((BATCH, D), 1.0),
              'ctx': normal((BATCH, CTX_LEN, D), 1.0),
              'c_ctx': normal((D,), 1.0)}
    inputs.update(common('l0_'))
    inputs.update({'l0_w_in': normal((D, AB_IN_W), D ** -0.5),
                   'l0_gla_gate_w_fwd': normal((GLA_LOWRANK, GLA_QK_W), GLA_LOWRANK ** -0.5),
                   'l0_gla_gate_b_fwd': normal((GLA_QK_W,), 0.02),
                   'l0_gla_gate_w_bwd': normal((GLA_LOWRANK, GLA_QK_W), GLA_LOWRANK ** -0.5),
                   'l0_gla_gate_b_bwd': normal((GLA_QK_W,), 0.02),
                   'l0_gla_norm': gain(GLA_V_W),
                   'l0_w_out': normal((AB_MIX_W, D), AB_MIX_W ** -0.5)})
    inputs.update(common('l1_'))
    inputs.update({'l1_w_qkv': normal((D, 3 * D), D ** -0.5),
                   'l1_rpb': normal((NA_HEADS, 2 * NA_KH_MAX - 1, 2 * NA_KW - 1), 0.1),
                   'l1_w_out': normal((D, D), D ** -0.5)})
    inputs['norm_out'] = gain(D)
    return inputs


def reference(x, c, ctx, c_ctx,
              l0_ada_w, l0_ada_b, l0_norm_ffn1, l0_norm_mix, l0_norm_ffn2,
              l0_ffn1_w_gu, l0_ffn1_w_down, l0_ffn2_w_gu, l0_ffn2_w_down,
              l0_w_in, l0_gla_gate_w_fwd, l0_gla_gate_b_fwd, l0_gla_gate_w_bwd, l0_gla_gate_b_bwd,
              l0_gla_norm, l0_w_out,
              l1_ada_w, l1_ada_b, l1_norm_ffn1, l1_norm_mix, l1_norm_ffn2,
              l1_ffn1_w_gu, l1_ffn1_w_down, l1_ffn2_w_gu, l1_ffn2_w_down,
              l1_w_qkv, l1_rpb, l1_w_out,
              norm_out):
    layers = [
        ((l0_ada_w, l0_ada_b, l0_norm_ffn1, l0_norm_mix, l0_norm_ffn2,
          l0_ffn1_w_gu, l0_ffn1_w_down, l0_ffn2_w_gu, l0_ffn2_w_down),
         lambda u, uc, need: mix_gla_fnet(u, uc, need, l0_w_in, l0_gla_gate_w_fwd, l0_gla_gate_b_fwd,
                                          l0_gla_gate_w_bwd, l0_gla_gate_b_bwd, l0_gla_norm, l0_w_out)),
        ((l1_ada_w, l1_ada_b, l1_norm_ffn1, l1_norm_mix, l1_norm_ffn2,
          l1_ffn1_w_gu, l1_ffn1_w_down, l1_ffn2_w_gu, l1_ffn2_w_down),
         lambda u, uc, need: mix_neighbourhood(u, uc, need, l1_w_qkv, l1_rpb, l1_w_out)),
    ]
    xc = ctx
    for i in range(DEPTH):
        common, mixer = layers[i]
        x, xc = trunk_layer(x, xc, c, c_ctx, common, mixer, last=(i == DEPTH - 1))
    return rms_norm(x, norm_out)
```

```python
import contextlib
import math
import numpy as np
import concourse.bass as bass
import concourse.mybir as mybir
from concourse.bass_utils import run_bass_kernel_spmd

F32 = mybir.dt.float32
BF16 = mybir.dt.bfloat16
AF = mybir.ActivationFunctionType
ALU = mybir.AluOpType
AX = mybir.AxisListType

D = 2048
KC = 16
DFF = 5632
JC = 44
NLAT = 4096
NCTX = 256
NTOK = NLAT + NCTX
EPS = 1e-6

ENGS = ("pe", "act", "dve", "pool", "sp")
SEM_LIMIT = 30000


class Buf:
    __slots__ = ("name", "w", "r", "dram")

    def __init__(self, name="", dram=False):
        self.name = name
        self.w = None
        self.r = {}
        self.dram = dram


class Sched:
    def __init__(self, nc, same_engine_sync=True):
        self.nc = nc
        self.gstack = contextlib.ExitStack()
        self.pstack = None
        self.ops = {e: [] for e in ENGS}
        self.sems = []
        self.prog = {}
        self.known = {e: {} for e in ENGS}
        self.dma_sems = {}
        self.free_dma = {True: [], False: []}
        self.bg_sems = {}
        self.same = same_engine_sync
        self.n_ops = 0
        self.phase_id = 0
        for e in ENGS:
            self.prog[e] = [self._new_sem("p_" + e), 0]

    def _new_sem(self, name):
        s = self.gstack.enter_context(self.nc.semaphore(f"{name}_{len(self.sems)}"))
        self.sems.append(s)
        return len(self.sems) - 1

    def sbuf(self, name, shape, dtype, persist=False):
        st = self.gstack if (persist or self.pstack is None) else self.pstack
        return st.enter_context(self.nc.sbuf_tensor(f"p{self.phase_id}_{name}", list(shape), dtype))

    def psum(self, name, shape, dtype=F32, persist=False):
        st = self.gstack if (persist or self.pstack is None) else self.pstack
        return st.enter_context(self.nc.psum_tensor(f"p{self.phase_id}_{name}", list(shape), dtype))

    def begin_phase(self):
        self.pstack = contextlib.ExitStack()
        self.phase_id += 1

    def end_phase(self):
        self.barrier()
        self.emit()
        for key, d in self.dma_sems.items():
            self.free_dma[key[0]].append(d)
        self.dma_sems = {}
        self.pstack.close()
        self.pstack = None

    def _need(self, eng, tok, waits):
        if tok is None:
            return
        peng, sidx, val = tok
        if peng == eng and (eng == "pe" or not self.same):
            return
        k = self.known[eng]
        if k.get(sidx, 0) >= val:
            return
        k[sidx] = val
        if waits.get(sidx, 0) < val:
            waits[sidx] = val

    def _deps(self, eng, reads, writes):
        waits = {}
        for b in reads:
            self._need(eng, b.w, waits)
        for b in writes:
            self._need(eng, b.w, waits)
            for (peng, sidx), val in b.r.items():
                self._need(eng, (peng, sidx, val), waits)
        return waits

    def _mark(self, tok, reads, writes):
        peng, sidx, val = tok
        for b in reads:
            b.r[(peng, sidx)] = val
        for b in writes:
            b.w = tok
            b.r = {}

    def op(self, eng, fn, reads=(), writes=()):
        waits = self._deps(eng, reads, writes)
        p = self.prog[eng]
        if p[1] >= SEM_LIMIT:
            p[0] = self._new_sem("p_" + eng)
            p[1] = 0
        p[1] += 1
        tok = (eng, p[0], p[1])
        self.ops[eng].append((sorted(waits.items()), fn, (p[0], 1)))
        self._mark(tok, reads, writes)
        self.n_ops += 1
        return tok

    def dma(self, q, fn, reads=(), writes=(), key=None):
        if key is None:
            cands = [b for b in (list(writes) + list(reads)) if not b.dram] or (list(writes) + list(reads))
            key = id(cands[0])
        key = (q == "pool", key)
        waits = self._deps(q, reads, writes)
        d = self.dma_sems.get(key)
        if d is None:
            fl = self.free_dma[q == "pool"]
            if fl:
                d = fl.pop()
            else:
                d = [self._new_sem("d"), 0]
            self.dma_sems[key] = d
        if d[1] + 16 > SEM_LIMIT:
            d[0] = self._new_sem("d")
            d[1] = 0
        d[1] += 16
        tok = ("dma", d[0], d[1])
        self.ops[q].append((sorted(waits.items()), fn, (d[0], 16)))
        self._mark(tok, reads, writes)
        self.n_ops += 1
        return tok

    def bg_dma(self, fn, buf, reads=()):
        d = self.bg_sems.get(id(buf))
        if d is None:
            d = [self._new_sem("bg"), 0]
            self.bg_sems[id(buf)] = d
        waits = self._deps("pool", reads, [])
        d[1] += 16
        tok = ("dma", d[0], d[1])
        self.ops["pool"].append((sorted(waits.items()), fn, (d[0], 16)))
        buf.w = tok
        buf.r = {}
        self.n_ops += 1
        return tok

    def barrier(self):
        targets = {}
        for e in ENGS:
            p = self.prog[e]
            if p[1] > 0:
                targets[p[0]] = p[1]
        for d in self.dma_sems.values():
            if d[1] > 0:
                targets[d[0]] = d[1]
        for e in ENGS:
            waits = {}
            for sidx, val in targets.items():
                if self.known[e].get(sidx, 0) < val:
                    self.known[e][sidx] = val
                    waits[sidx] = val
            if waits:
                self.ops[e].append((sorted(waits.items()), None, None))

    def emit(self):
        nc = self.nc
        sems = self.sems
        ops = self.ops

        def run(engh, lst):
            for waits, fn, inc in lst:
                for sidx, val in waits:
                    engh.wait_ge(sems[sidx], val)
                if fn is not None:
                    ins = fn(engh)
                    ins.then_inc(sems[inc[0]], inc[1])

        with nc.Block() as block:
            @block.tensor
            def _(e):
                run(e, ops["pe"])

            @block.scalar
            def _(e):
                run(e, ops["act"])

            @block.vector
            def _(e):
                run(e, ops["dve"])

            @block.gpsimd
            def _(e):
                run(e, ops["pool"])

            @block.sync
            def _(e):
                run(e, ops["sp"])
        self.ops = {e: [] for e in ENGS}

    def close(self):
        self.gstack.close()


class Ring:
    def __init__(self, S, name, shape, dtype, srcs, R, q="sp", nparts=1, rbufs=()):
        self.S = S
        self.rbufs = list(rbufs)
        self.tiles = [S.sbuf(f"{name}{i}", shape, dtype) for i in range(R)]
        self.bufs = [[Buf(f"{name}{i}_{j}") for j in range(nparts)] for i in range(R)]
        self.srcs = srcs
        self.R = R
        self.q = q
        self.issued = 0

    def _issue_upto(self, k):
        while self.issued <= k and self.issued < len(self.srcs):
            i = self.issued
            t = self.tiles[i % self.R]
            src = self.srcs[i]
            if not isinstance(src, list):
                src = [(lambda tt: tt[:], src)]
            for j, (sel, sr) in enumerate(src):
                self.S.dma(self.q, (lambda e, t=t, sr=sr, sel=sel: e.dma_start(out=sel(t), in_=sr)),
                           reads=self.rbufs, writes=[self.bufs[i % self.R][j]])
            self.issued += 1

    def get(self, k):
        self._issue_upto(k)
        b = self.bufs[k % self.R]
        return self.tiles[k % self.R], (b[0] if len(b) == 1 else b)

    def prefetch(self, k):
        self._issue_upto(k + self.R - 1)


class K:
    pass


def build(cfg):
    nlat = cfg.get("nlat", NLAT)
    nctx = cfg.get("nctx", NCTX)
    ntok = nlat + nctx
    phases = cfg.get("phases", None)
    debug = cfg.get("debug", set())
    nc = bass.Bass("TRN2", target_bir_lowering=False)
    S = Sched(nc, same_engine_sync=cfg.get("same", True))
    k = K()
    k.nc, k.S, k.nlat, k.nctx, k.ntok = nc, S, nlat, nctx, ntok

    def din(name, shape, dt=F32):
        return nc.dram_tensor(name, list(shape), dt, kind="ExternalInput").ap()

    def dscr(name, shape, dt):
        kind = "ExternalOutput" if name in debug else "Internal"
        return nc.dram_tensor(name, list(shape), dt, kind=kind).ap()

    k.din, k.dscr = din, dscr
    k.dbg = {}
    k.dbg_names = debug
    if "TABS" in debug:
        k.dbg["TABS"] = [nc.dram_tensor(f"TABS{l}", [128, 2, 9, KC], F32, kind="ExternalOutput").ap() for l in range(2)]
    k.x = din("x", [nlat, D])
    k.ctx = din("ctx", [nctx, D])
    k.cvec = din("cvec", [128, KC, 2])
    k.L = []
    for l in range(2):
        Lw = K()
        p = f"l{l}_"
        Lw.ada_w = din(p + "ada_w", [D, 9 * D])
        Lw.ada_b = din(p + "ada_b_t", [128, 144])
        Lw.norms = din(p + "norms_t", [128, 3 * KC])
        Lw.gu = [din(p + "ffn1_w_gu", [D, 2 * DFF]), din(p + "ffn2_w_gu", [D, 2 * DFF])]
        Lw.dn = [din(p + "ffn1_w_down", [DFF, D]), din(p + "ffn2_w_down", [DFF, D])]
        k.L.append(Lw)
    k.norm_out = din("norm_out_t", [128, KC])
    k.w_in = din("l0_w_in", [D, 4128])
    k.gw_aug = din("l0_gw_aug", [33, 2, 512])
    k.gnorm = din("l0_gnorm_t", [128, 8])
    k.w_out0 = din("l0_w_out", [D, D])
    k.rope = din("rope", [4, 128, ntok])
    k.dftc = din("dftc", [3, 256, 256], BF16)
    k.dftT = din("dftT", [2, nlat, nlat], BF16)
    k.w_qkv = din("l1_w_qkv", [D, 3 * D])
    k.w_out1 = din("l1_w_out", [D, D])
    _kh, _nch, _classes, _, _ = na_geometry(nlat // 64)
    k.nabias = din("nabias", [32, len(_classes), 128, _nch * 128])
    k.out = nc.dram_tensor("out", [nlat, D], F32, kind="ExternalOutput").ap()

    k.XT = dscr("XT", [D, ntok], F32)
    k.b_XT = Buf("XT", dram=True)
    k.QT = dscr("QT", [512, ntok], F32); k.KT = dscr("KT", [512, ntok], F32); k.b_QK = Buf("QK", dram=True)
    k.RG = dscr("RG", [1024, ntok], F32); k.b_RG = Buf("RG", dram=True)
    k.FT = dscr("FT", [1024, ntok], BF16); k.b_FT = Buf("FT", dram=True)
    k.V = dscr("V", [ntok, 1024], BF16); k.b_V = Buf("V", dram=True)
    k.GF = dscr("GF", [ntok, 512], F32); k.GB = dscr("GB", [ntok, 512], F32); k.b_G = Buf("G", dram=True)
    k.OF = dscr("OF", [1024, ntok], F32); k.OB = dscr("OB", [1024, ntok], F32); k.b_O = Buf("O", dram=True)
    k.GCS = dscr("GCS", [ntok, 2, 1024], BF16); k.b_GCS = Buf("GCS", dram=True)
    k.FMT = dscr("FMT", [1024, ntok], BF16); k.b_FMT = Buf("FMT", dram=True)
    k.QT1 = dscr("QT1", [D, nlat], BF16); k.KT1 = dscr("KT1", [D, ntok], BF16); k.b_QK1 = Buf("QK1", dram=True)
    k.V1 = dscr("V1", [ntok, D], BF16); k.b_V1 = Buf("V1", dram=True)
    k.O1 = dscr("O1", [nlat, D], BF16); k.b_O1 = Buf("O1", dram=True)

    k.ident = S.sbuf("ident", [128, 128], F32, persist=True)
    k.identb = S.sbuf("identb", [128, 128], BF16, persist=True)
    k.onesb = S.sbuf("onesb", [128, 128], BF16, persist=True)
    k.b_const = Buf("const")
    k.TAB = [S.sbuf(f"TAB{l}", [128, 2, 9, KC], F32, persist=True) for l in range(2)]
    k.b_TAB = [Buf("TAB0"), Buf("TAB1")]
    k.gout = S.sbuf("gout", [128, KC], F32, persist=True)

    S.begin_phase()
    S.op("pool", lambda e: e.memset(k.ident[:], 0.0), writes=[k.b_const])
    S.op("pool", lambda e: e.affine_select(out=k.ident[:], in_=k.ident[:], pattern=[[-1, 128]],
                                           compare_op=ALU.not_equal, fill=1.0, base=0, channel_multiplier=1),
         reads=[k.b_const], writes=[k.b_const])
    S.op("dve", lambda e: e.tensor_copy(out=k.identb[:], in_=k.ident[:]), reads=[k.b_const], writes=[k.b_const])
    S.op("dve", lambda e: e.memset(k.onesb[:], 1.0), writes=[k.b_const])
    S.dma("sp", lambda e: e.dma_start(out=k.gout[:], in_=k.norm_out), writes=[k.b_const])
    S.end_phase()

    def want(ph):
        return phases is None or ph in phases

    phase_precast(k)

    def flush_bg(upfront_only=False):
        if k.pc_tasks:
            S.begin_phase()
            run_bg(k, k.n_upfront if upfront_only else len(k.pc_tasks))
            S.end_phase()

    if want("ada"):
        phase_ada(k)
    else:
        flush_bg()
    tiles = []
    t = 0
    while t < nlat:
        tiles.append((t, min(512, nlat - t), 0))
        t += 512
    tc = nlat
    while tc < ntok:
        tiles.append((tc, min(512, ntok - tc), 1))
        tc += 512
    k.tiles = tiles
    if want("l0ffn1"):
        phase_ffn(k, 0, 0, tiles, first=True, last=False)
    flush_bg()
    if want("l0proj"):
        phase_l0proj(k, tiles)
    if want("gla"):
        phase_gla(k)
    if want("fnet"):
        phase_fnet(k)
    if want("l0out"):
        phase_l0out(k, tiles)
    if want("l0ffn2"):
        phase_ffn(k, 0, 1, tiles)
    lat_tiles = [t_ for t_ in tiles if t_[2] == 0]
    if want("l1ffn1"):
        phase_ffn(k, 1, 0, tiles)
    if want("l1proj"):
        phase_l1proj(k, tiles)
    if want("na"):
        phase_na(k)
    if want("l1out"):
        phase_l1out(k, lat_tiles)
    if want("l1ffn2"):
        phase_ffn(k, 1, 1, lat_tiles, last=True)
    S.close()
    return nc


def precast_blocks(k, name, W, Krows, col_starts, CB):
    S = k.S
    kc = Krows // 128
    WB = k.dscr(name, [len(col_starts), 128, kc, CB], BF16)
    b = Buf(name, dram=True)
    for nb, c0 in enumerate(col_starts):
        src = W[:, c0:c0 + CB].rearrange("(kc p) c -> p kc c", p=128)
        dst = WB[nb]
        k.pc_tasks.append(((lambda e, dst=dst, src=src: e.dma_start(out=dst, in_=src)), b))
    return WB, b


def phase_precast(k):
    k.pc_tasks = []

    def ffn(l, w):
        Lw = k.L[l]
        cs = []
        for j in range(JC):
            cs += [128 * j, DFF + 128 * j]
        Lw.guB[w], Lw.b_guB[w] = precast_blocks(k, f"guB{l}{w}", Lw.gu[w], D, cs, 128)
        Lw.dnB[w], Lw.b_dnB[w] = precast_blocks(k, f"dnB{l}{w}", Lw.dn[w], DFF, [128 * o for o in range(KC)], 128)

    for l in range(2):
        k.L[l].guB, k.L[l].dnB, k.L[l].b_guB, k.L[l].b_dnB = [None, None], [None, None], [None, None], [None, None]
    ffn(0, 0)
    precast_l0(k)
    k.n_upfront = len(k.pc_tasks)
    ffn(0, 1)
    ffn(1, 0)
    precast_l1(k)
    ffn(1, 1)


def run_bg(k, n):
    S = k.S
    for _ in range(n):
        if not k.pc_tasks:
            return
        fn, b = k.pc_tasks.pop(0)
        S.bg_dma(fn, b)


def phase_ada(k):
    S = k.S
    S.begin_phase()
    run_bg(k, k.n_upfront)
    s_t = S.sbuf("ada_s", [128, KC, 2], F32)
    b_s = Buf("ada_s")
    S.dma("sp", lambda e: e.dma_start(out=s_t[:], in_=k.cvec), writes=[b_s])
    S.op("act", lambda e: e.activation(out=s_t[:], in_=s_t[:], func=AF.Silu), reads=[b_s], writes=[b_s])
    M = S.sbuf("ada_M", [128, 144, 2], F32)
    b_M = Buf("M")
    adab = S.sbuf("ada_b", [128, 144], F32)
    b_adab = Buf("adab")
    gains = S.sbuf("ada_g", [128, 3 * KC], F32)
    b_g = Buf("gains")
    NW = 3
    wt = [S.sbuf(f"ada_w{i}", [128, KC, 512], F32) for i in range(NW)]
    b_wt = [Buf() for _ in range(NW)]
    psr = [S.psum(f"ada_psr{i}", [2, 512]) for i in range(2)]
    b_psr = [Buf(), Buf()]
    Rrow = S.sbuf("ada_R", [2, 9 * D], F32)
    b_R = Buf("R")
    ps = [S.psum(f"ada_ps{i}", [128, 4, 2]) for i in range(2)]
    b_ps = [Buf("aps0"), Buf("aps1")]
    ev = 0
    for l in range(2):
        Lw = k.L[l]
        S.dma("sp", lambda e, Lw=Lw: e.dma_start(out=adab[:], in_=Lw.ada_b), writes=[b_adab])
        S.dma("sp", lambda e, Lw=Lw: e.dma_start(out=gains[:], in_=Lw.norms), writes=[b_g])
        for cb in range(36):
            w_t, bw = wt[ev % NW], b_wt[ev % NW]
            src = Lw.ada_w[:, cb * 512:(cb + 1) * 512].rearrange("(kc p) c -> p kc c", p=128)
            S.dma("sp", (lambda e, w_t=w_t, src=src: e.dma_start(out=w_t[:], in_=src)), writes=[bw])
            p_t, bp = psr[ev % 2], b_psr[ev % 2]
            for kc in range(KC):
                S.op("pe", (lambda e, p_t=p_t, w_t=w_t, kc=kc: e.matmul(
                    p_t[:, :], lhsT=s_t[:, kc, :], rhs=w_t[:, kc, :], start=(kc == 0), stop=(kc == KC - 1))),
                    reads=[bw, b_s], writes=[bp])
            S.op("act", (lambda e, p_t=p_t, cb=cb: e.activation(out=Rrow[:, cb * 512:(cb + 1) * 512], in_=p_t[:, :], func=AF.Copy)),
                 reads=[bp], writes=[b_R])
            ev += 1
        for cq in range(36):
            p_t, bp = ps[cq % 2], b_ps[cq % 2]
            for c4 in range(4):
                cc = cq * 4 + c4
                S.op("pe", (lambda e, p_t=p_t, c4=c4, cc=cc: e.transpose(p_t[:, c4, :], Rrow[0:2, cc * 128:(cc + 1) * 128], k.ident[0:2, 0:2])),
                     reads=[b_R, k.b_const], writes=[bp])
            for c4 in range(4):
                cc = cq * 4 + c4
                S.op("dve", (lambda e, p_t=p_t, c4=c4, cc=cc: e.tensor_scalar(
                    out=M[:, cc, :], in0=p_t[:, c4, :], scalar1=adab[:, cc:cc + 1], scalar2=None, op0=ALU.add)),
                    reads=[bp, b_adab], writes=[b_M])
        TAB = k.TAB[l]
        for cls in range(2):
            for j in range(3):
                sh = M[:, (3 * j) * KC:(3 * j + 1) * KC, cls]
                sc = M[:, (3 * j + 1) * KC:(3 * j + 2) * KC, cls]
                gt = M[:, (3 * j + 2) * KC:(3 * j + 3) * KC, cls]
                S.op("dve", (lambda e, sc=sc, j=j, cls=cls, TAB=TAB: e.scalar_tensor_tensor(
                    out=TAB[:, cls, 3 * j, :], in0=sc, scalar=1.0, in1=gains[:, j * KC:(j + 1) * KC],
                    op0=ALU.add, op1=ALU.mult)), reads=[b_M, b_g], writes=[k.b_TAB[l]])
                S.op("dve", (lambda e, sh=sh, j=j, cls=cls, TAB=TAB: e.tensor_copy(
                    out=TAB[:, cls, 3 * j + 1, :], in_=sh)), reads=[b_M], writes=[k.b_TAB[l]])
                S.op("dve", (lambda e, gt=gt, j=j, cls=cls, TAB=TAB: e.tensor_scalar(
                    out=TAB[:, cls, 3 * j + 2, :], in0=gt, scalar1=(1.0 if j == 1 else 0.5), scalar2=None,
                    op0=ALU.mult)), reads=[b_M], writes=[k.b_TAB[l]])
    if "TABS" in k.dbg_names:
        for l in range(2):
            S.dma("sp", lambda e, l=l: e.dma_start(out=k.dbg["TABS"][l], in_=k.TAB[l][:]), reads=[k.b_TAB[l]])
    S.end_phase()


class NormCtx:
    def __init__(self, k, pfx="n"):
        S = k.S
        self.k = k
        self.sq = [S.sbuf(f"{pfx}sq{i}", [128, 512], BF16) for i in range(2)]
        self.b_sq = [Buf(), Buf()]
        self.tmp = [S.sbuf(f"{pfx}tmp{i}", [128, 512], F32) for i in range(2)]
        self.b_tmp = [Buf(), Buf()]
        self.rs = S.sbuf(pfx + "rs", [128, 512], F32)
        self.b_rs = Buf()
        self.rstd = S.sbuf(pfx + "rstd", [128, 512], F32)
        self.b_rstd = Buf()
        self.ps_ss = S.psum(pfx + "ps_ss", [128, 512])
        self.b_ss = Buf()

    def stats(self, x_t, bx, nt, nchunks=KC, dim=D):
        k, S = self.k, self.k.S
        for kc in range(nchunks):
            s_t, bs = self.sq[kc % 2], self.b_sq[kc % 2]
            S.op("act", (lambda e, s_t=s_t, kc=kc: e.activation(out=s_t[:, 0:nt], in_=x_t[:, kc, 0:nt], func=AF.Square)),
                 reads=[bx], writes=[bs])
            S.op("pe", (lambda e, s_t=s_t, kc=kc: e.matmul(self.ps_ss[:, 0:nt], lhsT=k.onesb[:], rhs=s_t[:, 0:nt],
                                                           start=(kc == 0), stop=(kc == nchunks - 1))),
                 reads=[bs, k.b_const], writes=[self.b_ss])
        S.op("act", (lambda e: e.activation(out=self.rs[:, 0:nt], in_=self.ps_ss[:, 0:nt], func=AF.Sqrt, bias=EPS, scale=1.0 / dim)),
             reads=[self.b_ss], writes=[self.b_rs])
        S.op("dve", (lambda e: e.reciprocal(out=self.rstd[:, 0:nt], in_=self.rs[:, 0:nt])), reads=[self.b_rs], writes=[self.b_rstd])

    def apply(self, x_t, bx, nt, A, Bv, btab, out_t, b_out):
        S = self.k.S
        for kc in range(KC):
            t_t, bt = self.tmp[kc % 2], self.b_tmp[kc % 2]
            S.op("dve", (lambda e, t_t=t_t, kc=kc: e.scalar_tensor_tensor(
                out=t_t[:, 0:nt], in0=x_t[:, kc, 0:nt], scalar=A[:, kc:kc + 1], in1=self.rstd[:, 0:nt],
                op0=ALU.mult, op1=ALU.mult)), reads=[bx, self.b_rstd, btab], writes=[bt])
            S.op("act", (lambda e, t_t=t_t, kc=kc: e.activation(
                out=out_t[:, kc, 0:nt], in_=t_t[:, 0:nt], func=AF.Identity, bias=Bv[:, kc:kc + 1], scale=1.0)),
                reads=[bt, btab], writes=[b_out])


def phase_ffn(k, l, w, tiles, first=False, last=False):
    S = k.S
    Lw = k.L[l]
    j3 = 0 if w == 0 else 2
    TAB = k.TAB[l]
    bTAB = k.b_TAB[l]
    S.begin_phase()
    XTv = k.XT.rearrange("(kc p) t -> p kc t", p=128)
    xt = [S.sbuf(f"xt{i}", [128, KC, 512], F32) for i in range(2)]
    b_xt = [Buf("xt0"), Buf("xt1")]
    ht = S.sbuf("ht", [128, KC, 512], BF16)
    b_ht = Buf("ht")
    gt = S.sbuf("gt", [128, JC, 512], BF16)
    b_gt = [Buf(f"gt{j}") for j in range(JC)]
    nrm = NormCtx(k)
    sa = [S.sbuf(f"sa{i}", [128, 512], F32) for i in range(2)]
    b_sa = [Buf("sa0"), Buf("sa1")]
    ps_a = [S.psum(f"ps_a{i}", [128, 512]) for i in range(2)]
    b_a = [Buf("psa0"), Buf("psa1")]
    ps_u = [S.psum(f"ps_u{i}", [128, 512]) for i in range(2)]
    b_u = [Buf("psu0"), Buf("psu1")]
    ps_y = [S.psum(f"ps_y{i}", [128, 512]) for i in range(2)]
    b_y = [Buf("psy0"), Buf("psy1")]
    if first or last:
        ps_tr = S.psum("ps_tr", [128, 512])
        b_tr = Buf("ps_tr")
        xtok = S.sbuf("xtok", [128, D], F32)
        b_xtok = Buf("xtok")
    nT = len(tiles)
    gu_srcs = []
    dn_srcs = []
    for ti in range(nT):
        for j in range(JC):
            gu_srcs.append(Lw.guB[w][2 * j:2 * j + 2].rearrange("b p kc c -> p b kc c"))
        for oc in range(KC):
            for jh in range(2):
                dn_srcs.append(Lw.dnB[w][oc][:, jh * (JC // 2):(jh + 1) * (JC // 2), :])
    ring_gu = Ring(S, "wgu", [128, 2, KC, 128], BF16, gu_srcs, 4, q="sp", rbufs=[Lw.b_guB[w]])
    ring_dn = Ring(S, "wdn", [128, JC // 2, 128], BF16, dn_srcs, 4, q="sp", rbufs=[Lw.b_dnB[w]])

    def load_x(ti):
        t0, nt, cls = tiles[ti]
        x_t, bx = xt[ti % 2], b_xt[ti % 2]
        if not first:
            S.dma("pool", (lambda e: e.dma_start(out=x_t[:, :, 0:nt], in_=XTv[:, :, t0:t0 + nt])),
                  reads=[k.b_XT], writes=[bx])
            return
        for s in range(nt // 128):
            tt = t0 + s * 128
            src = k.x[tt:tt + 128, :] if cls == 0 else k.ctx[tt - k.nlat:tt - k.nlat + 128, :]
            S.dma("pool", (lambda e, src=src: e.dma_start(out=xtok[:], in_=src)), writes=[b_xtok])
            for kq in range(KC // 4):
                for k4 in range(4):
                    kc = kq * 4 + k4
                    S.op("pe", (lambda e, kc=kc, k4=k4: e.transpose(ps_tr[:, k4 * 128:(k4 + 1) * 128],
                                                                   xtok[:, kc * 128:(kc + 1) * 128], k.ident[:])),
                         reads=[b_xtok, k.b_const], writes=[b_tr])
                S.op("dve", (lambda e, kq=kq, s=s: e.tensor_copy(
                    out=x_t[:, kq * 4:(kq + 1) * 4, s * 128:(s + 1) * 128],
                    in_=ps_tr[:, :].rearrange("p (a b) -> p a b", a=4))), reads=[b_tr], writes=[bx])

    def norm(ti):
        t0, nt, cls = tiles[ti]
        x_t, bx = xt[ti % 2], b_xt[ti % 2]
        nrm.stats(x_t, bx, nt)
        nrm.apply(x_t, bx, nt, TAB[:, cls, 3 * j3, :], TAB[:, cls, 3 * j3 + 1, :], bTAB, ht, b_ht)

    def gateup(ti):
        t0, nt, cls = tiles[ti]
        for j in range(JC):
            evn = ti * JC + j
            ring_gu.prefetch(evn)
            w_t, bw = ring_gu.get(evn)
            pa, ba = ps_a[j % 2], b_a[j % 2]
            pu, bu = ps_u[j % 2], b_u[j % 2]
            for kc in range(KC):
                S.op("pe", (lambda e, pa=pa, w_t=w_t, kc=kc: e.matmul(
                    pa[:, 0:nt], lhsT=w_t[:, 0, kc, :], rhs=ht[:, kc, 0:nt],
                    start=(kc == 0), stop=(kc == KC - 1))), reads=[bw, b_ht], writes=[ba])
            for kc in range(KC):
                S.op("pe", (lambda e, pu=pu, w_t=w_t, kc=kc: e.matmul(
                    pu[:, 0:nt], lhsT=w_t[:, 1, kc, :], rhs=ht[:, kc, 0:nt],
                    start=(kc == 0), stop=(kc == KC - 1))), reads=[bw, b_ht], writes=[bu])
            s_a, bsa = sa[j % 2], b_sa[j % 2]
            S.op("act", (lambda e, s_a=s_a, pa=pa: e.activation(out=s_a[:, 0:nt], in_=pa[:, 0:nt], func=AF.Silu)),
                 reads=[ba], writes=[bsa])
            S.op("dve", (lambda e, s_a=s_a, pu=pu, j=j: e.tensor_tensor(
                out=gt[:, j, 0:nt], in0=s_a[:, 0:nt], in1=pu[:, 0:nt], op=ALU.mult)),
                reads=[bsa, bu], writes=[b_gt[j]])

    def down(ti):
        t0, nt, cls = tiles[ti]
        x_t, bx = xt[ti % 2], b_xt[ti % 2]
        JH = JC // 2
        for oc in range(KC):
            py, by = ps_y[oc % 2], b_y[oc % 2]
            for jh in range(2):
                evn = (ti * KC + oc) * 2 + jh
                ring_dn.prefetch(evn)
                w_t, bw = ring_dn.get(evn)
                for jj in range(JH):
                    jc = jh * JH + jj
                    S.op("pe", (lambda e, py=py, w_t=w_t, jj=jj, jc=jc: e.matmul(
                        py[:, 0:nt], lhsT=w_t[:, jj, :], rhs=gt[:, jc, 0:nt], start=(jc == 0), stop=(jc == JC - 1))),
                        reads=[bw, b_gt[jc]], writes=[by])
            S.op("dve", (lambda e, py=py, oc=oc: e.scalar_tensor_tensor(
                out=x_t[:, oc, 0:nt], in0=py[:, 0:nt], scalar=TAB[:, cls, 3 * j3 + 2, oc:oc + 1], in1=x_t[:, oc, 0:nt],
                op0=ALU.mult, op1=ALU.add)), reads=[by, bx, bTAB], writes=[bx])

    def store(ti):
        t0, nt, cls = tiles[ti]
        x_t, bx = xt[ti % 2], b_xt[ti % 2]
        if not last:
            S.dma("pool", (lambda e: e.dma_start(out=XTv[:, :, t0:t0 + nt], in_=x_t[:, :, 0:nt])),
                  reads=[bx], writes=[k.b_XT])
            return
        nrm.stats(x_t, bx, nt)
        for kc in range(KC):
            S.op("dve", (lambda e, kc=kc: e.scalar_tensor_tensor(
                out=x_t[:, kc, 0:nt], in0=x_t[:, kc, 0:nt], scalar=k.gout[:, kc:kc + 1], in1=nrm.rstd[:, 0:nt],
                op0=ALU.mult, op1=ALU.mult)), reads=[bx, nrm.b_rstd, k.b_const], writes=[bx])
        for s in range(nt // 128):
            for kq in range(KC // 4):
                for k4 in range(4):
                    kc = kq * 4 + k4
                    S.op("pe", (lambda e, kc=kc, k4=k4, s=s: e.transpose(
                        ps_tr[:, k4 * 128:(k4 + 1) * 128], x_t[:, kc, s * 128:(s + 1) * 128], k.ident[:])),
                        reads=[bx, k.b_const], writes=[b_tr])
                S.op("act", (lambda e, kq=kq: e.activation(out=xtok[:, kq * 512:(kq + 1) * 512], in_=ps_tr[:, :], func=AF.Copy)),
                     reads=[b_tr], writes=[b_xtok])
            tt = t0 + s * 128
            S.dma("pool", (lambda e, tt=tt: e.dma_start(out=k.out[tt:tt + 128, :], in_=xtok[:])), reads=[b_xtok], key="outst")

    load_x(0)
    norm(0)
    for ti in range(nT):
        gateup(ti)
        if ti + 1 < nT:
            load_x(ti + 1)
        if k.pc_tasks:
            left = nT - ti
            run_bg(k, (len(k.pc_tasks) + left - 1) // left)
        if ti + 1 < nT:
            norm(ti + 1)
        down(ti)
        store(ti)
    S.end_phase()


W_Q0, W_K0, W_V0, W_LR0, W_R0, W_F0 = 0, 512, 1024, 2048, 2080, 3104


def precast_l0(k):
    S = k.S
    Lw = k.L[0]
    win = k.w_in
    nblk = 32
    WB = k.dscr("winFM", [nblk, 128, KC, 128], BF16)
    b = Buf("winFM", dram=True)
    perm = [1, 0, 3, 2]
    order = []
    for h in range(4):
        order += [("n", W_Q0 + 128 * h), ("s", W_Q0 + 128 * h), ("n", W_K0 + 128 * h), ("s", W_K0 + 128 * h)]
    for c in range(8):
        order.append(("n", W_R0 + 128 * c))
    for c in range(8):
        order.append(("n", W_F0 + 128 * c))
    for nb, (kind, c0) in enumerate(order):
        if kind == "n":
            src = win[:, c0:c0 + 128].rearrange("(kc p) c -> p kc c", p=128)
            k.pc_tasks.append(((lambda e, dst=WB[nb], src=src: e.dma_start(out=dst, in_=src)), b))
        else:
            for i in range(4):
                src = win[:, c0 + 32 * perm[i]:c0 + 32 * perm[i] + 32].rearrange("(kc p) c -> p kc c", p=128)
                dst = WB[nb][:, :, 32 * i:32 * i + 32]
                k.pc_tasks.append(((lambda e, dst=dst, src=src: e.dma_start(out=dst, in_=src)), b))
    k.winFM = WB
    k.b_winFM = b
    k.winLR, k.b_winLR = precast_blocks(k, "winLR", win, D, [W_LR0], 32)
    k.winV, k.b_winV = precast_blocks(k, "winV", win, D, [W_V0, W_V0 + 512], 512)
    k.wout0B, k.b_wout0B = precast_blocks(k, "wout0B", k.w_out0, D, [128 * o for o in range(KC)], 128)


def phase_l0proj(k, tiles):
    S = k.S
    TAB, bTAB = k.TAB[0], k.b_TAB[0]
    ntok = k.ntok
    S.begin_phase()
    XTv = k.XT.rearrange("(kc p) t -> p kc t", p=128)
    xt = [S.sbuf("xt0", [128, KC, 512], F32)] * 2
    b_xt = [Buf()] * 2
    uts = [S.sbuf(f"ut{i}", [128, KC, 512], BF16) for i in range(2)]
    b_uts = [Buf("ut0"), Buf("ut1")]
    nrm = NormCtx(k)
    rope = [S.sbuf(f"rope{i}", [128, 4, 512], F32) for i in range(2)]
    b_rope = [Buf(), Buf()]
    t1 = [S.sbuf(f"t1_{i}", [128, 512], F32) for i in range(2)]
    b_t1 = [Buf(), Buf()]
    t2 = [S.sbuf(f"t2_{i}", [128, 512], F32) for i in range(2)]
    b_t2 = [Buf(), Buf()]
    qr = [S.sbuf(f"qr{i}", [128, 512], F32) for i in range(2)]
    b_qr = [Buf(), Buf()]
    rg = [S.sbuf(f"rg{i}", [128, 512], F32) for i in range(2)]
    b_rg = [Buf(), Buf()]
    fb = [S.sbuf(f"fb{i}", [128, 512], BF16) for i in range(2)]
    b_fb = [Buf(), Buf()]
    lra = S.sbuf("lra", [33, 512], F32)
    b_lra = Buf("lra")
    gw = S.sbuf("gw", [33, 2, 512], F32)
    b_gw = Buf("gw")
    gn = S.sbuf("gn", [128, 8], F32)
    b_gn = Buf("gn")
    ge = [S.sbuf(f"ge{i}", [128, 512], F32) for i in range(2)]
    b_ge = [Buf(), Buf()]
    gs = [S.sbuf(f"gs{i}", [128, 512], F32) for i in range(2)]
    b_gs = [Buf(), Buf()]
    vt = [S.sbuf(f"vt{i}", [128, 1024], BF16) for i in range(2)]
    b_vt = [Buf(), Buf()]
    ps_fm = [S.psum(f"ps_fm{i}", [128, 512]) for i in range(4)]
    b_fm = [Buf() for _ in range(4)]
    ps_z = S.psum("ps_z", [128, 512])
    b_z = Buf()
    ps_v = [S.psum(f"ps_v{i}", [128, 512]) for i in range(2)]
    b_v = [Buf(), Buf()]

    S.op("dve", lambda e: e.memset(lra[:], 1.0), writes=[b_lra])
    S.dma("sp", lambda e: e.dma_start(out=gw[:], in_=k.gw_aug), writes=[b_gw])
    S.dma("sp", lambda e: e.dma_start(out=gn[:], in_=k.gnorm), writes=[b_gn])

    nT = len(tiles)
    fm_srcs, lr_srcs, v_srcs = [], [], []
    for ti in range(nT):
        lr_srcs.append(k.winLR[0])
        for nb in range(32):
            fm_srcs.append(k.winFM[nb])
        nsub = tiles[ti][1] // 128
        for s_ in range(nsub):
            v_srcs += [k.winV[0], k.winV[1]]
    ring_fm = Ring(S, "wfm", [128, KC, 128], BF16, fm_srcs, 8, q="sp", rbufs=[k.b_winFM])
    ring_lr = Ring(S, "wlr", [128, KC, 32], BF16, lr_srcs, 2, q="sp", rbufs=[k.b_winLR])
    wv_res = S.sbuf("wvres", [128, 2, KC, 512], BF16)
    b_wv = [Buf(), Buf()]
    for hf in range(2):
        S.dma("sp", (lambda e, hf=hf: e.dma_start(out=wv_res[:, hf, :, :], in_=k.winV[hf])), reads=[k.b_winV], writes=[b_wv[hf]])
    fm_ev = [0]
    v_ev = [0]
    pcount = [0]

    def load_x(ti):
        t0, nt, cls = tiles[ti]
        x_t, bx = xt[ti % 2], b_xt[ti % 2]
        S.dma("pool", (lambda e: e.dma_start(out=x_t[:, :, 0:nt], in_=XTv[:, :, t0:t0 + nt])), reads=[k.b_XT], writes=[bx])
        r_t, br = rope[ti % 2], b_rope[ti % 2]
        S.dma("pool", (lambda e: e.dma_start(out=r_t[:, :, 0:nt], in_=k.rope.rearrange("a p t -> p a t")[:, :, t0:t0 + nt])), writes=[br])

    def norm(ti):
        t0, nt, cls = tiles[ti]
        x_t, bx = xt[ti % 2], b_xt[ti % 2]
        nrm.stats(x_t, bx, nt)
        nrm.apply(x_t, bx, nt, TAB[:, cls, 3, :], TAB[:, cls, 4, :], bTAB, uts[ti % 2], b_uts[ti % 2])

    def fm_chunk(nt, M=128, ring=None, evn=None, ut=None, b_ut=None):
        ring.prefetch(evn)
        w_t, bw = ring.get(evn)
        i = pcount[0] % 4
        pcount[0] += 1
        p_t, bp = ps_fm[i], b_fm[i]
        for kc in range(KC):
            S.op("pe", (lambda e, p_t=p_t, w_t=w_t, kc=kc, ut=ut: e.matmul(
                p_t[0:M, 0:nt], lhsT=w_t[:, kc, :], rhs=ut[:, kc, 0:nt], start=(kc == 0), stop=(kc == KC - 1))),
                reads=[bw, b_ut], writes=[bp])
        return p_t, bp

    def proj(ti, hook=None):
        t0, nt, cls = tiles[ti]
        r_t, br = rope[ti % 2], b_rope[ti % 2]
        ut, b_ut = uts[ti % 2], b_uts[ti % 2]
        p_t, bp = fm_chunk(nt, M=32, ring=ring_lr, evn=ti, ut=ut, b_ut=b_ut)
        S.op("act", (lambda e, p_t=p_t: e.activation(out=lra[0:32, 0:nt], in_=p_t[0:32, 0:nt], func=AF.Copy)), reads=[bp], writes=[b_lra])
        for s_ in range(nt // 128):
            tok0 = t0 + s_ * 128
            for d in range(2):
                S.op("pe", (lambda e, s_=s_, d=d: e.matmul(ps_z[:, :], lhsT=lra[0:33, s_ * 128:(s_ + 1) * 128], rhs=gw[0:33, d, :],
                                                          start=True, stop=True)), reads=[b_lra, b_gw], writes=[b_z])
                i = (s_ * 2 + d) % 2
                S.op("act", (lambda e, i=i: e.activation(out=ge[i][:], in_=ps_z[:, :], func=AF.Exp, scale=-1.0)), reads=[b_z], writes=[b_ge[i]])
                S.op("act", (lambda e, i=i: e.activation(out=gs[i][:], in_=ge[i][:], func=AF.Ln, bias=1.0, scale=1.0)), reads=[b_ge[i]], writes=[b_gs[i]])
                S.op("dve", (lambda e, i=i: e.tensor_scalar(out=gs[i][:], in0=gs[i][:], scalar1=-1.0 / 16.0, scalar2=None, op0=ALU.mult)),
                     reads=[b_gs[i]], writes=[b_gs[i]])
                dst = (k.GF if d == 0 else k.GB)[tok0:tok0 + 128, :]
                S.dma("pool", (lambda e, i=i, dst=dst: e.dma_start(out=dst, in_=gs[i][:])), reads=[b_gs[i]], writes=[k.b_G])
        QTv = k.QT.rearrange("(h p) t -> p h t", p=128)
        KTv = k.KT.rearrange("(h p) t -> p h t", p=128)
        for h in range(4):
            for which in range(2):
                pn, bn = fm_chunk(nt, ring=ring_fm, evn=fm_ev[0], ut=ut, b_ut=b_ut); fm_ev[0] += 1
                psw, bsw = fm_chunk(nt, ring=ring_fm, evn=fm_ev[0], ut=ut, b_ut=b_ut); fm_ev[0] += 1
                i = (h * 2 + which) % 2
                S.op("dve", (lambda e, i=i, pn=pn, which=which: e.tensor_tensor(out=t1[i][:, 0:nt], in0=pn[:, 0:nt], in1=r_t[:, 2 * which, 0:nt], op=ALU.mult)),
                     reads=[bn, br], writes=[b_t1[i]])
                S.op("dve", (lambda e, i=i, psw=psw, which=which: e.tensor_tensor(out=t2[i][:, 0:nt], in0=psw[:, 0:nt], in1=r_t[:, 2 * which + 1, 0:nt], op=ALU.mult)),
                     reads=[bsw, br], writes=[b_t2[i]])
                S.op("pool", (lambda e, i=i: e.tensor_tensor(out=qr[i][:, 0:nt], in0=t1[i][:, 0:nt], in1=t2[i][:, 0:nt], op=ALU.add)),
                     reads=[b_t1[i], b_t2[i]], writes=[b_qr[i]])
                dst = (QTv if which == 0 else KTv)[:, h, t0:t0 + nt]
                S.dma("pool", (lambda e, i=i, dst=dst: e.dma_start(out=dst, in_=qr[i][:, 0:nt])), reads=[b_qr[i]], writes=[k.b_QK])
        RGv = k.RG.rearrange("(c p) t -> p c t", p=128)
        for c in range(8):
            p_t, bp = fm_chunk(nt, ring=ring_fm, evn=fm_ev[0], ut=ut, b_ut=b_ut); fm_ev[0] += 1
            i = c % 2
            S.op("act", (lambda e, i=i, p_t=p_t: e.activation(out=rg[i][:, 0:nt], in_=p_t[:, 0:nt], func=AF.Silu)), reads=[bp], writes=[b_rg[i]])
            S.op("dve", (lambda e, i=i, c=c: e.tensor_scalar(out=rg[i][:, 0:nt], in0=rg[i][:, 0:nt], scalar1=gn[:, c:c + 1], scalar2=None, op0=ALU.mult)),
                 reads=[b_rg[i], b_gn], writes=[b_rg[i]])
            S.dma("pool", (lambda e, i=i, c=c: e.dma_start(out=RGv[:, c, t0:t0 + nt], in_=rg[i][:, 0:nt])), reads=[b_rg[i]], writes=[k.b_RG])
        if hook is not None:
            hook()
        FTv = k.FT.rearrange("(c p) t -> p c t", p=128)
        for c in range(8):
            p_t, bp = fm_chunk(nt, ring=ring_fm, evn=fm_ev[0], ut=ut, b_ut=b_ut); fm_ev[0] += 1
            i = c % 2
            S.op("act", (lambda e, i=i, p_t=p_t: e.activation(out=fb[i][:, 0:nt], in_=p_t[:, 0:nt], func=AF.Copy)), reads=[bp], writes=[b_fb[i]])
            S.dma("pool", (lambda e, i=i, c=c: e.dma_start(out=FTv[:, c, t0:t0 + nt], in_=fb[i][:, 0:nt])), reads=[b_fb[i]], writes=[k.b_FT])
        for s_ in range(nt // 128):
            tok0 = t0 + s_ * 128
            v_t, bv = vt[s_ % 2], b_vt[s_ % 2]
            for half in range(2):
                w_t, bw = wv_res[:, half, :, :], b_wv[half]
                pv, bpv = ps_v[half], b_v[half]
                for kc in range(KC):
                    S.op("pe", (lambda e, pv=pv, w_t=w_t, kc=kc, s_=s_, ut=ut: e.matmul(
                        pv[:, :], lhsT=ut[:, kc, s_ * 128:(s_ + 1) * 128], rhs=w_t[:, kc, :], start=(kc == 0), stop=(kc == KC - 1))),
                        reads=[bw, b_ut], writes=[bpv])
                S.op("act", (lambda e, pv=pv, v_t=v_t, half=half: e.activation(out=v_t[:, half * 512:(half + 1) * 512], in_=pv[:, :], func=AF.Copy)),
                     reads=[bpv], writes=[bv])
            S.dma("pool", (lambda e, v_t=v_t, tok0=tok0: e.dma_start(out=k.V[tok0:tok0 + 128, :], in_=v_t[:])), reads=[bv], writes=[k.b_V])

    load_x(0)
    norm(0)
    if nT > 1:
        load_x(1)
    for ti in range(nT):
        def hook(ti=ti):
            if ti + 1 < nT:
                norm(ti + 1)
            if ti + 2 < nT:
                load_x(ti + 2)
        proj(ti, hook)
    S.end_phase()


def phase_gla(k):
    S = k.S
    nlat, nctx = k.nlat, k.nctx
    S.begin_phase()
    nl, ncx = nlat // 128, nctx // 128
    order = [[], []]
    order[0] = [nlat + 128 * c for c in range(ncx)] + [128 * i for i in range(nl)]
    order[1] = [nlat + 128 * c for c in reversed(range(ncx))] + [128 * i for i in reversed(range(nl))]
    nsteps = nl + ncx
    msk = S.sbuf("gmask", [128, 2, 128], F32)
    b_msk = Buf()
    S.op("pool", lambda e: e.memset(msk[:], 1.0), writes=[b_msk])
    S.op("pool", lambda e: e.affine_select(out=msk[:, 0, :], in_=msk[:, 0, :], pattern=[[1, 128]], compare_op=ALU.is_ge,
                                           fill=0.0, base=0, channel_multiplier=-1), reads=[b_msk], writes=[b_msk])
    S.op("pool", lambda e: e.affine_select(out=msk[:, 1, :], in_=msk[:, 1, :], pattern=[[-1, 128]], compare_op=ALU.is_ge,
                                           fill=0.0, base=0, channel_multiplier=1), reads=[b_msk], writes=[b_msk])
    St = S.sbuf("gS", [128, 8, 256], F32)
    Sb = S.sbuf("gSb", [128, 8, 256], BF16)
    b_S = [Buf() for _ in range(8)]
    b_Sb = [Buf() for _ in range(8)]
    S.op("dve", lambda e: e.memset(St[:], 0.0), writes=b_S)
    S.op("dve", lambda e: e.memset(Sb[:], 0.0), writes=b_Sb)
    gld = [[S.sbuf(f"gg{d}{i}", [128, 512], F32) for i in range(2)] for d in range(2)]
    qld = [[S.sbuf(f"gq{d}{i}", [128, 4, 128], F32) for i in range(2)] for d in range(2)]
    kld = [[S.sbuf(f"gk{d}{i}", [128, 4, 128], F32) for i in range(2)] for d in range(2)]
    vld = [[S.sbuf(f"gv{d}{i}", [128, 1024], BF16) for i in range(2)] for d in range(2)]
    b_gld = [[Buf(), Buf()] for _ in range(2)]
    b_qld = [[Buf(), Buf()] for _ in range(2)]
    b_kld = [[Buf(), Buf()] for _ in range(2)]
    b_vld = [[Buf(), Buf()] for _ in range(2)]
    eq = [S.sbuf(f"eq{c}", [128, 128], F32) for c in range(8)]
    ek = [S.sbuf(f"ek{c}", [128, 128], F32) for c in range(8)]
    qtl = [S.sbuf(f"qtl{c}", [128, 128], BF16) for c in range(8)]
    ktl = [S.sbuf(f"ktl{c}", [128, 128], BF16) for c in range(8)]
    attm = [S.sbuf(f"attm{c}", [128, 128], BF16) for c in range(8)]
    ktok = [S.sbuf(f"ktok{c}", [128, 128], BF16) for c in range(8)]
    osb = [S.sbuf(f"osb{c}", [128, 2, 128], F32) for c in range(8)]
    b_eq = [Buf() for _ in range(8)]; b_ek = [Buf() for _ in range(8)]; b_qtl = [Buf() for _ in range(8)]
    b_ktl = [Buf() for _ in range(8)]; b_attm = [Buf() for _ in range(8)]; b_ktok = [Buf() for _ in range(8)]
    b_osb = [Buf() for _ in range(8)]
    psA = [S.psum(f"gpA{i}", [128, 512]) for i in range(4)]
    _kv = [S.psum(f"gpKV{i}", [128, 512]) for i in range(2)]
    psKV = [_kv[i % 2][:, 0:256] for i in range(4)]
    _kt = [S.psum(f"gpKT{i}", [128, 1024], BF16) for i in range(2)]
    psKT = [_kt[i % 2][:, 0:128] for i in range(4)]
    b_bc = [Buf() for _ in range(4)]; b_att = b_bc; b_o = b_bc
    _bkv = [Buf(), Buf()]; b_kv = [_bkv[i % 2] for i in range(4)]
    _bkt = [Buf(), Buf()]; b_kt = [_bkt[i % 2] for i in range(4)]
    QTv = k.QT.rearrange("(h p) t -> p h t", p=128)
    KTv = k.KT.rearrange("(h p) t -> p h t", p=128)
    Ov = [k.OF.rearrange("(c p) t -> p c t", p=128), k.OB.rearrange("(c p) t -> p c t", p=128)]
    Gd = [k.GF, k.GB]

    def loads(i):
        for d in range(2):
            tok0 = order[d][i]
            j = i % 2
            S.dma("sp", (lambda e, d=d, j=j, tok0=tok0: e.dma_start(out=gld[d][j][:], in_=Gd[d][tok0:tok0 + 128, :])), reads=[k.b_G], writes=[b_gld[d][j]])
            S.dma("sp", (lambda e, d=d, j=j, tok0=tok0: e.dma_start(out=qld[d][j][:], in_=QTv[:, :, tok0:tok0 + 128])), reads=[k.b_QK], writes=[b_qld[d][j]])
            S.dma("sp", (lambda e, d=d, j=j, tok0=tok0: e.dma_start(out=kld[d][j][:], in_=KTv[:, :, tok0:tok0 + 128])), reads=[k.b_QK], writes=[b_kld[d][j]])
            S.dma("sp", (lambda e, d=d, j=j, tok0=tok0: e.dma_start(out=vld[d][j][:], in_=k.V[tok0:tok0 + 128, :])), reads=[k.b_V], writes=[b_vld[d][j]])

    def step(i):
        if i + 1 < nsteps:
            loads(i + 1)
        j = i % 2
        chains = [(d, h) for d in range(2) for h in range(4)]
        for (d, h) in chains:
            c = d * 4 + h; p = c % 4
            S.op("pe", (lambda e, d=d, h=h, p=p: e.matmul(psA[p][:, 0:128], lhsT=gld[d][j][:, h * 128:(h + 1) * 128], rhs=msk[:, d, :],
                                                         start=True, stop=True)), reads=[b_gld[d][j], b_msk], writes=[b_bc[p]])
            S.op("act", (lambda e, c=c, p=p: e.activation(out=eq[c][:], in_=psA[p][:, 0:128], func=AF.Exp)), reads=[b_bc[p]], writes=[b_eq[c]])
            S.op("act", (lambda e, c=c, p=p: e.activation(out=ek[c][:], in_=psA[p][:, 0:128], func=AF.Exp, scale=-1.0)), reads=[b_bc[p]], writes=[b_ek[c]])
            S.op("dve", (lambda e, c=c, d=d, h=h: e.tensor_tensor(out=qtl[c][:], in0=qld[d][j][:, h, :], in1=eq[c][:], op=ALU.mult)),
                 reads=[b_qld[d][j], b_eq[c]], writes=[b_qtl[c]])
            S.op("dve", (lambda e, c=c, d=d, h=h: e.tensor_tensor(out=ktl[c][:], in0=kld[d][j][:, h, :], in1=ek[c][:], op=ALU.mult)),
                 reads=[b_kld[d][j], b_ek[c]], writes=[b_ktl[c]])
        for (d, h) in chains:
            c = d * 4 + h; p = c % 4
            S.op("pe", (lambda e, c=c, p=p: e.matmul(psA[p][:, 128:256], lhsT=ktl[c][:], rhs=qtl[c][:], start=True, stop=True)),
                 reads=[b_ktl[c], b_qtl[c]], writes=[b_att[p]])
            S.op("dve", (lambda e, c=c, p=p, d=d: e.tensor_tensor(out=attm[c][:], in0=psA[p][:, 128:256], in1=msk[:, d, :], op=ALU.mult)),
                 reads=[b_att[p], b_msk], writes=[b_attm[c]])
            S.op("pe", (lambda e, c=c, p=p: e.transpose(psKT[p], ktl[c][:], k.identb[:])), reads=[b_ktl[c], k.b_const], writes=[b_kt[p]])
            S.op("act", (lambda e, c=c, p=p: e.activation(out=ktok[c][:], in_=psKT[p], func=AF.Copy)), reads=[b_kt[p]], writes=[b_ktok[c]])
        for (d, h) in chains:
            c = d * 4 + h; p = c % 4
            tok0 = order[d][i]
            for vc in range(2):
                S.op("pe", (lambda e, c=c, p=p, vc=vc: e.matmul(psA[p][:, 256 + vc * 128:384 + vc * 128], lhsT=Sb[:, c, vc * 128:(vc + 1) * 128],
                                                               rhs=qtl[c][:], start=True, stop=False)), reads=[b_Sb[c], b_qtl[c]], writes=[b_o[p]])
                S.op("pe", (lambda e, c=c, p=p, vc=vc, d=d, h=h: e.matmul(psA[p][:, 256 + vc * 128:384 + vc * 128],
                                                                         lhsT=vld[d][j][:, h * 256 + vc * 128:h * 256 + (vc + 1) * 128],
                                                                         rhs=attm[c][:], start=False, stop=True)),
                     reads=[b_vld[d][j], b_attm[c]], writes=[b_o[p]])
            S.op("act", (lambda e, c=c, p=p: e.activation(out=osb[c][:], in_=psA[p][:, 256:512].rearrange("p (a b) -> p a b", a=2), func=AF.Copy)),
                 reads=[b_o[p]], writes=[b_osb[c]])
            S.dma("sp", (lambda e, c=c, d=d, h=h, tok0=tok0: e.dma_start(out=Ov[d][:, 2 * h:2 * h + 2, tok0:tok0 + 128], in_=osb[c][:])),
                  reads=[b_osb[c]], writes=[k.b_O])
            S.op("pe", (lambda e, c=c, p=p, d=d, h=h: e.matmul(psKV[p], lhsT=ktok[c][:], rhs=vld[d][j][:, h * 256:(h + 1) * 256],
                                                              start=True, stop=True)), reads=[b_ktok[c], b_vld[d][j]], writes=[b_kv[p]])
            el = eq[c][:, 127:128] if d == 0 else eq[c][:, 0:1]
            S.op("dve", (lambda e, c=c, el=el: e.tensor_scalar(out=St[:, c, :], in0=St[:, c, :], scalar1=el, scalar2=None, op0=ALU.mult)),
                 reads=[b_S[c], b_eq[c]], writes=[b_S[c]])
            S.op("dve", (lambda e, c=c, p=p, el=el: e.scalar_tensor_tensor(out=St[:, c, :], in0=psKV[p], scalar=el, in1=St[:, c, :],
                                                                          op0=ALU.mult, op1=ALU.add)), reads=[b_kv[p], b_S[c], b_eq[c]], writes=[b_S[c]])
            S.op("act", (lambda e, c=c: e.activation(out=Sb[:, c, :], in_=St[:, c, :], func=AF.Copy)), reads=[b_S[c]], writes=[b_Sb[c]])

    loads(0)
    for i in range(nsteps):
        step(i)
    S.end_phase()


def phase_fnet(k):
    S = k.S
    nlat, nctx, ntok = k.nlat, k.nctx, k.ntok
    S.begin_phase()
    cc = S.sbuf("fcc", [128, 3, 2, 256], BF16)
    b_cc = Buf()
    for a_ in range(3):
        S.dma("sp", lambda e, a_=a_: e.dma_start(out=cc[:, a_, :, :], in_=k.dftc[a_].rearrange("(c2 p) c -> p c2 c", p=128)), writes=[b_cc], key=("cc", a_))
    ft = [S.sbuf(f"fft{i}", [128, 8, 128], BF16) for i in range(2)]
    b_ft = [Buf(), Buf()]
    gsb = [S.sbuf(f"fgsb{i}", [128, 2, 1024], BF16) for i in range(2)]
    b_gsb = [Buf(), Buf()]
    ps1 = [S.psum(f"fps1_{i}", [128, 2, 256]) for i in range(4)]
    b_ps1 = [Buf() for _ in range(4)]
    FTv = k.FT.rearrange("(c p) t -> p c t", p=128)
    ntile = ntok // 128
    pc = 0
    for tt in range(ntile):
        tok0 = tt * 128
        f_t, bf = ft[tt % 2], b_ft[tt % 2]
        g_t, bg = gsb[tt % 2], b_gsb[tt % 2]
        S.dma("sp", (lambda e, f_t=f_t, tok0=tok0: e.dma_start(out=f_t[:], in_=FTv[:, :, tok0:tok0 + 128])), reads=[k.b_FT], writes=[bf])
        for g in range(4):
            p_t, bp = ps1[pc % 4], b_ps1[pc % 4]
            pc += 1
            for a in range(2):
                for c2 in range(2):
                    S.op("pe", (lambda e, p_t=p_t, f_t=f_t, g=g, a=a, c2=c2: e.matmul(
                        p_t[:, a, :], lhsT=f_t[:, 2 * g + c2, :], rhs=cc[:, a, c2, :], start=(c2 == 0), stop=(c2 == 1))),
                        reads=[bf, b_cc], writes=[bp])
            S.op("act" if g % 2 == 0 else "dve",
                 (lambda e, p_t=p_t, g_t=g_t, g=g: (e.activation(out=g_t[:, :, g * 256:(g + 1) * 256], in_=p_t[:, :, :], func=AF.Copy)
                                                    if g % 2 == 0 else e.tensor_copy(out=g_t[:, :, g * 256:(g + 1) * 256], in_=p_t[:, :, :]))),
                 reads=[bp], writes=[bg])
        S.dma("pool", (lambda e, g_t=g_t, tok0=tok0: e.dma_start(out=k.GCS[tok0:tok0 + 128, :, :], in_=g_t[:])), reads=[bg], writes=[k.b_GCS])
    S.end_phase()
    S.begin_phase()
    cc = S.sbuf("fcc2", [128, 3, 2, 256], BF16)
    b_cc = Buf()
    for a_ in range(3):
        S.dma("sp", lambda e, a_=a_: e.dma_start(out=cc[:, a_, :, :], in_=k.dftc[a_].rearrange("(c2 p) c -> p c2 c", p=128)), writes=[b_cc], key=("cc", a_))
    ntc = nlat // 128
    G = [S.sbuf("fG0", [128, ntc, 2, 256], BF16)] * 2
    b_G = [Buf()] * 2
    Gc = [S.sbuf(f"fGc{i}", [128, 2, 2, 256], BF16) for i in range(2)]
    b_Gc = [Buf(), Buf()]
    fo = [S.sbuf(f"ffo{i}", [128, 512], BF16) for i in range(2)]
    b_fo = [Buf(), Buf()]
    ps2 = [S.psum(f"fps2_{i}", [128, 512]) for i in range(2)]
    b_ps2 = [Buf(), Buf()]
    ntp = nlat // 512
    srcs = []
    for g in range(4):
        for tp in range(ntp):
            srcs.append([((lambda tt, a_=a_: tt[:, a_, :, :]), k.dftT[a_].rearrange("(tc p) t -> p tc t", p=128)[:, :, tp * 512:(tp + 1) * 512]) for a_ in range(2)])
    ringT = Ring(S, "fT", [128, 2, ntc, 512], BF16, srcs, 2, q="sp", nparts=2)
    FMv = k.FMT.rearrange("(c p) t -> p c t", p=128)
    GCSv = k.GCS.rearrange("(tc p) a c -> p tc a c", p=128)
    ev = 0
    oc_ = 0
    for g in range(4):
        G_t, bG = G[g % 2], b_G[g % 2]
        for a_ in range(2):
            S.dma("sp", (lambda e, G_t=G_t, g=g, a_=a_: e.dma_start(out=G_t[:, :, a_, :], in_=GCSv[:, 0:ntc, a_, g * 256:(g + 1) * 256])), reads=[k.b_GCS], writes=[bG], key=("fG", a_))
        Gc_t, bGc = Gc[g % 2], b_Gc[g % 2]
        for a_ in range(2):
            S.dma("pool", (lambda e, Gc_t=Gc_t, g=g, a_=a_: e.dma_start(out=Gc_t[:, :, a_, :], in_=GCSv[:, ntc:ntc + 2, a_, g * 256:(g + 1) * 256])), reads=[k.b_GCS], writes=[bGc], key=("fGc", a_))
        for tp in range(ntp):
            ringT.prefetch(ev)
            T_t, bT = ringT.get(ev)
            ev += 1
            for c2 in range(2):
                p_t, bp = ps2[oc_ % 2], b_ps2[oc_ % 2]
                f_t, bfo = fo[oc_ % 2], b_fo[oc_ % 2]
                oc_ += 1
                n = 0
                for a in range(2):
                    for tc in range(ntc):
                        S.op("pe", (lambda e, p_t=p_t, G_t=G_t, T_t=T_t, a=a, tc=tc, c2=c2, n=n: e.matmul(
                            p_t[:, :], lhsT=G_t[:, tc, a, c2 * 128:(c2 + 1) * 128], rhs=T_t[:, a, tc, :],
                            start=(n == 0), stop=(n == 2 * ntc - 1))), reads=[bG] + bT, writes=[bp])
                        n += 1
                S.op("act", (lambda e, p_t=p_t, f_t=f_t: e.activation(out=f_t[:], in_=p_t[:, :], func=AF.Copy)), reads=[bp], writes=[bfo])
                S.dma("pool", (lambda e, f_t=f_t, g=g, c2=c2, tp=tp: e.dma_start(out=FMv[:, 2 * g + c2, tp * 512:(tp + 1) * 512], in_=f_t[:])),
                      reads=[bfo], writes=[k.b_FMT])
        for c2 in range(2):
            p_t, bp = ps2[oc_ % 2], b_ps2[oc_ % 2]
            f_t, bfo = fo[oc_ % 2], b_fo[oc_ % 2]
            oc_ += 1
            n = 0
            for a in range(2):
                tab = 0 if a == 0 else 2
                for tc in range(2):
                    S.op("pe", (lambda e, p_t=p_t, Gc_t=Gc_t, a=a, tc=tc, c2=c2, n=n, tab=tab: e.matmul(
                        p_t[:, 0:256], lhsT=Gc_t[:, tc, a, c2 * 128:(c2 + 1) * 128], rhs=cc[:, tab, tc, :],
                        start=(n == 0), stop=(n == 3))), reads=[bGc, b_cc], writes=[bp])
                    n += 1
            S.op("act", (lambda e, p_t=p_t, f_t=f_t: e.activation(out=f_t[:, 0:256], in_=p_t[:, 0:256], func=AF.Copy)), reads=[bp], writes=[bfo])
            S.dma("pool", (lambda e, f_t=f_t, g=g, c2=c2: e.dma_start(out=FMv[:, 2 * g + c2, nlat:nlat + 256], in_=f_t[:, 0:256])),
                  reads=[bfo], writes=[k.b_FMT])
    S.end_phase()


def phase_l0out(k, tiles):
    S = k.S
    TAB, bTAB = k.TAB[0], k.b_TAB[0]
    S.begin_phase()
    XTv = k.XT.rearrange("(kc p) t -> p kc t", p=128)
    Ofv = k.OF.rearrange("(c p) t -> p c t", p=128)
    Obv = k.OB.rearrange("(c p) t -> p c t", p=128)
    RGv = k.RG.rearrange("(c p) t -> p c t", p=128)
    FMv = k.FMT.rearrange("(c p) t -> p c t", p=128)
    ring_wo = Ring(S, "wo", [128, KC, 128], BF16, [k.wout0B[oc] for _ in range(len(tiles)) for oc in range(KC)], 8, q="sp", rbufs=[k.b_wout0B])
    xt = [S.sbuf(f"xt{i}", [128, KC, 512], F32) for i in range(2)]
    b_xt = [Buf(), Buf()]
    of_ = [S.sbuf("of0", [128, 8, 512], F32)] * 2
    b_of = [Buf()] * 2
    ob_ = [S.sbuf("ob0", [128, 8, 512], F32)] * 2
    b_ob = [Buf()] * 2
    rgt = [S.sbuf("rgt0", [128, 8, 512], F32)] * 2
    b_rgt = [Buf()] * 2
    mixin = [S.sbuf(f"mixin{i}", [128, KC, 512], BF16) for i in range(2)]
    b_mo = [Buf(), Buf()]
    b_mf = [Buf(), Buf()]
    nrm = NormCtx(k)
    ps_y = [S.psum(f"ps_y{i}", [128, 512]) for i in range(2)]
    b_y = [Buf(), Buf()]
    nT = len(tiles)

    def loads(ti):
        t0, nt, cls = tiles[ti]
        i = ti % 2
        S.dma("pool", (lambda e: e.dma_start(out=xt[i][:, :, 0:nt], in_=XTv[:, :, t0:t0 + nt])), reads=[k.b_XT], writes=[b_xt[i]])
        S.dma("pool", (lambda e: e.dma_start(out=of_[i][:, :, 0:nt], in_=Ofv[:, :, t0:t0 + nt])), reads=[k.b_O], writes=[b_of[i]])
        S.dma("pool", (lambda e: e.dma_start(out=ob_[i][:, :, 0:nt], in_=Obv[:, :, t0:t0 + nt])), reads=[k.b_O], writes=[b_ob[i]])
        S.dma("pool", (lambda e: e.dma_start(out=rgt[i][:, :, 0:nt], in_=RGv[:, :, t0:t0 + nt])), reads=[k.b_RG], writes=[b_rgt[i]])
        S.dma("pool", (lambda e: e.dma_start(out=mixin[i][:, 8:16, 0:nt], in_=FMv[:, :, t0:t0 + nt])), reads=[k.b_FMT], writes=[b_mf[i]])

    def prep(ti):
        t0, nt, cls = tiles[ti]
        i = ti % 2
        S.op("pool", (lambda e: e.tensor_tensor(out=of_[i][:, :, 0:nt], in0=of_[i][:, :, 0:nt], in1=ob_[i][:, :, 0:nt], op=ALU.add)),
             reads=[b_of[i], b_ob[i]], writes=[b_of[i]])
        for h in range(4):
            nrm.stats(of_[i][:, 2 * h:2 * h + 2, :], b_of[i], nt, nchunks=2, dim=256)
            for vc in range(2):
                c = 2 * h + vc
                S.op("dve", (lambda e, c=c: e.tensor_tensor(out=rgt[i][:, c, 0:nt], in0=rgt[i][:, c, 0:nt], in1=nrm.rstd[:, 0:nt], op=ALU.mult)),
                     reads=[b_rgt[i], nrm.b_rstd], writes=[b_rgt[i]])
                S.op("dve", (lambda e, c=c: e.tensor_tensor(out=mixin[i][:, c, 0:nt], in0=rgt[i][:, c, 0:nt], in1=of_[i][:, c, 0:nt], op=ALU.mult)),
                     reads=[b_rgt[i], b_of[i]], writes=[b_mo[i]])

    def outp(ti):
        t0, nt, cls = tiles[ti]
        i = ti % 2
        for oc in range(KC):
            py, by = ps_y[oc % 2], b_y[oc % 2]
            evn = ti * KC + oc
            ring_wo.prefetch(evn)
            wo, b_wo = ring_wo.get(evn)
            for kc in range(KC):
                S.op("pe", (lambda e, py=py, wo=wo, kc=kc: e.matmul(py[:, 0:nt], lhsT=wo[:, kc, :], rhs=mixin[i][:, kc, 0:nt],
                                                                   start=(kc == 0), stop=(kc == KC - 1))),
                     reads=[b_wo, b_mo[i], b_mf[i]], writes=[by])
            S.op("dve", (lambda e, py=py, oc=oc: e.scalar_tensor_tensor(
                out=xt[i][:, oc, 0:nt], in0=py[:, 0:nt], scalar=TAB[:, cls, 5, oc:oc + 1], in1=xt[i][:, oc, 0:nt],
                op0=ALU.mult, op1=ALU.add)), reads=[by, b_xt[i], bTAB], writes=[b_xt[i]])
        S.dma("pool", (lambda e: e.dma_start(out=XTv[:, :, t0:t0 + nt], in_=xt[i][:, :, 0:nt])), reads=[b_xt[i]], writes=[k.b_XT])

    loads(0)
    prep(0)
    for ti in range(nT):
        if ti + 1 < nT:
            loads(ti + 1)
            prep(ti + 1)
        outp(ti)
    S.end_phase()


NEG = -30000.0


def na_geometry(rows):
    kh = min(8, rows)
    nch = min(5, rows // 2)
    classes = {}
    rp_cls, rp_lo = [], []
    for rp in range(rows // 2):
        r = 2 * rp
        lo = int(np.clip(r - 4, 0, rows - 2 * nch))
        key = []
        for j in range(nch):
            for a in range(2):
                for b in range(2):
                    kr = lo + 2 * j + a
                    rq = r + b
                    r0 = int(np.clip(rq - kh // 2, 0, rows - kh))
                    valid = (r0 <= kr < r0 + kh)
                    key.append((valid, kr - rq + 7 if valid else 0))
        key = tuple(key)
        if key not in classes:
            classes[key] = len(classes)
        rp_cls.append(classes[key])
        rp_lo.append(lo)
    return kh, nch, classes, rp_cls, rp_lo


def na_bias_tables(rpb, rows):
    kh, nch, classes, rp_cls, rp_lo = na_geometry(rows)
    H = rpb.shape[0]
    c = np.arange(64)
    col_start = np.clip(c - 8, 0, 48)
    cmask = (c[None, :] >= col_start[:, None]) & (c[None, :] < col_start[:, None] + 16)
    dc = np.clip(c[None, :] - c[:, None] + 15, 0, 30)
    out = np.full((H, len(classes), 128, nch * 128), NEG, np.float32)
    for key, cid in classes.items():
        n = 0
        for j in range(nch):
            for a in range(2):
                for b in range(2):
                    valid, dr = key[n]
                    n += 1
                    if not valid:
                        continue
                    blk = rpb[:, dr][:, dc]
                    blk = np.where(cmask[None], blk, NEG)
                    out[:, cid, b * 64:(b + 1) * 64, j * 128 + a * 64:j * 128 + (a + 1) * 64] = blk
    return out


def precast_l1(k):
    k.wqkvFM, k.b_wqkvFM = precast_blocks(k, "wqkvFM", k.w_qkv, D, [128 * i for i in range(32)], 128)
    k.wqkvV, k.b_wqkvV = precast_blocks(k, "wqkvV", k.w_qkv, D, [2 * D + 512 * i for i in range(4)], 512)
    k.wout1B, k.b_wout1B = precast_blocks(k, "wout1B", k.w_out1, D, [128 * o for o in range(KC)], 128)


def phase_l1proj(k, tiles):
    S = k.S
    TAB, bTAB = k.TAB[1], k.b_TAB[1]
    S.begin_phase()
    XTv = k.XT.rearrange("(kc p) t -> p kc t", p=128)
    xt = S.sbuf("xt0", [128, KC, 512], F32)
    b_xt = Buf()
    uts = [S.sbuf(f"ut{i}", [128, KC, 512], BF16) for i in range(2)]
    b_uts = [Buf(), Buf()]
    nrm = NormCtx(k)
    ob = [S.sbuf(f"ob{i}", [128, 512], BF16) for i in range(4)]
    b_ob = [Buf() for _ in range(4)]
    vt = [S.sbuf(f"vt{i}", [128, 2048], BF16) for i in range(2)]
    b_vt = [Buf(), Buf()]
    ps_fm = [S.psum(f"ps_fm{i}", [128, 512]) for i in range(3)]
    b_fm = [Buf() for _ in range(3)]
    ps_v = [S.psum(f"ps_v{i}", [128, 512]) for i in range(4)]
    b_v = [Buf() for _ in range(4)]
    nT = len(tiles)
    fm_srcs, v_srcs = [], []
    for ti in range(nT):
        t0, nt, cls = tiles[ti]
        for nb in range(32):
            if cls == 1 and nb < 16:
                continue
            fm_srcs.append(k.wqkvFM[nb])
        for s_ in range(nt // 128):
            v_srcs += [k.wqkvV[i] for i in range(4)]
    ring_fm = Ring(S, "wfm", [128, KC, 128], BF16, fm_srcs, 8, q="sp", rbufs=[k.b_wqkvFM])
    wv_res = S.sbuf("wvres", [128, 4, KC, 512], BF16)
    b_wv = [Buf() for _ in range(4)]
    for hf in range(4):
        S.dma("sp", (lambda e, hf=hf: e.dma_start(out=wv_res[:, hf, :, :], in_=k.wqkvV[hf])), reads=[k.b_wqkvV], writes=[b_wv[hf]])
    QTv = k.QT1.rearrange("(c p) t -> p c t", p=128)
    KTv = k.KT1.rearrange("(c p) t -> p c t", p=128)
    fm_ev, v_ev, pc = [0], [0], [0]
    def loadnorm(ti):
        t0, nt, cls = tiles[ti]
        S.dma("pool", (lambda e, t0=t0, nt=nt: e.dma_start(out=xt[:, :, 0:nt], in_=XTv[:, :, t0:t0 + nt])), reads=[k.b_XT], writes=[b_xt])
        nrm.stats(xt, b_xt, nt)
        nrm.apply(xt, b_xt, nt, TAB[:, cls, 3, :], TAB[:, cls, 4, :], bTAB, uts[ti % 2], b_uts[ti % 2])

    loadnorm(0)
    for ti in range(nT):
        t0, nt, cls = tiles[ti]
        ut, b_ut = uts[ti % 2], b_uts[ti % 2]
        ndone = 0
        for nb in range(32):
            if cls == 1 and nb < 16:
                continue
            ring_fm.prefetch(fm_ev[0])
            w_t, bw = ring_fm.get(fm_ev[0]); fm_ev[0] += 1
            i = pc[0] % 3; o_i = pc[0] % 4; pc[0] += 1
            p_t, bp = ps_fm[i], b_fm[i]
            for kc in range(KC):
                S.op("pe", (lambda e, p_t=p_t, w_t=w_t, kc=kc, nt=nt, ut=ut: e.matmul(
                    p_t[:, 0:nt], lhsT=w_t[:, kc, :], rhs=ut[:, kc, 0:nt], start=(kc == 0), stop=(kc == KC - 1))),
                    reads=[bw, b_ut], writes=[bp])
            sc = 0.125 if nb < 16 else 1.0
            if pc[0] % 2 == 0:
                S.op("act", (lambda e, p_t=p_t, o_i=o_i, nt=nt, sc=sc: e.activation(out=ob[o_i][:, 0:nt], in_=p_t[:, 0:nt], func=AF.Identity, scale=sc)),
                     reads=[bp], writes=[b_ob[o_i]])
            else:
                S.op("dve", (lambda e, p_t=p_t, o_i=o_i, nt=nt, sc=sc: e.tensor_scalar(out=ob[o_i][:, 0:nt], in0=p_t[:, 0:nt], scalar1=sc, scalar2=None, op0=ALU.mult)),
                     reads=[bp], writes=[b_ob[o_i]])
            dst = QTv[:, nb, t0:t0 + nt] if nb < 16 else KTv[:, nb - 16, t0:t0 + nt]
            S.dma("pool", (lambda e, o_i=o_i, dst=dst, nt=nt: e.dma_start(out=dst, in_=ob[o_i][:, 0:nt])), reads=[b_ob[o_i]], writes=[k.b_QK1])
            ndone += 1
            if ndone == 12 and ti + 1 < nT:
                loadnorm(ti + 1)
        for s_ in range(nt // 128):
            tok0 = t0 + s_ * 128
            v_t, bv = vt[s_ % 2], b_vt[s_ % 2]
            for q4 in range(4):
                w_t, bw = wv_res[:, q4, :, :], b_wv[q4]
                pv, bpv = ps_v[q4], b_v[q4]
                for kc in range(KC):
                    S.op("pe", (lambda e, pv=pv, w_t=w_t, kc=kc, s_=s_, ut=ut: e.matmul(
                        pv[:, :], lhsT=ut[:, kc, s_ * 128:(s_ + 1) * 128], rhs=w_t[:, kc, :], start=(kc == 0), stop=(kc == KC - 1))),
                        reads=[bw, b_ut], writes=[bpv])
                if q4 % 2 == 0:
                    S.op("act", (lambda e, pv=pv, v_t=v_t, q4=q4: e.activation(out=v_t[:, q4 * 512:(q4 + 1) * 512], in_=pv[:, :], func=AF.Copy)),
                         reads=[bpv], writes=[bv])
                else:
                    S.op("dve", (lambda e, pv=pv, v_t=v_t, q4=q4: e.tensor_copy(out=v_t[:, q4 * 512:(q4 + 1) * 512], in_=pv[:, :])),
                         reads=[bpv], writes=[bv])
            S.dma("pool", (lambda e, v_t=v_t, tok0=tok0: e.dma_start(out=k.V1[tok0:tok0 + 128, :], in_=v_t[:])), reads=[bv], writes=[k.b_V1])
    S.end_phase()


def phase_na(k):
    S = k.S
    nlat, nctx, ntok = k.nlat, k.nctx, k.ntok
    rows = nlat // 64
    kh, nch, classes, rp_cls, rp_lo = na_geometry(rows)
    ncls = len(classes)
    nrp = rows // 2
    nlk = nch * 128
    nk = nlk + nctx
    nkc = nk // 128
    ntile = ntok // 128
    S.begin_phase()
    qT = [S.sbuf(f"naq{i}", [128, nlat], BF16) for i in range(2)]
    kT = [S.sbuf(f"nak{i}", [128, ntok], BF16) for i in range(2)]
    Vh = [S.sbuf(f"nav{i}", [128, ntile, 128], BF16) for i in range(2)]
    TB = [S.sbuf(f"natb{i}", [128, 2, ncls, nlk], BF16) for i in range(2)]
    Op = [S.sbuf(f"nao{i}", [128, nrp, 128], BF16) for i in range(2)]
    b_qT = [Buf(), Buf()]; b_kT = [Buf(), Buf()]; b_Vh = [Buf(), Buf()]; b_TB = [[Buf(), Buf()], [Buf(), Buf()]]; b_Op = [Buf(), Buf()]
    P = [S.sbuf(f"nap{i}", [128, nk], BF16) for i in range(3)]
    b_P = [Buf() for _ in range(3)]
    PT = [S.sbuf(f"napt{i}", [128, nkc, 128], BF16) for i in range(3)]
    b_PT = [Buf() for _ in range(3)]
    st = [S.sbuf(f"nast{i}", [128, 4], F32) for i in range(3)]
    b_st = [Buf() for _ in range(3)]
    psS = [S.psum(f"napsS{i}", [128, 1024]) for i in range(2)]
    b_S = [Buf(), Buf()]
    psT = [S.psum(f"napsT{i}", [128, 1024], BF16) for i in range(2)]
    b_T = [Buf(), Buf()]
    psO = [S.psum(f"napsO{i}", [128, 512]) for i in range(2)]
    b_O = [Buf(), Buf()]
    QTv = k.QT1.rearrange("(c p) t -> p c t", p=128)
    KTv = k.KT1.rearrange("(c p) t -> p c t", p=128)
    V1v = k.V1.rearrange("(tt p) c -> p tt c", p=128)
    O1v = k.O1.rearrange("(rp p) c -> p rp c", p=128)

    def loads(hp):
        i = hp % 2
        S.dma("sp", (lambda e: e.dma_start(out=qT[i][:], in_=QTv[:, hp, :])), reads=[k.b_QK1], writes=[b_qT[i]])
        S.dma("sp", (lambda e: e.dma_start(out=kT[i][:], in_=KTv[:, hp, :])), reads=[k.b_QK1], writes=[b_kT[i]])
        S.dma("sp", (lambda e: e.dma_start(out=Vh[i][:], in_=V1v[:, :, hp * 128:(hp + 1) * 128])), reads=[k.b_V1], writes=[b_Vh[i]])
        for hh in range(2):
            S.dma("pool", (lambda e, hh=hh: e.dma_start(out=TB[i][:, hh, :, :], in_=k.nabias[2 * hp + hh].rearrange("c q n -> q c n"))),
                  writes=[b_TB[i][hh]])

    NB = 3

    def stageA(u, hp, hh, rp):
        i = hp % 2
        pb = 64 * hh
        lo = rp_lo[rp]
        cls = rp_cls[rp]
        kbase = lo * 64
        q_ap = qT[i][pb:pb + 64, rp * 128:(rp + 1) * 128]
        Sp, bS = psS[u % 2], b_S[u % 2]
        n0 = min(512, nlk)
        S.op("pe", (lambda e: e.matmul(Sp[:, 0:n0], lhsT=q_ap, rhs=kT[i][pb:pb + 64, kbase:kbase + n0], start=True, stop=False)),
             reads=[b_qT[i], b_kT[i]], writes=[bS])
        S.op("pe", (lambda e: e.matmul(Sp[:, 0:n0], lhsT=k.identb[:], rhs=TB[i][:, hh, cls, 0:n0], start=False, stop=True)),
             reads=[b_TB[i][hh], k.b_const], writes=[bS])
        if nlk > 512:
            S.op("pe", (lambda e: e.matmul(Sp[:, 512:nlk], lhsT=q_ap, rhs=kT[i][pb:pb + 64, kbase + 512:kbase + nlk], start=True, stop=False)),
                 reads=[b_qT[i], b_kT[i]], writes=[bS])
            S.op("pe", (lambda e: e.matmul(Sp[:, 512:nlk], lhsT=k.identb[:], rhs=TB[i][:, hh, cls, 512:nlk], start=False, stop=True)),
                 reads=[b_TB[i][hh], k.b_const], writes=[bS])
            c0 = nlk
        else:
            c0 = 512
        S.op("pe", (lambda e: e.matmul(Sp[:, c0:c0 + nctx], lhsT=q_ap, rhs=kT[i][pb:pb + 64, nlat:nlat + nctx], start=True, stop=True)),
             reads=[b_qT[i], b_kT[i]], writes=[bS])
        single = (c0 == nlk)
        s_t, bst = st[u % NB], b_st[u % NB]
        if single:
            S.op("dve", (lambda e: e.tensor_reduce(out=s_t[:, 0:1], in_=Sp[:, 0:nk], axis=AX.X, op=ALU.max)), reads=[bS], writes=[bst])
        else:
            S.op("dve", (lambda e: e.tensor_reduce(out=s_t[:, 0:1], in_=Sp[:, 0:nlk], axis=AX.X, op=ALU.max)), reads=[bS], writes=[bst])
            S.op("dve", (lambda e: e.tensor_reduce(out=s_t[:, 1:2], in_=Sp[:, c0:c0 + nctx], axis=AX.X, op=ALU.max)), reads=[bS], writes=[bst])
            S.op("dve", (lambda e: e.tensor_tensor(out=s_t[:, 0:1], in0=s_t[:, 0:1], in1=s_t[:, 1:2], op=ALU.max)), reads=[bst], writes=[bst])
        S.op("dve", (lambda e: e.tensor_scalar(out=s_t[:, 1:2], in0=s_t[:, 0:1], scalar1=-1.0, scalar2=None, op0=ALU.mult)), reads=[bst], writes=[bst])
        P_t, bP = P[u % NB], b_P[u % NB]
        if single:
            S.op("act", (lambda e: e.activation(out=P_t[:, 0:nk], in_=Sp[:, 0:nk], func=AF.Exp, bias=s_t[:, 1:2], scale=1.0, accum_out=s_t[:, 2:3])),
                 reads=[bS, bst], writes=[bP, bst])
        else:
            S.op("act", (lambda e: e.activation(out=P_t[:, 0:nlk], in_=Sp[:, 0:nlk], func=AF.Exp, bias=s_t[:, 1:2], scale=1.0, accum_out=s_t[:, 2:3])),
                 reads=[bS, bst], writes=[bP, bst])
            S.op("act", (lambda e: e.activation(out=P_t[:, nlk:nk], in_=Sp[:, c0:c0 + nctx], func=AF.Exp, bias=s_t[:, 1:2], scale=1.0, accum_out=s_t[:, 3:4])),
                 reads=[bS, bst], writes=[bP, bst])
            S.op("dve", (lambda e: e.tensor_tensor(out=s_t[:, 2:3], in0=s_t[:, 2:3], in1=s_t[:, 3:4], op=ALU.add)), reads=[bst], writes=[bst])

    def stageB(u, hp, hh, rp):
        s_t, bst = st[u % NB], b_st[u % NB]
        P_t, bP = P[u % NB], b_P[u % NB]
        S.op("dve", (lambda e: e.reciprocal(out=s_t[:, 3:4], in_=s_t[:, 2:3])), reads=[bst], writes=[bst])
        Tp, bT = psT[u % 2], b_T[u % 2]
        for j in range(nkc):
            S.op("pe", (lambda e, j=j: e.transpose(Tp[:, j * 128:(j + 1) * 128], P_t[:, j * 128:(j + 1) * 128], k.identb[:])),
                 reads=[bP, k.b_const], writes=[bT])
        PT_t, bPT = PT[u % NB], b_PT[u % NB]
        if u % 2 == 0:
            S.op("act", (lambda e: e.activation(out=PT_t[:].rearrange("p a b -> p (a b)"), in_=Tp[:, 0:nk], func=AF.Copy)), reads=[bT], writes=[bPT])
        else:
            S.op("dve", (lambda e: e.tensor_copy(out=PT_t[:].rearrange("p a b -> p (a b)"), in_=Tp[:, 0:nk])), reads=[bT], writes=[bPT])

    def stageC(u, hp, hh, rp):
        i = hp % 2
        pb = 64 * hh
        kbase = rp_lo[rp] * 64
        s_t, bst = st[u % NB], b_st[u % NB]
        PT_t, bPT = PT[u % NB], b_PT[u % NB]
        Opp, bO = psO[u % 2], b_O[u % 2]
        for j in range(nkc):
            tt = (kbase // 128 + j) if j < nch else (nlat // 128 + (j - nch))
            S.op("pe", (lambda e, j=j, tt=tt: e.matmul(Opp[:, 0:64], lhsT=PT_t[:, j, :], rhs=Vh[i][:, tt, pb:pb + 64], start=(j == 0), stop=(j == nkc - 1))),
                 reads=[bPT, b_Vh[i]], writes=[bO])
        S.op("dve", (lambda e: e.tensor_scalar(out=Op[i][:, rp, pb:pb + 64], in0=Opp[:, 0:64], scalar1=s_t[:, 3:4], scalar2=None, op0=ALU.mult)),
             reads=[bO, bst], writes=[b_Op[i]])
        if rp == nrp - 1 and hh == 1:
            S.dma("sp", (lambda e: e.dma_start(out=O1v[:, :, hp * 128:(hp + 1) * 128], in_=Op[i][:])), reads=[b_Op[i]], writes=[k.b_O1])

    units = [(hp, hh, rp) for hp in range(16) for rp in range(nrp) for hh in range(2)]
    loads(0)
    nU = len(units)
    for n in range(nU + 2):
        if n < nU:
            stageA(n, *units[n])
        if 0 <= n - 1 < nU:
            stageB(n - 1, *units[n - 1])
        if 0 <= n - 2 < nU:
            stageC(n - 2, *units[n - 2])
        if n < nU:
            hp, hh, rp = units[n]
            if n == hp * 2 * nrp + 1 and hp + 1 < 16:
                loads(hp + 1)
    S.end_phase()


def phase_l1out(k, tiles):
    S = k.S
    TAB, bTAB = k.TAB[1], k.b_TAB[1]
    S.begin_phase()
    XTv = k.XT.rearrange("(kc p) t -> p kc t", p=128)
    xt = [S.sbuf(f"xt{i}", [128, KC, 512], F32) for i in range(2)]
    b_xt = [Buf(), Buf()]
    otok = [S.sbuf(f"otok{i}", [128, D], BF16) for i in range(2)]
    b_otok = [Buf(), Buf()]
    mixin = [S.sbuf(f"mixin{i}", [128, KC, 512], BF16) for i in range(2)]
    b_mi = [Buf(), Buf()]
    ps_tr = [S.psum(f"ps_tr{i}", [128, 1024], BF16) for i in range(2)]
    b_tr = [Buf(), Buf()]
    ps_y = [S.psum(f"ps_y{i}", [128, 512]) for i in range(2)]
    b_y = [Buf(), Buf()]
    nT = len(tiles)
    ring_wo = Ring(S, "wo", [128, KC, 128], BF16, [k.wout1B[oc] for _ in range(nT) for oc in range(KC)], 8, q="sp", rbufs=[k.b_wout1B])
    cnt = [0]

    def prep(ti):
        t0, nt, cls = tiles[ti]
        i = ti % 2
        S.dma("pool", (lambda e: e.dma_start(out=xt[i][:, :, 0:nt], in_=XTv[:, :, t0:t0 + nt])), reads=[k.b_XT], writes=[b_xt[i]])
        for s_ in range(nt // 128):
            o_t, bo = otok[s_ % 2], b_otok[s_ % 2]
            tok0 = t0 + s_ * 128
            S.dma("pool", (lambda e, o_t=o_t, tok0=tok0: e.dma_start(out=o_t[:], in_=k.O1[tok0:tok0 + 128, :])), reads=[k.b_O1], writes=[bo])
            for kq in range(KC // 4):
                p_t, bp = ps_tr[cnt[0] % 2], b_tr[cnt[0] % 2]
                cnt[0] += 1
                for k4 in range(4):
                    kc = kq * 4 + k4
                    S.op("pe", (lambda e, p_t=p_t, o_t=o_t, kc=kc, k4=k4: e.transpose(p_t[:, k4 * 128:(k4 + 1) * 128], o_t[:, kc * 128:(kc + 1) * 128], k.identb[:])),
                         reads=[bo, k.b_const], writes=[bp])
                eng = "act" if kq % 2 == 0 else "dve"
                S.op(eng, (lambda e, p_t=p_t, kq=kq, s_=s_, eng=eng: (
                    e.activation(out=mixin[i][:, kq * 4:(kq + 1) * 4, s_ * 128:(s_ + 1) * 128], in_=p_t[:, 0:512].rearrange("p (a b) -> p a b", a=4), func=AF.Copy)
                    if eng == "act" else
                    e.tensor_copy(out=mixin[i][:, kq * 4:(kq + 1) * 4, s_ * 128:(s_ + 1) * 128], in_=p_t[:, 0:512].rearrange("p (a b) -> p a b", a=4)))),
                    reads=[bp], writes=[b_mi[i]])

    def outp(ti):
        t0, nt, cls = tiles[ti]
        i = ti % 2
        for oc in range(KC):
            py, by = ps_y[oc % 2], b_y[oc % 2]
            evn = ti * KC + oc
            ring_wo.prefetch(evn)
            wo, b_wo = ring_wo.get(evn)
            for kc in range(KC):
                S.op("pe", (lambda e, py=py, wo=wo, kc=kc: e.matmul(py[:, 0:nt], lhsT=wo[:, kc, :], rhs=mixin[i][:, kc, 0:nt],
                                                                   start=(kc == 0), stop=(kc == KC - 1))),
                     reads=[b_wo, b_mi[i]], writes=[by])
            S.op("dve", (lambda e, py=py, oc=oc: e.scalar_tensor_tensor(
                out=xt[i][:, oc, 0:nt], in0=py[:, 0:nt], scalar=TAB[:, cls, 5, oc:oc + 1], in1=xt[i][:, oc, 0:nt],
                op0=ALU.mult, op1=ALU.add)), reads=[by, b_xt[i], bTAB], writes=[b_xt[i]])
        S.dma("pool", (lambda e: e.dma_start(out=XTv[:, :, t0:t0 + nt], in_=xt[i][:, :, 0:nt])), reads=[b_xt[i]], writes=[k.b_XT])

    prep(0)
    for ti in range(nT):
        if ti + 1 < nT:
            prep(ti + 1)
        outp(ti)
    S.end_phase()


def tvec(v, n=KC):
    return np.ascontiguousarray(np.asarray(v, np.float32).reshape(n, 128).T)


def host_inputs(inp, b, nlat=NLAT, nctx=NCTX):
    m = {}
    m["x"] = np.ascontiguousarray(inp["x"][b, :nlat])
    m["ctx"] = np.ascontiguousarray(inp["ctx"][b, :nctx])
    cv = np.stack([tvec(inp["c"][b]), tvec(inp["c_ctx"])], axis=-1)
    m["cvec"] = np.ascontiguousarray(cv)
    for l in range(2):
        p = f"l{l}_"
        m[p + "ada_w"] = inp[p + "ada_w"]
        m[p + "ada_b_t"] = tvec(inp[p + "ada_b"], 144)
        m[p + "norms_t"] = np.ascontiguousarray(np.concatenate(
            [tvec(inp[p + "norm_ffn1"]), tvec(inp[p + "norm_mix"]), tvec(inp[p + "norm_ffn2"])], axis=1))
        for n in ["ffn1_w_gu", "ffn2_w_gu", "ffn1_w_down", "ffn2_w_down"]:
            m[p + n] = inp[p + n]
    m["norm_out_t"] = tvec(inp["norm_out"])
    m["l0_w_in"] = inp["l0_w_in"]
    gw = np.zeros((33, 2, 512), np.float32)
    gw[0:16, 0] = inp["l0_gla_gate_w_fwd"]; gw[32, 0] = inp["l0_gla_gate_b_fwd"]
    gw[16:32, 1] = inp["l0_gla_gate_w_bwd"]; gw[32, 1] = inp["l0_gla_gate_b_bwd"]
    m["l0_gw_aug"] = gw
    m["l0_gnorm_t"] = tvec(inp["l0_gla_norm"], 8)
    m["l0_w_out"] = inp["l0_w_out"]
    m["l1_w_qkv"] = inp["l1_w_qkv"]
    m["l1_w_out"] = inp["l1_w_out"]
    m["nabias"] = na_bias_tables(np.asarray(inp["l1_rpb"], np.float32), nlat // 64)
    m.update(host_consts(nlat, nctx))
    return m


def _bf16(a):
    import ml_dtypes
    return np.ascontiguousarray(a.astype(ml_dtypes.bfloat16))


_CONST = {}


def host_consts(nlat=NLAT, nctx=NCTX):
    key = (nlat, nctx)
    if key in _CONST:
        return _CONST[key]
    c = {}
    ntok = nlat + nctx
    t = np.arange(nlat)
    row, col = t // 64, t % 64
    inv = (10000.0 ** (-np.arange(32, dtype=np.float64) / 32.0))
    d = np.arange(128)
    pos = np.where(d[:, None] < 64, row[None, :], col[None, :]).astype(np.float64)
    ang = pos * inv[d % 32][:, None]
    cos, sin = np.cos(ang), np.sin(ang)
    sgn = np.where((d % 64) < 32, -1.0, 1.0)[:, None]
    sins = sin * sgn
    cosf = np.concatenate([cos, np.ones((128, nctx))], 1)
    sinf = np.concatenate([sins, np.zeros((128, nctx))], 1)
    sc = 128.0 ** -0.5
    c["rope"] = np.ascontiguousarray(np.stack([cosf * sc, sinf * sc, cosf, sinf]).astype(np.float32))
    i = np.arange(256)
    a = 2 * np.pi * ((i[:, None] * i[None, :]) % 256) / 256.0
    c["dftc"] = _bf16(np.stack([np.cos(a) / 16, np.sin(a) / 16, -np.sin(a) / 16]))
    i = np.arange(nlat, dtype=np.int64)
    a = 2 * np.pi * ((i[:, None] * i[None, :]) % nlat) / float(nlat)
    c["dftT"] = _bf16(np.stack([np.cos(a) / math.sqrt(nlat), -np.sin(a) / math.sqrt(nlat)]))
    _CONST[key] = c
    return c


_NC_CACHE = {}


def kernel(**inputs):
    inp = {k_: np.asarray(v) for k_, v in inputs.items()}
    B = inp["x"].shape[0]
    if "nc" not in _NC_CACHE:
        _NC_CACHE["nc"] = build(dict(nlat=NLAT, nctx=NCTX))
    nc = _NC_CACHE["nc"]
    shared = None
    in_maps = []
    for b in range(B):
        m = host_inputs(inp, b, NLAT, NCTX)
        if shared is None:
            shared = m
        else:
            for k_ in m:
                if k_ not in ("x", "ctx", "cvec"):
                    m[k_] = shared[k_]
        in_maps.append(m)
    res = run_bass_kernel_spmd(nc, in_maps, core_ids=list(range(B)))
    out = np.stack([np.asarray(res.results[b]["out"]) for b in range(B)], axis=0)
    return out.astype(np.float32)
```

```python
import contextlib
import math
import numpy as np
import concourse.bass as bass
import concourse.mybir as mybir
from concourse.bass_utils import run_bass_kernel_spmd

F32 = mybir.dt.float32
BF16 = mybir.dt.bfloat16
AF = mybir.ActivationFunctionType
ALU = mybir.AluOpType
AX = mybir.AxisListType

D = 2048
KC = 16
DFF = 5632
JC = 44
NLAT = 4096
NCTX = 256
NTOK = NLAT + NCTX
EPS = 1e-6

ENGS = ("pe", "act", "dve", "pool", "sp")
SEM_LIMIT = 30000


class Buf:
    __slots__ = ("name", "w", "r", "dram")

    def __init__(self, name="", dram=False):
        self.name = name
        self.w = None
        self.r = {}
        self.dram = dram


class Sched:
    def __init__(self, nc, same_engine_sync=True):
        self.nc = nc
        self.gstack = contextlib.ExitStack()
        self.pstack = None
        self.ops = {e: [] for e in ENGS}
        self.sems = []
        self.prog = {}
        self.known = {e: {} for e in ENGS}
        self.dma_sems = {}
        self.free_dma = {True: [], False: []}
        self.bg_sems = {}
        self.same = same_engine_sync
        self.n_ops = 0
        self.phase_id = 0
        for e in ENGS:
            self.prog[e] = [self._new_sem("p_" + e), 0]

    def _new_sem(self, name):
        s = self.gstack.enter_context(self.nc.semaphore(f"{name}_{len(self.sems)}"))
        self.sems.append(s)
        return len(self.sems) - 1

    def sbuf(self, name, shape, dtype, persist=False):
        st = self.gstack if (persist or self.pstack is None) else self.pstack
        return st.enter_context(self.nc.sbuf_tensor(f"p{self.phase_id}_{name}", list(shape), dtype))

    def psum(self, name, shape, dtype=F32, persist=False):
        st = self.gstack if (persist or self.pstack is None) else self.pstack
        return st.enter_context(self.nc.psum_tensor(f"p{self.phase_id}_{name}", list(shape), dtype))

    def begin_phase(self):
        self.pstack = contextlib.ExitStack()
        self.phase_id += 1

    def end_phase(self):
        self.barrier()
        self.emit()
        for key, d in self.dma_sems.items():
            self.free_dma[key[0]].append(d)
        self.dma_sems = {}
        self.pstack.close()
        self.pstack = None

    def _need(self, eng, tok, waits):
        if tok is None:
            return
        peng, sidx, val = tok
        if peng == eng and (eng == "pe" or not self.same):
            return
        k = self.known[eng]
        if k.get(sidx, 0) >= val:
            return
        k[sidx] = val
        if waits.get(sidx, 0) < val:
            waits[sidx] = val

    def _deps(self, eng, reads, writes):
        waits = {}
        for b in reads:
            self._need(eng, b.w, waits)
        for b in writes:
            self._need(eng, b.w, waits)
            for (peng, sidx), val in b.r.items():
                self._need(eng, (peng, sidx, val), waits)
        return waits

    def _mark(self, tok, reads, writes):
        peng, sidx, val = tok
        for b in reads:
            b.r[(peng, sidx)] = val
        for b in writes:
            b.w = tok
            b.r = {}

    def op(self, eng, fn, reads=(), writes=()):
        waits = self._deps(eng, reads, writes)
        p = self.prog[eng]
        if p[1] >= SEM_LIMIT:
            p[0] = self._new_sem("p_" + eng)
            p[1] = 0
        p[1] += 1
        tok = (eng, p[0], p[1])
        self.ops[eng].append((sorted(waits.items()), fn, (p[0], 1)))
        self._mark(tok, reads, writes)
        self.n_ops += 1
        return tok

    def dma(self, q, fn, reads=(), writes=(), key=None):
        if key is None:
            cands = [b for b in (list(writes) + list(reads)) if not b.dram] or (list(writes) + list(reads))
            key = id(cands[0])
        key = (q == "pool", key)
        waits = self._deps(q, reads, writes)
        d = self.dma_sems.get(key)
        if d is None:
            fl = self.free_dma[q == "pool"]
            if fl:
                d = fl.pop()
            else:
                d = [self._new_sem("d"), 0]
            self.dma_sems[key] = d
        if d[1] + 16 > SEM_LIMIT:
            d[0] = self._new_sem("d")
            d[1] = 0
        d[1] += 16
        tok = ("dma", d[0], d[1])
        self.ops[q].append((sorted(waits.items()), fn, (d[0], 16)))
        self._mark(tok, reads, writes)
        self.n_ops += 1
        return tok

    def bg_dma(self, fn, buf, reads=()):
        d = self.bg_sems.get(id(buf))
        if d is None:
            d = [self._new_sem("bg"), 0]
            self.bg_sems[id(buf)] = d
        waits = self._deps("pool", reads, [])
        d[1] += 16
        tok = ("dma", d[0], d[1])
        self.ops["pool"].append((sorted(waits.items()), fn, (d[0], 16)))
        buf.w = tok
        buf.r = {}
        self.n_ops += 1
        return tok

    def barrier(self):
        targets = {}
        for e in ENGS:
            p = self.prog[e]
            if p[1] > 0:
                targets[p[0]] = p[1]
        for d in self.dma_sems.values():
            if d[1] > 0:
                targets[d[0]] = d[1]
        for e in ENGS:
            waits = {}
            for sidx, val in targets.items():
                if self.known[e].get(sidx, 0) < val:
                    self.known[e][sidx] = val
                    waits[sidx] = val
            if waits:
                self.ops[e].append((sorted(waits.items()), None, None))

    def emit(self):
        nc = self.nc
        sems = self.sems
        ops = self.ops

        def run(engh, lst):
            for waits, fn, inc in lst:
                for sidx, val in waits:
                    engh.wait_ge(sems[sidx], val)
                if fn is not None:
                    ins = fn(engh)
                    ins.then_inc(sems[inc[0]], inc[1])

        with nc.Block() as block:
            @block.tensor
            def _(e):
                run(e, ops["pe"])

            @block.scalar
            def _(e):
                run(e, ops["act"])

            @block.vector
            def _(e):
                run(e, ops["dve"])

            @block.gpsimd
            def _(e):
                run(e, ops["pool"])

            @block.sync
            def _(e):
                run(e, ops["sp"])
        self.ops = {e: [] for e in ENGS}

    def close(self):
        self.gstack.close()


class Ring:
    def __init__(self, S, name, shape, dtype, srcs, R, q="sp", nparts=1, rbufs=()):
        self.S = S
        self.rbufs = list(rbufs)
        self.tiles = [S.sbuf(f"{name}{i}", shape, dtype) for i in range(R)]
        self.bufs = [[Buf(f"{name}{i}_{j}") for j in range(nparts)] for i in range(R)]
        self.srcs = srcs
        self.R = R
        self.q = q
        self.issued = 0

    def _issue_upto(self, k):
        while self.issued <= k and self.issued < len(self.srcs):
            i = self.issued
            t = self.tiles[i % self.R]
            src = self.srcs[i]
            if not isinstance(src, list):
                src = [(lambda tt: tt[:], src)]
            for j, (sel, sr) in enumerate(src):
                self.S.dma(self.q, (lambda e, t=t, sr=sr, sel=sel: e.dma_start(out=sel(t), in_=sr)),
                           reads=self.rbufs, writes=[self.bufs[i % self.R][j]])
            self.issued += 1

    def get(self, k):
        self._issue_upto(k)
        b = self.bufs[k % self.R]
        return self.tiles[k % self.R], (b[0] if len(b) == 1 else b)

    def prefetch(self, k):
        self._issue_upto(k + self.R - 1)


class K:
    pass


def build(cfg):
    nlat = cfg.get("nlat", NLAT)
    nctx = cfg.get("nctx", NCTX)
    ntok = nlat + nctx
    phases = cfg.get("phases", None)
    debug = cfg.get("debug", set())
    nc = bass.Bass("TRN2", target_bir_lowering=False)
    S = Sched(nc, same_engine_sync=cfg.get("same", True))
    k = K()
    k.nc, k.S, k.nlat, k.nctx, k.ntok = nc, S, nlat, nctx, ntok

    def din(name, shape, dt=F32):
        return nc.dram_tensor(name, list(shape), dt, kind="ExternalInput").ap()

    def dscr(name, shape, dt):
        kind = "ExternalOutput" if name in debug else "Internal"
        return nc.dram_tensor(name, list(shape), dt, kind=kind).ap()

    k.din, k.dscr = din, dscr
    k.dbg = {}
    k.dbg_names = debug
    if "TABS" in debug:
        k.dbg["TABS"] = [nc.dram_tensor(f"TABS{l}", [128, 2, 9, KC], F32, kind="ExternalOutput").ap() for l in range(2)]
    k.x = din("x", [nlat, D])
    k.ctx = din("ctx", [nctx, D])
    k.cvec = din("cvec", [128, KC, 2])
    k.L = []
    for l in range(2):
        Lw = K()
        p = f"l{l}_"
        Lw.ada_w = din(p + "ada_w", [D, 9 * D])
        Lw.ada_b = din(p + "ada_b_t", [128, 144])
        Lw.norms = din(p + "norms_t", [128, 3 * KC])
        Lw.gu = [din(p + "ffn1_w_gu", [D, 2 * DFF]), din(p + "ffn2_w_gu", [D, 2 * DFF])]
        Lw.dn = [din(p + "ffn1_w_down", [DFF, D]), din(p + "ffn2_w_down", [DFF, D])]
        k.L.append(Lw)
    k.norm_out = din("norm_out_t", [128, KC])
    k.w_in = din("l0_w_in", [D, 4128])
    k.gw_aug = din("l0_gw_aug", [33, 2, 512])
    k.gnorm = din("l0_gnorm_t", [128, 8])
    k.w_out0 = din("l0_w_out", [D, D])
    k.rope = din("rope", [4, 128, ntok])
    k.dftc = din("dftc", [3, 256, 256], BF16)
    k.dftT = din("dftT", [2, nlat, nlat], BF16)
    k.w_qkv = din("l1_w_qkv", [D, 3 * D])
    k.w_out1 = din("l1_w_out", [D, D])
    _kh, _nch, _classes, _, _ = na_geometry(nlat // 64)
    k.nabias = din("nabias", [32, len(_classes), 128, _nch * 128])
    k.out = nc.dram_tensor("out", [nlat, D], F32, kind="ExternalOutput").ap()

    k.XT = dscr("XT", [D, ntok], F32)
    k.b_XT = Buf("XT", dram=True)
    k.QT = dscr("QT", [512, ntok], F32); k.KT = dscr("KT", [512, ntok], F32); k.b_QK = Buf("QK", dram=True)
    k.RG = dscr("RG", [1024, ntok], F32); k.b_RG = Buf("RG", dram=True)
    k.FT = dscr("FT", [1024, ntok], BF16); k.b_FT = Buf("FT", dram=True)
    k.V = dscr("V", [ntok, 1024], BF16); k.b_V = Buf("V", dram=True)
    k.GF = dscr("GF", [ntok, 512], F32); k.GB = dscr("GB", [ntok, 512], F32); k.b_G = Buf("G", dram=True)
    k.OF = dscr("OF", [1024, ntok], F32); k.OB = dscr("OB", [1024, ntok], F32); k.b_O = Buf("O", dram=True)
    k.GCS = dscr("GCS", [ntok, 2, 1024], BF16); k.b_GCS = Buf("GCS", dram=True)
    k.FMT = dscr("FMT", [1024, ntok], BF16); k.b_FMT = Buf("FMT", dram=True)
    k.QT1 = dscr("QT1", [D, nlat], BF16); k.KT1 = dscr("KT1", [D, ntok], BF16); k.b_QK1 = Buf("QK1", dram=True)
    k.V1 = dscr("V1", [ntok, D], BF16); k.b_V1 = Buf("V1", dram=True)
    k.O1 = dscr("O1", [nlat, D], BF16); k.b_O1 = Buf("O1", dram=True)

    k.ident = S.sbuf("ident", [128, 128], F32, persist=True)
    k.identb = S.sbuf("identb", [128, 128], BF16, persist=True)
    k.onesb = S.sbuf("onesb", [128, 128], BF16, persist=True)
    k.b_const = Buf("const")
    k.TAB = [S.sbuf(f"TAB{l}", [128, 2, 9, KC], F32, persist=True) for l in range(2)]
    k.b_TAB = [Buf("TAB0"), Buf("TAB1")]
    k.gout = S.sbuf("gout", [128, KC], F32, persist=True)

    S.begin_phase()
    S.op("pool", lambda e: e.memset(k.ident[:], 0.0), writes=[k.b_const])
    S.op("pool", lambda e: e.affine_select(out=k.ident[:], in_=k.ident[:], pattern=[[-1, 128]],
                                           compare_op=ALU.not_equal, fill=1.0, base=0, channel_multiplier=1),
         reads=[k.b_const], writes=[k.b_const])
    S.op("dve", lambda e: e.tensor_copy(out=k.identb[:], in_=k.ident[:]), reads=[k.b_const], writes=[k.b_const])
    S.op("dve", lambda e: e.memset(k.onesb[:], 1.0), writes=[k.b_const])
    S.dma("sp", lambda e: e.dma_start(out=k.gout[:], in_=k.norm_out), writes=[k.b_const])
    S.end_phase()

    def want(ph):
        return phases is None or ph in phases

    phase_precast(k)

    def flush_bg(upfront_only=False):
        if k.pc_tasks:
            S.begin_phase()
            run_bg(k, k.n_upfront if upfront_only else len(k.pc_tasks))
            S.end_phase()

    if want("ada"):
        phase_ada(k)
    else:
        flush_bg()
    tiles = []
    t = 0
    while t < nlat:
        tiles.append((t, min(512, nlat - t), 0))
        t += 512
    tc = nlat
    while tc < ntok:
        tiles.append((tc, min(512, ntok - tc), 1))
        tc += 512
    k.tiles = tiles
    if want("l0ffn1"):
        phase_ffn(k, 0, 0, tiles, first=True, last=False)
    flush_bg()
    if want("l0proj"):
        phase_l0proj(k, tiles)
    if want("gla"):
        phase_gla(k)
    if want("fnet"):
        phase_fnet(k)
    if want("l0out"):
        phase_l0out(k, tiles)
    if want("l0ffn2"):
        phase_ffn(k, 0, 1, tiles)
    lat_tiles = [t_ for t_ in tiles if t_[2] == 0]
    if want("l1ffn1"):
        phase_ffn(k, 1, 0, tiles)
    if want("l1proj"):
        phase_l1proj(k, tiles)
    if want("na"):
        phase_na(k)
    if want("l1out"):
        phase_l1out(k, lat_tiles)
    if want("l1ffn2"):
        phase_ffn(k, 1, 1, lat_tiles, last=True)
    S.close()
    return nc


def precast_blocks(k, name, W, Krows, col_starts, CB):
    S = k.S
    kc = Krows // 128
    WB = k.dscr(name, [len(col_starts), 128, kc, CB], BF16)
    b = Buf(name, dram=True)
    for nb, c0 in enumerate(col_starts):
        src = W[:, c0:c0 + CB].rearrange("(kc p) c -> p kc c", p=128)
        dst = WB[nb]
        k.pc_tasks.append(((lambda e, dst=dst, src=src: e.dma_start(out=dst, in_=src)), b))
    return WB, b


def phase_precast(k):
    k.pc_tasks = []

    def ffn(l, w):
        Lw = k.L[l]
        cs = []
        for j in range(JC):
            cs += [128 * j, DFF + 128 * j]
        Lw.guB[w], Lw.b_guB[w] = precast_blocks(k, f"guB{l}{w}", Lw.gu[w], D, cs, 128)
        Lw.dnB[w], Lw.b_dnB[w] = precast_blocks(k, f"dnB{l}{w}", Lw.dn[w], DFF, [128 * o for o in range(KC)], 128)

    for l in range(2):
        k.L[l].guB, k.L[l].dnB, k.L[l].b_guB, k.L[l].b_dnB = [None, None], [None, None], [None, None], [None, None]
    ffn(0, 0)
    precast_l0(k)
    k.n_upfront = len(k.pc_tasks)
    ffn(0, 1)
    ffn(1, 0)
    precast_l1(k)
    ffn(1, 1)


def run_bg(k, n):
    S = k.S
    for _ in range(n):
        if not k.pc_tasks:
            return
        fn, b = k.pc_tasks.pop(0)
        S.bg_dma(fn, b)


def phase_ada(k):
    S = k.S
    S.begin_phase()
    run_bg(k, k.n_upfront)
    s_t = S.sbuf("ada_s", [128, KC, 2], F32)
    b_s = Buf("ada_s")
    S.dma("sp", lambda e: e.dma_start(out=s_t[:], in_=k.cvec), writes=[b_s])
    S.op("act", lambda e: e.activation(out=s_t[:], in_=s_t[:], func=AF.Silu), reads=[b_s], writes=[b_s])
    M = S.sbuf("ada_M", [128, 144, 2], F32)
    b_M = Buf("M")
    adab = S.sbuf("ada_b", [128, 144], F32)
    b_adab = Buf("adab")
    gains = S.sbuf("ada_g", [128, 3 * KC], F32)
    b_g = Buf("gains")
    NW = 3
    wt = [S.sbuf(f"ada_w{i}", [128, KC, 512], F32) for i in range(NW)]
    b_wt = [Buf() for _ in range(NW)]
    psr = [S.psum(f"ada_psr{i}", [2, 512]) for i in range(2)]
    b_psr = [Buf(), Buf()]
    Rrow = S.sbuf("ada_R", [2, 9 * D], F32)
    b_R = Buf("R")
    ps = [S.psum(f"ada_ps{i}", [128, 4, 2]) for i in range(2)]
    b_ps = [Buf("aps0"), Buf("aps1")]
    ev = 0
    for l in range(2):
        Lw = k.L[l]
        S.dma("sp", lambda e, Lw=Lw: e.dma_start(out=adab[:], in_=Lw.ada_b), writes=[b_adab])
        S.dma("sp", lambda e, Lw=Lw: e.dma_start(out=gains[:], in_=Lw.norms), writes=[b_g])
        for cb in range(36):
            w_t, bw = wt[ev % NW], b_wt[ev % NW]
            src = Lw.ada_w[:, cb * 512:(cb + 1) * 512].rearrange("(kc p) c -> p kc c", p=128)
            S.dma("sp", (lambda e, w_t=w_t, src=src: e.dma_start(out=w_t[:], in_=src)), writes=[bw])
            p_t, bp = psr[ev % 2], b_psr[ev % 2]
            for kc in range(KC):
                S.op("pe", (lambda e, p_t=p_t, w_t=w_t, kc=kc: e.matmul(
                    p_t[:, :], lhsT=s_t[:, kc, :], rhs=w_t[:, kc, :], start=(kc == 0), stop=(kc == KC - 1))),
                    reads=[bw, b_s], writes=[bp])
            S.op("act", (lambda e, p_t=p_t, cb=cb: e.activation(out=Rrow[:, cb * 512:(cb + 1) * 512], in_=p_t[:, :], func=AF.Copy)),
                 reads=[bp], writes=[b_R])
            ev += 1
        for cq in range(36):
            p_t, bp = ps[cq % 2], b_ps[cq % 2]
            for c4 in range(4):
                cc = cq * 4 + c4
                S.op("pe", (lambda e, p_t=p_t, c4=c4, cc=cc: e.transpose(p_t[:, c4, :], Rrow[0:2, cc * 128:(cc + 1) * 128], k.ident[0:2, 0:2])),
                     reads=[b_R, k.b_const], writes=[bp])
            for c4 in range(4):
                cc = cq * 4 + c4
                S.op("dve", (lambda e, p_t=p_t, c4=c4, cc=cc: e.tensor_scalar(
                    out=M[:, cc, :], in0=p_t[:, c4, :], scalar1=adab[:, cc:cc + 1], scalar2=None, op0=ALU.add)),
                    reads=[bp, b_adab], writes=[b_M])
        TAB = k.TAB[l]
        for cls in range(2):
            for j in range(3):
                sh = M[:, (3 * j) * KC:(3 * j + 1) * KC, cls]
                sc = M[:, (3 * j + 1) * KC:(3 * j + 2) * KC, cls]
                gt = M[:, (3 * j + 2) * KC:(3 * j + 3) * KC, cls]
                S.op("dve", (lambda e, sc=sc, j=j, cls=cls, TAB=TAB: e.scalar_tensor_tensor(
                    out=TAB[:, cls, 3 * j, :], in0=sc, scalar=1.0, in1=gains[:, j * KC:(j + 1) * KC],
                    op0=ALU.add, op1=ALU.mult)), reads=[b_M, b_g], writes=[k.b_TAB[l]])
                S.op("dve", (lambda e, sh=sh, j=j, cls=cls, TAB=TAB: e.tensor_copy(
                    out=TAB[:, cls, 3 * j + 1, :], in_=sh)), reads=[b_M], writes=[k.b_TAB[l]])
                S.op("dve", (lambda e, gt=gt, j=j, cls=cls, TAB=TAB: e.tensor_scalar(
                    out=TAB[:, cls, 3 * j + 2, :], in0=gt, scalar1=(1.0 if j == 1 else 0.5), scalar2=None,
                    op0=ALU.mult)), reads=[b_M], writes=[k.b_TAB[l]])
    if "TABS" in k.dbg_names:
        for l in range(2):
            S.dma("sp", lambda e, l=l: e.dma_start(out=k.dbg["TABS"][l], in_=k.TAB[l][:]), reads=[k.b_TAB[l]])
    S.end_phase()


class NormCtx:
    def __init__(self, k, pfx="n"):
        S = k.S
        self.k = k
        self.sq = [S.sbuf(f"{pfx}sq{i}", [128, 512], BF16) for i in range(2)]
        self.b_sq = [Buf(), Buf()]
        self.tmp = [S.sbuf(f"{pfx}tmp{i}", [128, 512], F32) for i in range(2)]
        self.b_tmp = [Buf(), Buf()]
        self.rs = S.sbuf(pfx + "rs", [128, 512], F32)
        self.b_rs = Buf()
        self.rstd = S.sbuf(pfx + "rstd", [128, 512], F32)
        self.b_rstd = Buf()
        self.ps_ss = S.psum(pfx + "ps_ss", [128, 512])
        self.b_ss = Buf()

    def stats(self, x_t, bx, nt, nchunks=KC, dim=D):
        k, S = self.k, self.k.S
        for kc in range(nchunks):
            s_t, bs = self.sq[kc % 2], self.b_sq[kc % 2]
            S.op("act", (lambda e, s_t=s_t, kc=kc: e.activation(out=s_t[:, 0:nt], in_=x_t[:, kc, 0:nt], func=AF.Square)),
                 reads=[bx], writes=[bs])
            S.op("pe", (lambda e, s_t=s_t, kc=kc: e.matmul(self.ps_ss[:, 0:nt], lhsT=k.onesb[:], rhs=s_t[:, 0:nt],
                                                           start=(kc == 0), stop=(kc == nchunks - 1))),
                 reads=[bs, k.b_const], writes=[self.b_ss])
        S.op("act", (lambda e: e.activation(out=self.rs[:, 0:nt], in_=self.ps_ss[:, 0:nt], func=AF.Sqrt, bias=EPS, scale=1.0 / dim)),
             reads=[self.b_ss], writes=[self.b_rs])
        S.op("dve", (lambda e: e.reciprocal(out=self.rstd[:, 0:nt], in_=self.rs[:, 0:nt])), reads=[self.b_rs], writes=[self.b_rstd])

    def apply(self, x_t, bx, nt, A, Bv, btab, out_t, b_out):
        S = self.k.S
        for kc in range(KC):
            t_t, bt = self.tmp[kc % 2], self.b_tmp[kc % 2]
            S.op("dve", (lambda e, t_t=t_t, kc=kc: e.scalar_tensor_tensor(
                out=t_t[:, 0:nt], in0=x_t[:, kc, 0:nt], scalar=A[:, kc:kc + 1], in1=self.rstd[:, 0:nt],
                op0=ALU.mult, op1=ALU.mult)), reads=[bx, self.b_rstd, btab], writes=[bt])
            S.op("act", (lambda e, t_t=t_t, kc=kc: e.activation(
                out=out_t[:, kc, 0:nt], in_=t_t[:, 0:nt], func=AF.Identity, bias=Bv[:, kc:kc + 1], scale=1.0)),
                reads=[bt, btab], writes=[b_out])


def phase_ffn(k, l, w, tiles, first=False, last=False):
    S = k.S
    Lw = k.L[l]
    j3 = 0 if w == 0 else 2
    TAB = k.TAB[l]
    bTAB = k.b_TAB[l]
    S.begin_phase()
    XTv = k.XT.rearrange("(kc p) t -> p kc t", p=128)
    xt = [S.sbuf(f"xt{i}", [128, KC, 512], F32) for i in range(2)]
    b_xt = [Buf("xt0"), Buf("xt1")]
    ht = S.sbuf("ht", [128, KC, 512], BF16)
    b_ht = Buf("ht")
    gt = S.sbuf("gt", [128, JC, 512], BF16)
    b_gt = [Buf(f"gt{j}") for j in range(JC)]
    nrm = NormCtx(k)
    sa = [S.sbuf(f"sa{i}", [128, 512], F32) for i in range(2)]
    b_sa = [Buf("sa0"), Buf("sa1")]
    ps_a = [S.psum(f"ps_a{i}", [128, 512]) for i in range(2)]
    b_a = [Buf("psa0"), Buf("psa1")]
    ps_u = [S.psum(f"ps_u{i}", [128, 512]) for i in range(2)]
    b_u = [Buf("psu0"), Buf("psu1")]
    ps_y = [S.psum(f"ps_y{i}", [128, 512]) for i in range(2)]
    b_y = [Buf("psy0"), Buf("psy1")]
    if first or last:
        ps_tr = S.psum("ps_tr", [128, 512])
        b_tr = Buf("ps_tr")
        xtok = S.sbuf("xtok", [128, D], F32)
        b_xtok = Buf("xtok")
    nT = len(tiles)
    gu_srcs = []
    dn_srcs = []
    for ti in range(nT):
        for j in range(JC):
            gu_srcs.append(Lw.guB[w][2 * j:2 * j + 2].rearrange("b p kc c -> p b kc c"))
        for oc in range(KC):
            for jh in range(2):
                dn_srcs.append(Lw.dnB[w][oc][:, jh * (JC // 2):(jh + 1) * (JC // 2), :])
    ring_gu = Ring(S, "wgu", [128, 2, KC, 128], BF16, gu_srcs, 4, q="sp", rbufs=[Lw.b_guB[w]])
    ring_dn = Ring(S, "wdn", [128, JC // 2, 128], BF16, dn_srcs, 4, q="sp", rbufs=[Lw.b_dnB[w]])

    def load_x(ti):
        t0, nt, cls = tiles[ti]
        x_t, bx = xt[ti % 2], b_xt[ti % 2]
        if not first:
            S.dma("pool", (lambda e: e.dma_start(out=x_t[:, :, 0:nt], in_=XTv[:, :, t0:t0 + nt])),
                  reads=[k.b_XT], writes=[bx])
            return
        for s in range(nt // 128):
            tt = t0 + s * 128
            src = k.x[tt:tt + 128, :] if cls == 0 else k.ctx[tt - k.nlat:tt - k.nlat + 128, :]
            S.dma("pool", (lambda e, src=src: e.dma_start(out=xtok[:], in_=src)), writes=[b_xtok])
            for kq in range(KC // 4):
                for k4 in range(4):
                    kc = kq * 4 + k4
                    S.op("pe", (lambda e, kc=kc, k4=k4: e.transpose(ps_tr[:, k4 * 128:(k4 + 1) * 128],
                                                                   xtok[:, kc * 128:(kc + 1) * 128], k.ident[:])),
                         reads=[b_xtok, k.b_const], writes=[b_tr])
                S.op("dve", (lambda e, kq=kq, s=s: e.tensor_copy(
                    out=x_t[:, kq * 4:(kq + 1) * 4, s * 128:(s + 1) * 128],
                    in_=ps_tr[:, :].rearrange("p (a b) -> p a b", a=4))), reads=[b_tr], writes=[bx])

    def norm(ti):
        t0, nt, cls = tiles[ti]
        x_t, bx = xt[ti % 2], b_xt[ti % 2]
        nrm.stats(x_t, bx, nt)
        nrm.apply(x_t, bx, nt, TAB[:, cls, 3 * j3, :], TAB[:, cls, 3 * j3 + 1, :], bTAB, ht, b_ht)

    def gateup(ti):
        t0, nt, cls = tiles[ti]
        for j in range(JC):
            evn = ti * JC + j
            ring_gu.prefetch(evn)
            w_t, bw = ring_gu.get(evn)
            pa, ba = ps_a[j % 2], b_a[j % 2]
            pu, bu = ps_u[j % 2], b_u[j % 2]
            for kc in range(KC):
                S.op("pe", (lambda e, pa=pa, w_t=w_t, kc=kc: e.matmul(
                    pa[:, 0:nt], lhsT=w_t[:, 0, kc, :], rhs=ht[:, kc, 0:nt],
                    start=(kc == 0), stop=(kc == KC - 1))), reads=[bw, b_ht], writes=[ba])
            for kc in range(KC):
                S.op("pe", (lambda e, pu=pu, w_t=w_t, kc=kc: e.matmul(
                    pu[:, 0:nt], lhsT=w_t[:, 1, kc, :], rhs=ht[:, kc, 0:nt],
                    start=(kc == 0), stop=(kc == KC - 1))), reads=[bw, b_ht], writes=[bu])
            s_a, bsa = sa[j % 2], b_sa[j % 2]
            S.op("act", (lambda e, s_a=s_a, pa=pa: e.activation(out=s_a[:, 0:nt], in_=pa[:, 0:nt], func=AF.Silu)),
                 reads=[ba], writes=[bsa])
            S.op("dve", (lambda e, s_a=s_a, pu=pu, j=j: e.tensor_tensor(
                out=gt[:, j, 0:nt], in0=s_a[:, 0:nt], in1=pu[:, 0:nt], op=ALU.mult)),
                reads=[bsa, bu], writes=[b_gt[j]])

    def down(ti):
        t0, nt, cls = tiles[ti]
        x_t, bx = xt[ti % 2], b_xt[ti % 2]
        JH = JC // 2
        for oc in range(KC):
            py, by = ps_y[oc % 2], b_y[oc % 2]
            for jh in range(2):
                evn = (ti * KC + oc) * 2 + jh
                ring_dn.prefetch(evn)
                w_t, bw = ring_dn.get(evn)
                for jj in range(JH):
                    jc = jh * JH + jj
                    S.op("pe", (lambda e, py=py, w_t=w_t, jj=jj, jc=jc: e.matmul(
                        py[:, 0:nt], lhsT=w_t[:, jj, :], rhs=gt[:, jc, 0:nt], start=(jc == 0), stop=(jc == JC - 1))),
                        reads=[bw, b_gt[jc]], writes=[by])
            S.op("dve", (lambda e, py=py, oc=oc: e.scalar_tensor_tensor(
                out=x_t[:, oc, 0:nt], in0=py[:, 0:nt], scalar=TAB[:, cls, 3 * j3 + 2, oc:oc + 1], in1=x_t[:, oc, 0:nt],
                op0=ALU.mult, op1=ALU.add)), reads=[by, bx, bTAB], writes=[bx])

    def store(ti):
        t0, nt, cls = tiles[ti]
        x_t, bx = xt[ti % 2], b_xt[ti % 2]
        if not last:
            S.dma("pool", (lambda e: e.dma_start(out=XTv[:, :, t0:t0 + nt], in_=x_t[:, :, 0:nt])),
                  reads=[bx], writes=[k.b_XT])
            return
        nrm.stats(x_t, bx, nt)
        for kc in range(KC):
            S.op("dve", (lambda e, kc=kc: e.scalar_tensor_tensor(
                out=x_t[:, kc, 0:nt], in0=x_t[:, kc, 0:nt], scalar=k.gout[:, kc:kc + 1], in1=nrm.rstd[:, 0:nt],
                op0=ALU.mult, op1=ALU.mult)), reads=[bx, nrm.b_rstd, k.b_const], writes=[bx])
        for s in range(nt // 128):
            for kq in range(KC // 4):
                for k4 in range(4):
                    kc = kq * 4 + k4
                    S.op("pe", (lambda e, kc=kc, k4=k4, s=s: e.transpose(
                        ps_tr[:, k4 * 128:(k4 + 1) * 128], x_t[:, kc, s * 128:(s + 1) * 128], k.ident[:])),
                        reads=[bx, k.b_const], writes=[b_tr])
                S.op("act", (lambda e, kq=kq: e.activation(out=xtok[:, kq * 512:(kq + 1) * 512], in_=ps_tr[:, :], func=AF.Copy)),
                     reads=[b_tr], writes=[b_xtok])
            tt = t0 + s * 128
            S.dma("pool", (lambda e, tt=tt: e.dma_start(out=k.out[tt:tt + 128, :], in_=xtok[:])), reads=[b_xtok], key="outst")

    load_x(0)
    norm(0)
    for ti in range(nT):
        gateup(ti)
        if ti + 1 < nT:
            load_x(ti + 1)
        if k.pc_tasks:
            left = nT - ti
            run_bg(k, (len(k.pc_tasks) + left - 1) // left)
        if ti + 1 < nT:
            norm(ti + 1)
        down(ti)
        store(ti)
    S.end_phase()


W_Q0, W_K0, W_V0, W_LR0, W_R0, W_F0 = 0, 512, 1024, 2048, 2080, 3104


def precast_l0(k):
    S = k.S
    Lw = k.L[0]
    win = k.w_in
    nblk = 32
    WB = k.dscr("winFM", [nblk, 128, KC, 128], BF16)
    b = Buf("winFM", dram=True)
    perm = [1, 0, 3, 2]
    order = []
    for h in range(4):
        order += [("n", W_Q0 + 128 * h), ("s", W_Q0 + 128 * h), ("n", W_K0 + 128 * h), ("s", W_K0 + 128 * h)]
    for c in range(8):
        order.append(("n", W_R0 + 128 * c))
    for c in range(8):
        order.append(("n", W_F0 + 128 * c))
    for nb, (kind, c0) in enumerate(order):
        if kind == "n":
            src = win[:, c0:c0 + 128].rearrange("(kc p) c -> p kc c", p=128)
            k.pc_tasks.append(((lambda e, dst=WB[nb], src=src: e.dma_start(out=dst, in_=src)), b))
        else:
            for i in range(4):
                src = win[:, c0 + 32 * perm[i]:c0 + 32 * perm[i] + 32].rearrange("(kc p) c -> p kc c", p=128)
                dst = WB[nb][:, :, 32 * i:32 * i + 32]
                k.pc_tasks.append(((lambda e, dst=dst, src=src: e.dma_start(out=dst, in_=src)), b))
    k.winFM = WB
    k.b_winFM = b
    k.winLR, k.b_winLR = precast_blocks(k, "winLR", win, D, [W_LR0], 32)
    k.winV, k.b_winV = precast_blocks(k, "winV", win, D, [W_V0, W_V0 + 512], 512)
    k.wout0B, k.b_wout0B = precast_blocks(k, "wout0B", k.w_out0, D, [128 * o for o in range(KC)], 128)


def phase_l0proj(k, tiles):
    S = k.S
    TAB, bTAB = k.TAB[0], k.b_TAB[0]
    ntok = k.ntok
    S.begin_phase()
    XTv = k.XT.rearrange("(kc p) t -> p kc t", p=128)
    xt = [S.sbuf("xt0", [128, KC, 512], F32)] * 2
    b_xt = [Buf()] * 2
    uts = [S.sbuf(f"ut{i}", [128, KC, 512], BF16) for i in range(2)]
    b_uts = [Buf("ut0"), Buf("ut1")]
    nrm = NormCtx(k)
    rope = [S.sbuf(f"rope{i}", [128, 4, 512], F32) for i in range(2)]
    b_rope = [Buf(), Buf()]
    t1 = [S.sbuf(f"t1_{i}", [128, 512], F32) for i in range(2)]
    b_t1 = [Buf(), Buf()]
    t2 = [S.sbuf(f"t2_{i}", [128, 512], F32) for i in range(2)]
    b_t2 = [Buf(), Buf()]
    qr = [S.sbuf(f"qr{i}", [128, 512], F32) for i in range(2)]
    b_qr = [Buf(), Buf()]
    rg = [S.sbuf(f"rg{i}", [128, 512], F32) for i in range(2)]
    b_rg = [Buf(), Buf()]
    fb = [S.sbuf(f"fb{i}", [128, 512], BF16) for i in range(2)]
    b_fb = [Buf(), Buf()]
    lra = S.sbuf("lra", [33, 512], F32)
    b_lra = Buf("lra")
    gw = S.sbuf("gw", [33, 2, 512], F32)
    b_gw = Buf("gw")
    gn = S.sbuf("gn", [128, 8], F32)
    b_gn = Buf("gn")
    ge = [S.sbuf(f"ge{i}", [128, 512], F32) for i in range(2)]
    b_ge = [Buf(), Buf()]
    gs = [S.sbuf(f"gs{i}", [128, 512], F32) for i in range(2)]
    b_gs = [Buf(), Buf()]
    vt = [S.sbuf(f"vt{i}", [128, 1024], BF16) for i in range(2)]
    b_vt = [Buf(), Buf()]
    ps_fm = [S.psum(f"ps_fm{i}", [128, 512]) for i in range(4)]
    b_fm = [Buf() for _ in range(4)]
    ps_z = S.psum("ps_z", [128, 512])
    b_z = Buf()
    ps_v = [S.psum(f"ps_v{i}", [128, 512]) for i in range(2)]
    b_v = [Buf(), Buf()]

    S.op("dve", lambda e: e.memset(lra[:], 1.0), writes=[b_lra])
    S.dma("sp", lambda e: e.dma_start(out=gw[:], in_=k.gw_aug), writes=[b_gw])
    S.dma("sp", lambda e: e.dma_start(out=gn[:], in_=k.gnorm), writes=[b_gn])

    nT = len(tiles)
    fm_srcs, lr_srcs, v_srcs = [], [], []
    for ti in range(nT):
        lr_srcs.append(k.winLR[0])
        for nb in range(32):
            fm_srcs.append(k.winFM[nb])
        nsub = tiles[ti][1] // 128
        for s_ in range(nsub):
            v_srcs += [k.winV[0], k.winV[1]]
    ring_fm = Ring(S, "wfm", [128, KC, 128], BF16, fm_srcs, 8, q="sp", rbufs=[k.b_winFM])
    ring_lr = Ring(S, "wlr", [128, KC, 32], BF16, lr_srcs, 2, q="sp", rbufs=[k.b_winLR])
    wv_res = S.sbuf("wvres", [128, 2, KC, 512], BF16)
    b_wv = [Buf(), Buf()]
    for hf in range(2):
        S.dma("sp", (lambda e, hf=hf: e.dma_start(out=wv_res[:, hf, :, :], in_=k.winV[hf])), reads=[k.b_winV], writes=[b_wv[hf]])
    fm_ev = [0]
    v_ev = [0]
    pcount = [0]

    def load_x(ti):
        t0, nt, cls = tiles[ti]
        x_t, bx = xt[ti % 2], b_xt[ti % 2]
        S.dma("pool", (lambda e: e.dma_start(out=x_t[:, :, 0:nt], in_=XTv[:, :, t0:t0 + nt])), reads=[k.b_XT], writes=[bx])
        r_t, br = rope[ti % 2], b_rope[ti % 2]
        S.dma("pool", (lambda e: e.dma_start(out=r_t[:, :, 0:nt], in_=k.rope.rearrange("a p t -> p a t")[:, :, t0:t0 + nt])), writes=[br])

    def norm(ti):
        t0, nt, cls = tiles[ti]
        x_t, bx = xt[ti % 2], b_xt[ti % 2]
        nrm.stats(x_t, bx, nt)
        nrm.apply(x_t, bx, nt, TAB[:, cls, 3, :], TAB[:, cls, 4, :], bTAB, uts[ti % 2], b_uts[ti % 2])

    def fm_chunk(nt, M=128, ring=None, evn=None, ut=None, b_ut=None):
        ring.prefetch(evn)
        w_t, bw = ring.get(evn)
        i = pcount[0] % 4
        pcount[0] += 1
        p_t, bp = ps_fm[i], b_fm[i]
        for kc in range(KC):
            S.op("pe", (lambda e, p_t=p_t, w_t=w_t, kc=kc, ut=ut: e.matmul(
                p_t[0:M, 0:nt], lhsT=w_t[:, kc, :], rhs=ut[:, kc, 0:nt], start=(kc == 0), stop=(kc == KC - 1))),
                reads=[bw, b_ut], writes=[bp])
        return p_t, bp

    def proj(ti, hook=None):
        t0, nt, cls = tiles[ti]
        r_t, br = rope[ti % 2], b_rope[ti % 2]
        ut, b_ut = uts[ti % 2], b_uts[ti % 2]
        p_t, bp = fm_chunk(nt, M=32, ring=ring_lr, evn=ti, ut=ut, b_ut=b_ut)
        S.op("act", (lambda e, p_t=p_t: e.activation(out=lra[0:32, 0:nt], in_=p_t[0:32, 0:nt], func=AF.Copy)), reads=[bp], writes=[b_lra])
        for s_ in range(nt // 128):
            tok0 = t0 + s_ * 128
            for d in range(2):
                S.op("pe", (lambda e, s_=s_, d=d: e.matmul(ps_z[:, :], lhsT=lra[0:33, s_ * 128:(s_ + 1) * 128], rhs=gw[0:33, d, :],
                                                          start=True, stop=True)), reads=[b_lra, b_gw], writes=[b_z])
                i = (s_ * 2 + d) % 2
                S.op("act", (lambda e, i=i: e.activation(out=ge[i][:], in_=ps_z[:, :], func=AF.Exp, scale=-1.0)), reads=[b_z], writes=[b_ge[i]])
                S.op("act", (lambda e, i=i: e.activation(out=gs[i][:], in_=ge[i][:], func=AF.Ln, bias=1.0, scale=1.0)), reads=[b_ge[i]], writes=[b_gs[i]])
                S.op("dve", (lambda e, i=i: e.tensor_scalar(out=gs[i][:], in0=gs[i][:], scalar1=-1.0 / 16.0, scalar2=None, op0=ALU.mult)),
                     reads=[b_gs[i]], writes=[b_gs[i]])
                dst = (k.GF if d == 0 else k.GB)[tok0:tok0 + 128, :]
                S.dma("pool", (lambda e, i=i, dst=dst: e.dma_start(out=dst, in_=gs[i][:])), reads=[b_gs[i]], writes=[k.b_G])
        QTv = k.QT.rearrange("(h p) t -> p h t", p=128)
        KTv = k.KT.rearrange("(h p) t -> p h t", p=128)
        for h in range(4):
            for which in range(2):
                pn, bn = fm_chunk(nt, ring=ring_fm, evn=fm_ev[0], ut=ut, b_ut=b_ut); fm_ev[0] += 1
                psw, bsw = fm_chunk(nt, ring=ring_fm, evn=fm_ev[0], ut=ut, b_ut=b_ut); fm_ev[0] += 1
                i = (h * 2 + which) % 2
                S.op("dve", (lambda e, i=i, pn=pn, which=which: e.tensor_tensor(out=t1[i][:, 0:nt], in0=pn[:, 0:nt], in1=r_t[:, 2 * which, 0:nt], op=ALU.mult)),
                     reads=[bn, br], writes=[b_t1[i]])
                S.op("dve", (lambda e, i=i, psw=psw, which=which: e.tensor_tensor(out=t2[i][:, 0:nt], in0=psw[:, 0:nt], in1=r_t[:, 2 * which + 1, 0:nt], op=ALU.mult)),
                     reads=[bsw, br], writes=[b_t2[i]])
                S.op("pool", (lambda e, i=i: e.tensor_tensor(out=qr[i][:, 0:nt], in0=t1[i][:, 0:nt], in1=t2[i][:, 0:nt], op=ALU.add)),
                     reads=[b_t1[i], b_t2[i]], writes=[b_qr[i]])
                dst = (QTv if which == 0 else KTv)[:, h, t0:t0 + nt]
                S.dma("pool", (lambda e, i=i, dst=dst: e.dma_start(out=dst, in_=qr[i][:, 0:nt])), reads=[b_qr[i]], writes=[k.b_QK])
        RGv = k.RG.rearrange("(c p) t -> p c t", p=128)
        for c in range(8):
            p_t, bp = fm_chunk(nt, ring=ring_fm, evn=fm_ev[0], ut=ut, b_ut=b_ut); fm_ev[0] += 1
            i = c % 2
            S.op("act", (lambda e, i=i, p_t=p_t: e.activation(out=rg[i][:, 0:nt], in_=p_t[:, 0:nt], func=AF.Silu)), reads=[bp], writes=[b_rg[i]])
            S.op("dve", (lambda e, i=i, c=c: e.tensor_scalar(out=rg[i][:, 0:nt], in0=rg[i][:, 0:nt], scalar1=gn[:, c:c + 1], scalar2=None, op0=ALU.mult)),
                 reads=[b_rg[i], b_gn], writes=[b_rg[i]])
            S.dma("pool", (lambda e, i=i, c=c: e.dma_start(out=RGv[:, c, t0:t0 + nt], in_=rg[i][:, 0:nt])), reads=[b_rg[i]], writes=[k.b_RG])
        if hook is not None:
            hook()
        FTv = k.FT.rearrange("(c p) t -> p c t", p=128)
        for c in range(8):
            p_t, bp = fm_chunk(nt, ring=ring_fm, evn=fm_ev[0], ut=ut, b_ut=b_ut); fm_ev[0] += 1
            i = c % 2
            S.op("act", (lambda e, i=i, p_t=p_t: e.activation(out=fb[i][:, 0:nt], in_=p_t[:, 0:nt], func=AF.Copy)), reads=[bp], writes=[b_fb[i]])
            S.dma("pool", (lambda e, i=i, c=c: e.dma_start(out=FTv[:, c, t0:t0 + nt], in_=fb[i][:, 0:nt])), reads=[b_fb[i]], writes=[k.b_FT])
        for s_ in range(nt // 128):
            tok0 = t0 + s_ * 128
            v_t, bv = vt[s_ % 2], b_vt[s_ % 2]
            for half in range(2):
                w_t, bw = wv_res[:, half, :, :], b_wv[half]
                pv, bpv = ps_v[half], b_v[half]
                for kc in range(KC):
                    S.op("pe", (lambda e, pv=pv, w_t=w_t, kc=kc, s_=s_, ut=ut: e.matmul(
                        pv[:, :], lhsT=ut[:, kc, s_ * 128:(s_ + 1) * 128], rhs=w_t[:, kc, :], start=(kc == 0), stop=(kc == KC - 1))),
                        reads=[bw, b_ut], writes=[bpv])
                S.op("act", (lambda e, pv=pv, v_t=v_t, half=half: e.activation(out=v_t[:, half * 512:(half + 1) * 512], in_=pv[:, :], func=AF.Copy)),
                     reads=[bpv], writes=[bv])
            S.dma("pool", (lambda e, v_t=v_t, tok0=tok0: e.dma_start(out=k.V[tok0:tok0 + 128, :], in_=v_t[:])), reads=[bv], writes=[k.b_V])

    load_x(0)
    norm(0)
    if nT > 1:
        load_x(1)
    for ti in range(nT):
        def hook(ti=ti):
            if ti + 1 < nT:
                norm(ti + 1)
            if ti + 2 < nT:
                load_x(ti + 2)
        proj(ti, hook)
    S.end_phase()


def phase_gla(k):
    S = k.S
    nlat, nctx = k.nlat, k.nctx
    S.begin_phase()
    nl, ncx = nlat // 128, nctx // 128
    order = [[], []]
    order[0] = [nlat + 128 * c for c in range(ncx)] + [128 * i for i in range(nl)]
    order[1] = [nlat + 128 * c for c in reversed(range(ncx))] + [128 * i for i in reversed(range(nl))]
    nsteps = nl + ncx
    msk = S.sbuf("gmask", [128, 2, 128], F32)
    b_msk = Buf()
    S.op("pool", lambda e: e.memset(msk[:], 1.0), writes=[b_msk])
    S.op("pool", lambda e: e.affine_select(out=msk[:, 0, :], in_=msk[:, 0, :], pattern=[[1, 128]], compare_op=ALU.is_ge,
                                           fill=0.0, base=0, channel_multiplier=-1), reads=[b_msk], writes=[b_msk])
    S.op("pool", lambda e: e.affine_select(out=msk[:, 1, :], in_=msk[:, 1, :], pattern=[[-1, 128]], compare_op=ALU.is_ge,
                                           fill=0.0, base=0, channel_multiplier=1), reads=[b_msk], writes=[b_msk])
    St = S.sbuf("gS", [128, 8, 256], F32)
    Sb = S.sbuf("gSb", [128, 8, 256], BF16)
    b_S = [Buf() for _ in range(8)]
    b_Sb = [Buf() for _ in range(8)]
    S.op("dve", lambda e: e.memset(St[:], 0.0), writes=b_S)
    S.op("dve", lambda e: e.memset(Sb[:], 0.0), writes=b_Sb)
    gld = [[S.sbuf(f"gg{d}{i}", [128, 512], F32) for i in range(2)] for d in range(2)]
    qld = [[S.sbuf(f"gq{d}{i}", [128, 4, 128], F32) for i in range(2)] for d in range(2)]
    kld = [[S.sbuf(f"gk{d}{i}", [128, 4, 128], F32) for i in range(2)] for d in range(2)]
    vld = [[S.sbuf(f"gv{d}{i}", [128, 1024], BF16) for i in range(2)] for d in range(2)]
    b_gld = [[Buf(), Buf()] for _ in range(2)]
    b_qld = [[Buf(), Buf()] for _ in range(2)]
    b_kld = [[Buf(), Buf()] for _ in range(2)]
    b_vld = [[Buf(), Buf()] for _ in range(2)]
    eq = [S.sbuf(f"eq{c}", [128, 128], F32) for c in range(8)]
    ek = [S.sbuf(f"ek{c}", [128, 128], F32) for c in range(8)]
    qtl = [S.sbuf(f"qtl{c}", [128, 128], BF16) for c in range(8)]
    ktl = [S.sbuf(f"ktl{c}", [128, 128], BF16) for c in range(8)]
    attm = [S.sbuf(f"attm{c}", [128, 128], BF16) for c in range(8)]
    ktok = [S.sbuf(f"ktok{c}", [128, 128], BF16) for c in range(8)]
    osb = [S.sbuf(f"osb{c}", [128, 2, 128], F32) for c in range(8)]
    b_eq = [Buf() for _ in range(8)]; b_ek = [Buf() for _ in range(8)]; b_qtl = [Buf() for _ in range(8)]
    b_ktl = [Buf() for _ in range(8)]; b_attm = [Buf() for _ in range(8)]; b_ktok = [Buf() for _ in range(8)]
    b_osb = [Buf() for _ in range(8)]
    psA = [S.psum(f"gpA{i}", [128, 512]) for i in range(4)]
    _kv = [S.psum(f"gpKV{i}", [128, 512]) for i in range(2)]
    psKV = [_kv[i % 2][:, 0:256] for i in range(4)]
    _kt = [S.psum(f"gpKT{i}", [128, 1024], BF16) for i in range(2)]
    psKT = [_kt[i % 2][:, 0:128] for i in range(4)]
    b_bc = [Buf() for _ in range(4)]; b_att = b_bc; b_o = b_bc
    _bkv = [Buf(), Buf()]; b_kv = [_bkv[i % 2] for i in range(4)]
    _bkt = [Buf(), Buf()]; b_kt = [_bkt[i % 2] for i in range(4)]
    QTv = k.QT.rearrange("(h p) t -> p h t", p=128)
    KTv = k.KT.rearrange("(h p) t -> p h t", p=128)
    Ov = [k.OF.rearrange("(c p) t -> p c t", p=128), k.OB.rearrange("(c p) t -> p c t", p=128)]
    Gd = [k.GF, k.GB]

    def loads(i):
        for d in range(2):
            tok0 = order[d][i]
            j = i % 2
            S.dma("sp", (lambda e, d=d, j=j, tok0=tok0: e.dma_start(out=gld[d][j][:], in_=Gd[d][tok0:tok0 + 128, :])), reads=[k.b_G], writes=[b_gld[d][j]])
            S.dma("sp", (lambda e, d=d, j=j, tok0=tok0: e.dma_start(out=qld[d][j][:], in_=QTv[:, :, tok0:tok0 + 128])), reads=[k.b_QK], writes=[b_qld[d][j]])
            S.dma("sp", (lambda e, d=d, j=j, tok0=tok0: e.dma_start(out=kld[d][j][:], in_=KTv[:, :, tok0:tok0 + 128])), reads=[k.b_QK], writes=[b_kld[d][j]])
            S.dma("sp", (lambda e, d=d, j=j, tok0=tok0: e.dma_start(out=vld[d][j][:], in_=k.V[tok0:tok0 + 128, :])), reads=[k.b_V], writes=[b_vld[d][j]])

    def step(i):
        if i + 1 < nsteps:
            loads(i + 1)
        j = i % 2
        chains = [(d, h) for d in range(2) for h in range(4)]
        for (d, h) in chains:
            c = d * 4 + h; p = c % 4
            S.op("pe", (lambda e, d=d, h=h, p=p: e.matmul(psA[p][:, 0:128], lhsT=gld[d][j][:, h * 128:(h + 1) * 128], rhs=msk[:, d, :],
                                                         start=True, stop=True)), reads=[b_gld[d][j], b_msk], writes=[b_bc[p]])
            S.op("act", (lambda e, c=c, p=p: e.activation(out=eq[c][:], in_=psA[p][:, 0:128], func=AF.Exp)), reads=[b_bc[p]], writes=[b_eq[c]])
            S.op("act", (lambda e, c=c, p=p: e.activation(out=ek[c][:], in_=psA[p][:, 0:128], func=AF.Exp, scale=-1.0)), reads=[b_bc[p]], writes=[b_ek[c]])
            S.op("dve", (lambda e, c=c, d=d, h=h: e.tensor_tensor(out=qtl[c][:], in0=qld[d][j][:, h, :], in1=eq[c][:], op=ALU.mult)),
                 reads=[b_qld[d][j], b_eq[c]], writes=[b_qtl[c]])
            S.op("dve", (lambda e, c=c, d=d, h=h: e.tensor_tensor(out=ktl[c][:], in0=kld[d][j][:, h, :], in1=ek[c][:], op=ALU.mult)),
                 reads=[b_kld[d][j], b_ek[c]], writes=[b_ktl[c]])
        for (d, h) in chains:
            c = d * 4 + h; p = c % 4
            S.op("pe", (lambda e, c=c, p=p: e.matmul(psA[p][:, 128:256], lhsT=ktl[c][:], rhs=qtl[c][:], start=True, stop=True)),
                 reads=[b_ktl[c], b_qtl[c]], writes=[b_att[p]])
            S.op("dve", (lambda e, c=c, p=p, d=d: e.tensor_tensor(out=attm[c][:], in0=psA[p][:, 128:256], in1=msk[:, d, :], op=ALU.mult)),
                 reads=[b_att[p], b_msk], writes=[b_attm[c]])
            S.op("pe", (lambda e, c=c, p=p: e.transpose(psKT[p], ktl[c][:], k.identb[:])), reads=[b_ktl[c], k.b_const], writes=[b_kt[p]])
            S.op("act", (lambda e, c=c, p=p: e.activation(out=ktok[c][:], in_=psKT[p], func=AF.Copy)), reads=[b_kt[p]], writes=[b_ktok[c]])
        for (d, h) in chains:
            c = d * 4 + h; p = c % 4
            tok0 = order[d][i]
            for vc in range(2):
                S.op("pe", (lambda e, c=c, p=p, vc=vc: e.matmul(psA[p][:, 256 + vc * 128:384 + vc * 128], lhsT=Sb[:, c, vc * 128:(vc + 1) * 128],
                                                               rhs=qtl[c][:], start=True, stop=False)), reads=[b_Sb[c], b_qtl[c]], writes=[b_o[p]])
                S.op("pe", (lambda e, c=c, p=p, vc=vc, d=d, h=h: e.matmul(psA[p][:, 256 + vc * 128:384 + vc * 128],
                                                                         lhsT=vld[d][j][:, h * 256 + vc * 128:h * 256 + (vc + 1) * 128],
                                                                         rhs=attm[c][:], start=False, stop=True)),
                     reads=[b_vld[d][j], b_attm[c]], writes=[b_o[p]])
            S.op("act", (lambda e, c=c, p=p: e.activation(out=osb[c][:], in_=psA[p][:, 256:512].rearrange("p (a b) -> p a b", a=2), func=AF.Copy)),
                 reads=[b_o[p]], writes=[b_osb[c]])
            S.dma("sp", (lambda e, c=c, d=d, h=h, tok0=tok0: e.dma_start(out=Ov[d][:, 2 * h:2 * h + 2, tok0:tok0 + 128], in_=osb[c][:])),
                  reads=[b_osb[c]], writes=[k.b_O])
            S.op("pe", (lambda e, c=c, p=p, d=d, h=h: e.matmul(psKV[p], lhsT=ktok[c][:], rhs=vld[d][j][:, h * 256:(h + 1) * 256],
                                                              start=True, stop=True)), reads=[b_ktok[c], b_vld[d][j]], writes=[b_kv[p]])
            el = eq[c][:, 127:128] if d == 0 else eq[c][:, 0:1]
            S.op("dve", (lambda e, c=c, el=el: e.tensor_scalar(out=St[:, c, :], in0=St[:, c, :], scalar1=el, scalar2=None, op0=ALU.mult)),
                 reads=[b_S[c], b_eq[c]], writes=[b_S[c]])
            S.op("dve", (lambda e, c=c, p=p, el=el: e.scalar_tensor_tensor(out=St[:, c, :], in0=psKV[p], scalar=el, in1=St[:, c, :],
                                                                          op0=ALU.mult, op1=ALU.add)), reads=[b_kv[p], b_S[c], b_eq[c]], writes=[b_S[c]])
            S.op("act", (lambda e, c=c: e.activation(out=Sb[:, c, :], in_=St[:, c, :], func=AF.Copy)), reads=[b_S[c]], writes=[b_Sb[c]])

    loads(0)
    for i in range(nsteps):
        step(i)
    S.end_phase()


def phase_fnet(k):
    S = k.S
    nlat, nctx, ntok = k.nlat, k.nctx, k.ntok
    S.begin_phase()
    cc = S.sbuf("fcc", [128, 3, 2, 256], BF16)
    b_cc = Buf()
    for a_ in range(3):
        S.dma("sp", lambda e, a_=a_: e.dma_start(out=cc[:, a_, :, :], in_=k.dftc[a_].rearrange("(c2 p) c -> p c2 c", p=128)), writes=[b_cc], key=("cc", a_))
    ft = [S.sbuf(f"fft{i}", [128, 8, 128], BF16) for i in range(2)]
    b_ft = [Buf(), Buf()]
    gsb = [S.sbuf(f"fgsb{i}", [128, 2, 1024], BF16) for i in range(2)]
    b_gsb = [Buf(), Buf()]
    ps1 = [S.psum(f"fps1_{i}", [128, 2, 256]) for i in range(4)]
    b_ps1 = [Buf() for _ in range(4)]
    FTv = k.FT.rearrange("(c p) t -> p c t", p=128)
    ntile = ntok // 128
    pc = 0
    for tt in range(ntile):
        tok0 = tt * 128
        f_t, bf = ft[tt % 2], b_ft[tt % 2]
        g_t, bg = gsb[tt % 2], b_gsb[tt % 2]
        S.dma("sp", (lambda e, f_t=f_t, tok0=tok0: e.dma_start(out=f_t[:], in_=FTv[:, :, tok0:tok0 + 128])), reads=[k.b_FT], writes=[bf])
        for g in range(4):
            p_t, bp = ps1[pc % 4], b_ps1[pc % 4]
            pc += 1
            for a in range(2):
                for c2 in range(2):
                    S.op("pe", (lambda e, p_t=p_t, f_t=f_t, g=g, a=a, c2=c2: e.matmul(
                        p_t[:, a, :], lhsT=f_t[:, 2 * g + c2, :], rhs=cc[:, a, c2, :], start=(c2 == 0), stop=(c2 == 1))),
                        reads=[bf, b_cc], writes=[bp])
            S.op("act" if g % 2 == 0 else "dve",
                 (lambda e, p_t=p_t, g_t=g_t, g=g: (e.activation(out=g_t[:, :, g * 256:(g + 1) * 256], in_=p_t[:, :, :], func=AF.Copy)
                                                    if g % 2 == 0 else e.tensor_copy(out=g_t[:, :, g * 256:(g + 1) * 256], in_=p_t[:, :, :]))),
                 reads=[bp], writes=[bg])
        S.dma("pool", (lambda e, g_t=g_t, tok0=tok0: e.dma_start(out=k.GCS[tok0:tok0 + 128, :, :], in_=g_t[:])), reads=[bg], writes=[k.b_GCS])
    S.end_phase()
    S.begin_phase()
    cc = S.sbuf("fcc2", [128, 3, 2, 256], BF16)
    b_cc = Buf()
    for a_ in range(3):
        S.dma("sp", lambda e, a_=a_: e.dma_start(out=cc[:, a_, :, :], in_=k.dftc[a_].rearrange("(c2 p) c -> p c2 c", p=128)), writes=[b_cc], key=("cc", a_))
    ntc = nlat // 128
    G = [S.sbuf("fG0", [128, ntc, 2, 256], BF16)] * 2
    b_G = [Buf()] * 2
    Gc = [S.sbuf(f"fGc{i}", [128, 2, 2, 256], BF16) for i in range(2)]
    b_Gc = [Buf(), Buf()]
    fo = [S.sbuf(f"ffo{i}", [128, 512], BF16) for i in range(2)]
    b_fo = [Buf(), Buf()]
    ps2 = [S.psum(f"fps2_{i}", [128, 512]) for i in range(2)]
    b_ps2 = [Buf(), Buf()]
    ntp = nlat // 512
    srcs = []
    for g in range(4):
        for tp in range(ntp):
            srcs.append([((lambda tt, a_=a_: tt[:, a_, :, :]), k.dftT[a_].rearrange("(tc p) t -> p tc t", p=128)[:, :, tp * 512:(tp + 1) * 512]) for a_ in range(2)])
    ringT = Ring(S, "fT", [128, 2, ntc, 512], BF16, srcs, 2, q="sp", nparts=2)
    FMv = k.FMT.rearrange("(c p) t -> p c t", p=128)
    GCSv = k.GCS.rearrange("(tc p) a c -> p tc a c", p=128)
    ev = 0
    oc_ = 0
    for g in range(4):
        G_t, bG = G[g % 2], b_G[g % 2]
        for a_ in range(2):
            S.dma("sp", (lambda e, G_t=G_t, g=g, a_=a_: e.dma_start(out=G_t[:, :, a_, :], in_=GCSv[:, 0:ntc, a_, g * 256:(g + 1) * 256])), reads=[k.b_GCS], writes=[bG], key=("fG", a_))
        Gc_t, bGc = Gc[g % 2], b_Gc[g % 2]
        for a_ in range(2):
            S.dma("pool", (lambda e, Gc_t=Gc_t, g=g, a_=a_: e.dma_start(out=Gc_t[:, :, a_, :], in_=GCSv[:, ntc:ntc + 2, a_, g * 256:(g + 1) * 256])), reads=[k.b_GCS], writes=[bGc], key=("fGc", a_))
        for tp in range(ntp):
            ringT.prefetch(ev)
            T_t, bT = ringT.get(ev)
            ev += 1
            for c2 in range(2):
                p_t, bp = ps2[oc_ % 2], b_ps2[oc_ % 2]
                f_t, bfo = fo[oc_ % 2], b_fo[oc_ % 2]
                oc_ += 1
                n = 0
                for a in range(2):
                    for tc in range(ntc):
                        S.op("pe", (lambda e, p_t=p_t, G_t=G_t, T_t=T_t, a=a, tc=tc, c2=c2, n=n: e.matmul(
                            p_t[:, :], lhsT=G_t[:, tc, a, c2 * 128:(c2 + 1) * 128], rhs=T_t[:, a, tc, :],
                            start=(n == 0), stop=(n == 2 * ntc - 1))), reads=[bG] + bT, writes=[bp])
                        n += 1
                S.op("act", (lambda e, p_t=p_t, f_t=f_t: e.activation(out=f_t[:], in_=p_t[:, :], func=AF.Copy)), reads=[bp], writes=[bfo])
                S.dma("pool", (lambda e, f_t=f_t, g=g, c2=c2, tp=tp: e.dma_start(out=FMv[:, 2 * g + c2, tp * 512:(tp + 1) * 512], in_=f_t[:])),
                      reads=[bfo], writes=[k.b_FMT])
        for c2 in range(2):
            p_t, bp = ps2[oc_ % 2], b_ps2[oc_ % 2]
            f_t, bfo = fo[oc_ % 2], b_fo[oc_ % 2]
            oc_ += 1
            n = 0
            for a in range(2):
                tab = 0 if a == 0 else 2
                for tc in range(2):
                    S.op("pe", (lambda e, p_t=p_t, Gc_t=Gc_t, a=a, tc=tc, c2=c2, n=n, tab=tab: e.matmul(
                        p_t[:, 0:256], lhsT=Gc_t[:, tc, a, c2 * 128:(c2 + 1) * 128], rhs=cc[:, tab, tc, :],
                        start=(n == 0), stop=(n == 3))), reads=[bGc, b_cc], writes=[bp])
                    n += 1
            S.op("act", (lambda e, p_t=p_t, f_t=f_t: e.activation(out=f_t[:, 0:256], in_=p_t[:, 0:256], func=AF.Copy)), reads=[bp], writes=[bfo])
            S.dma("pool", (lambda e, f_t=f_t, g=g, c2=c2: e.dma_start(out=FMv[:, 2 * g + c2, nlat:nlat + 256], in_=f_t[:, 0:256])),
                  reads=[bfo], writes=[k.b_FMT])
    S.end_phase()


def phase_l0out(k, tiles):
    S = k.S
    TAB, bTAB = k.TAB[0], k.b_TAB[0]
    S.begin_phase()
    XTv = k.XT.rearrange("(kc p) t -> p kc t", p=128)
    Ofv = k.OF.rearrange("(c p) t -> p c t", p=128)
    Obv = k.OB.rearrange("(c p) t -> p c t", p=128)
    RGv = k.RG.rearrange("(c p) t -> p c t", p=128)
    FMv = k.FMT.rearrange("(c p) t -> p c t", p=128)
    ring_wo = Ring(S, "wo", [128, KC, 128], BF16, [k.wout0B[oc] for _ in range(len(tiles)) for oc in range(KC)], 8, q="sp", rbufs=[k.b_wout0B])
    xt = [S.sbuf(f"xt{i}", [128, KC, 512], F32) for i in range(2)]
    b_xt = [Buf(), Buf()]
    of_ = [S.sbuf("of0", [128, 8, 512], F32)] * 2
    b_of = [Buf()] * 2
    ob_ = [S.sbuf("ob0", [128, 8, 512], F32)] * 2
    b_ob = [Buf()] * 2
    rgt = [S.sbuf("rgt0", [128, 8, 512], F32)] * 2
    b_rgt = [Buf()] * 2
    mixin = [S.sbuf(f"mixin{i}", [128, KC, 512], BF16) for i in range(2)]
    b_mo = [Buf(), Buf()]
    b_mf = [Buf(), Buf()]
    nrm = NormCtx(k)
    ps_y = [S.psum(f"ps_y{i}", [128, 512]) for i in range(2)]
    b_y = [Buf(), Buf()]
    nT = len(tiles)

    def loads(ti):
        t0, nt, cls = tiles[ti]
        i = ti % 2
        S.dma("pool", (lambda e: e.dma_start(out=xt[i][:, :, 0:nt], in_=XTv[:, :, t0:t0 + nt])), reads=[k.b_XT], writes=[b_xt[i]])
        S.dma("sp", (lambda e: e.dma_start(out=of_[i][:, :, 0:nt], in_=Ofv[:, :, t0:t0 + nt])), reads=[k.b_O], writes=[b_of[i]])
        S.dma("sp", (lambda e: e.dma_start(out=ob_[i][:, :, 0:nt], in_=Obv[:, :, t0:t0 + nt])), reads=[k.b_O], writes=[b_ob[i]])
        S.dma("sp", (lambda e: e.dma_start(out=rgt[i][:, :, 0:nt], in_=RGv[:, :, t0:t0 + nt])), reads=[k.b_RG], writes=[b_rgt[i]])
        S.dma("pool", (lambda e: e.dma_start(out=mixin[i][:, 8:16, 0:nt], in_=FMv[:, :, t0:t0 + nt])), reads=[k.b_FMT], writes=[b_mf[i]])

    def prep(ti):
        t0, nt, cls = tiles[ti]
        i = ti % 2
        S.op("pool", (lambda e: e.tensor_tensor(out=of_[i][:, :, 0:nt], in0=of_[i][:, :, 0:nt], in1=ob_[i][:, :, 0:nt], op=ALU.add)),
             reads=[b_of[i], b_ob[i]], writes=[b_of[i]])
        for h in range(4):
            nrm.stats(of_[i][:, 2 * h:2 * h + 2, :], b_of[i], nt, nchunks=2, dim=256)
            for vc in range(2):
                c = 2 * h + vc
                S.op("dve", (lambda e, c=c: e.tensor_tensor(out=rgt[i][:, c, 0:nt], in0=rgt[i][:, c, 0:nt], in1=nrm.rstd[:, 0:nt], op=ALU.mult)),
                     reads=[b_rgt[i], nrm.b_rstd], writes=[b_rgt[i]])
                S.op("dve", (lambda e, c=c: e.tensor_tensor(out=mixin[i][:, c, 0:nt], in0=rgt[i][:, c, 0:nt], in1=of_[i][:, c, 0:nt], op=ALU.mult)),
                     reads=[b_rgt[i], b_of[i]], writes=[b_mo[i]])

    def outp(ti):
        t0, nt, cls = tiles[ti]
        i = ti % 2
        for oc in range(KC):
            py, by = ps_y[oc % 2], b_y[oc % 2]
            evn = ti * KC + oc
            ring_wo.prefetch(evn)
            wo, b_wo = ring_wo.get(evn)
            for kc in range(KC):
                S.op("pe", (lambda e, py=py, wo=wo, kc=kc: e.matmul(py[:, 0:nt], lhsT=wo[:, kc, :], rhs=mixin[i][:, kc, 0:nt],
                                                                   start=(kc == 0), stop=(kc == KC - 1))),
                     reads=[b_wo, b_mo[i], b_mf[i]], writes=[by])
            S.op("dve", (lambda e, py=py, oc=oc: e.scalar_tensor_tensor(
                out=xt[i][:, oc, 0:nt], in0=py[:, 0:nt], scalar=TAB[:, cls, 5, oc:oc + 1], in1=xt[i][:, oc, 0:nt],
                op0=ALU.mult, op1=ALU.add)), reads=[by, b_xt[i], bTAB], writes=[b_xt[i]])
        S.dma("pool", (lambda e: e.dma_start(out=XTv[:, :, t0:t0 + nt], in_=xt[i][:, :, 0:nt])), reads=[b_xt[i]], writes=[k.b_XT])

    loads(0)
    prep(0)
    for ti in range(nT):
        if ti + 1 < nT:
            loads(ti + 1)
            prep(ti + 1)
        outp(ti)
    S.end_phase()


NEG = -30000.0


def na_geometry(rows):
    kh = min(8, rows)
    nch = min(5, rows // 2)
    classes = {}
    rp_cls, rp_lo = [], []
    for rp in range(rows // 2):
        r = 2 * rp
        lo = int(np.clip(r - 4, 0, rows - 2 * nch))
        key = []
        for j in range(nch):
            for a in range(2):
                for b in range(2):
                    kr = lo + 2 * j + a
                    rq = r + b
                    r0 = int(np.clip(rq - kh // 2, 0, rows - kh))
                    valid = (r0 <= kr < r0 + kh)
                    key.append((valid, kr - rq + 7 if valid else 0))
        key = tuple(key)
        if key not in classes:
            classes[key] = len(classes)
        rp_cls.append(classes[key])
        rp_lo.append(lo)
    return kh, nch, classes, rp_cls, rp_lo


def na_bias_tables(rpb, rows):
    kh, nch, classes, rp_cls, rp_lo = na_geometry(rows)
    H = rpb.shape[0]
    c = np.arange(64)
    col_start = np.clip(c - 8, 0, 48)
    cmask = (c[None, :] >= col_start[:, None]) & (c[None, :] < col_start[:, None] + 16)
    dc = np.clip(c[None, :] - c[:, None] + 15, 0, 30)
    out = np.full((H, len(classes), 128, nch * 128), NEG, np.float32)
    for key, cid in classes.items():
        n = 0
        for j in range(nch):
            for a in range(2):
                for b in range(2):
                    valid, dr = key[n]
                    n += 1
                    if not valid:
                        continue
                    blk = rpb[:, dr][:, dc]
                    blk = np.where(cmask[None], blk, NEG)
                    out[:, cid, b * 64:(b + 1) * 64, j * 128 + a * 64:j * 128 + (a + 1) * 64] = blk
    return out


def precast_l1(k):
    k.wqkvFM, k.b_wqkvFM = precast_blocks(k, "wqkvFM", k.w_qkv, D, [128 * i for i in range(32)], 128)
    k.wqkvV, k.b_wqkvV = precast_blocks(k, "wqkvV", k.w_qkv, D, [2 * D + 512 * i for i in range(4)], 512)
    k.wout1B, k.b_wout1B = precast_blocks(k, "wout1B", k.w_out1, D, [128 * o for o in range(KC)], 128)


def phase_l1proj(k, tiles):
    S = k.S
    TAB, bTAB = k.TAB[1], k.b_TAB[1]
    S.begin_phase()
    XTv = k.XT.rearrange("(kc p) t -> p kc t", p=128)
    xt = S.sbuf("xt0", [128, KC, 512], F32)
    b_xt = Buf()
    uts = [S.sbuf(f"ut{i}", [128, KC, 512], BF16) for i in range(2)]
    b_uts = [Buf(), Buf()]
    nrm = NormCtx(k)
    ob = [S.sbuf(f"ob{i}", [128, 512], BF16) for i in range(4)]
    b_ob = [Buf() for _ in range(4)]
    vt = [S.sbuf(f"vt{i}", [128, 2048], BF16) for i in range(2)]
    b_vt = [Buf(), Buf()]
    ps_fm = [S.psum(f"ps_fm{i}", [128, 512]) for i in range(3)]
    b_fm = [Buf() for _ in range(3)]
    ps_v = [S.psum(f"ps_v{i}", [128, 512]) for i in range(4)]
    b_v = [Buf() for _ in range(4)]
    nT = len(tiles)
    fm_srcs, v_srcs = [], []
    for ti in range(nT):
        t0, nt, cls = tiles[ti]
        for nb in range(32):
            if cls == 1 and nb < 16:
                continue
            fm_srcs.append(k.wqkvFM[nb])
        for s_ in range(nt // 128):
            v_srcs += [k.wqkvV[i] for i in range(4)]
    ring_fm = Ring(S, "wfm", [128, KC, 128], BF16, fm_srcs, 8, q="sp", rbufs=[k.b_wqkvFM])
    wv_res = S.sbuf("wvres", [128, 4, KC, 512], BF16)
    b_wv = [Buf() for _ in range(4)]
    for hf in range(4):
        S.dma("sp", (lambda e, hf=hf: e.dma_start(out=wv_res[:, hf, :, :], in_=k.wqkvV[hf])), reads=[k.b_wqkvV], writes=[b_wv[hf]])
    QTv = k.QT1.rearrange("(c p) t -> p c t", p=128)
    KTv = k.KT1.rearrange("(c p) t -> p c t", p=128)
    fm_ev, v_ev, pc = [0], [0], [0]
    def loadnorm(ti):
        t0, nt, cls = tiles[ti]
        S.dma("pool", (lambda e, t0=t0, nt=nt: e.dma_start(out=xt[:, :, 0:nt], in_=XTv[:, :, t0:t0 + nt])), reads=[k.b_XT], writes=[b_xt])
        nrm.stats(xt, b_xt, nt)
        nrm.apply(xt, b_xt, nt, TAB[:, cls, 3, :], TAB[:, cls, 4, :], bTAB, uts[ti % 2], b_uts[ti % 2])

    loadnorm(0)
    for ti in range(nT):
        t0, nt, cls = tiles[ti]
        ut, b_ut = uts[ti % 2], b_uts[ti % 2]
        ndone = 0
        for nb in range(32):
            if cls == 1 and nb < 16:
                continue
            ring_fm.prefetch(fm_ev[0])
            w_t, bw = ring_fm.get(fm_ev[0]); fm_ev[0] += 1
            i = pc[0] % 3; o_i = pc[0] % 4; pc[0] += 1
            p_t, bp = ps_fm[i], b_fm[i]
            for kc in range(KC):
                S.op("pe", (lambda e, p_t=p_t, w_t=w_t, kc=kc, nt=nt, ut=ut: e.matmul(
                    p_t[:, 0:nt], lhsT=w_t[:, kc, :], rhs=ut[:, kc, 0:nt], start=(kc == 0), stop=(kc == KC - 1))),
                    reads=[bw, b_ut], writes=[bp])
            sc = 0.125 if nb < 16 else 1.0
            if pc[0] % 2 == 0:
                S.op("act", (lambda e, p_t=p_t, o_i=o_i, nt=nt, sc=sc: e.activation(out=ob[o_i][:, 0:nt], in_=p_t[:, 0:nt], func=AF.Identity, scale=sc)),
                     reads=[bp], writes=[b_ob[o_i]])
            else:
                S.op("dve", (lambda e, p_t=p_t, o_i=o_i, nt=nt, sc=sc: e.tensor_scalar(out=ob[o_i][:, 0:nt], in0=p_t[:, 0:nt], scalar1=sc, scalar2=None, op0=ALU.mult)),
                     reads=[bp], writes=[b_ob[o_i]])
            dst = QTv[:, nb, t0:t0 + nt] if nb < 16 else KTv[:, nb - 16, t0:t0 + nt]
            S.dma("pool", (lambda e, o_i=o_i, dst=dst, nt=nt: e.dma_start(out=dst, in_=ob[o_i][:, 0:nt])), reads=[b_ob[o_i]], writes=[k.b_QK1])
            ndone += 1
            if ndone == 12 and ti + 1 < nT:
                loadnorm(ti + 1)
        for s_ in range(nt // 128):
            tok0 = t0 + s_ * 128
            v_t, bv = vt[s_ % 2], b_vt[s_ % 2]
            for q4 in range(4):
                w_t, bw = wv_res[:, q4, :, :], b_wv[q4]
                pv, bpv = ps_v[q4], b_v[q4]
                for kc in range(KC):
                    S.op("pe", (lambda e, pv=pv, w_t=w_t, kc=kc, s_=s_, ut=ut: e.matmul(
                        pv[:, :], lhsT=ut[:, kc, s_ * 128:(s_ + 1) * 128], rhs=w_t[:, kc, :], start=(kc == 0), stop=(kc == KC - 1))),
                        reads=[bw, b_ut], writes=[bpv])
                if q4 % 2 == 0:
                    S.op("act", (lambda e, pv=pv, v_t=v_t, q4=q4: e.activation(out=v_t[:, q4 * 512:(q4 + 1) * 512], in_=pv[:, :], func=AF.Copy)),
                         reads=[bpv], writes=[bv])
                else:
                    S.op("dve", (lambda e, pv=pv, v_t=v_t, q4=q4: e.tensor_copy(out=v_t[:, q4 * 512:(q4 + 1) * 512], in_=pv[:, :])),
                         reads=[bpv], writes=[bv])
            S.dma("pool", (lambda e, v_t=v_t, tok0=tok0: e.dma_start(out=k.V1[tok0:tok0 + 128, :], in_=v_t[:])), reads=[bv], writes=[k.b_V1])
    S.end_phase()


def phase_na(k):
    S = k.S
    nlat, nctx, ntok = k.nlat, k.nctx, k.ntok
    rows = nlat // 64
    kh, nch, classes, rp_cls, rp_lo = na_geometry(rows)
    ncls = len(classes)
    nrp = rows // 2
    nlk = nch * 128
    nk = nlk + nctx
    nkc = nk // 128
    ntile = ntok // 128
    S.begin_phase()
    qT = [S.sbuf(f"naq{i}", [128, nlat], BF16) for i in range(2)]
    kT = [S.sbuf(f"nak{i}", [128, ntok], BF16) for i in range(2)]
    Vh = [S.sbuf(f"nav{i}", [128, ntile, 128], BF16) for i in range(2)]
    TB = [S.sbuf(f"natb{i}", [128, 2, ncls, nlk], BF16) for i in range(2)]
    Op = [S.sbuf(f"nao{i}", [128, nrp, 128], BF16) for i in range(2)]
    b_qT = [Buf(), Buf()]; b_kT = [Buf(), Buf()]; b_Vh = [Buf(), Buf()]; b_TB = [[Buf(), Buf()], [Buf(), Buf()]]; b_Op = [Buf(), Buf()]
    P = [S.sbuf(f"nap{i}", [128, nk], BF16) for i in range(3)]
    b_P = [Buf() for _ in range(3)]
    PT = [S.sbuf(f"napt{i}", [128, nkc, 128], BF16) for i in range(3)]
    b_PT = [Buf() for _ in range(3)]
    st = [S.sbuf(f"nast{i}", [128, 4], F32) for i in range(3)]
    b_st = [Buf() for _ in range(3)]
    psS = [S.psum(f"napsS{i}", [128, 1024]) for i in range(2)]
    b_S = [Buf(), Buf()]
    psT = [S.psum(f"napsT{i}", [128, 1024], BF16) for i in range(2)]
    b_T = [Buf(), Buf()]
    psO = [S.psum(f"napsO{i}", [128, 512]) for i in range(2)]
    b_O = [Buf(), Buf()]
    QTv = k.QT1.rearrange("(c p) t -> p c t", p=128)
    KTv = k.KT1.rearrange("(c p) t -> p c t", p=128)
    V1v = k.V1.rearrange("(tt p) c -> p tt c", p=128)
    O1v = k.O1.rearrange("(rp p) c -> p rp c", p=128)

    def loads(hp):
        i = hp % 2
        S.dma("sp", (lambda e: e.dma_start(out=qT[i][:], in_=QTv[:, hp, :])), reads=[k.b_QK1], writes=[b_qT[i]])
        S.dma("sp", (lambda e: e.dma_start(out=kT[i][:], in_=KTv[:, hp, :])), reads=[k.b_QK1], writes=[b_kT[i]])
        S.dma("sp", (lambda e: e.dma_start(out=Vh[i][:], in_=V1v[:, :, hp * 128:(hp + 1) * 128])), reads=[k.b_V1], writes=[b_Vh[i]])
        for hh in range(2):
            S.dma("pool", (lambda e, hh=hh: e.dma_start(out=TB[i][:, hh, :, :], in_=k.nabias[2 * hp + hh].rearrange("c q n -> q c n"))),
                  writes=[b_TB[i][hh]])

    NB = 3

    def stageA(u, hp, hh, rp):
        i = hp % 2
        pb = 64 * hh
        lo = rp_lo[rp]
        cls = rp_cls[rp]
        kbase = lo * 64
        q_ap = qT[i][pb:pb + 64, rp * 128:(rp + 1) * 128]
        Sp, bS = psS[u % 2], b_S[u % 2]
        n0 = min(512, nlk)
        S.op("pe", (lambda e: e.matmul(Sp[:, 0:n0], lhsT=q_ap, rhs=kT[i][pb:pb + 64, kbase:kbase + n0], start=True, stop=False)),
             reads=[b_qT[i], b_kT[i]], writes=[bS])
        S.op("pe", (lambda e: e.matmul(Sp[:, 0:n0], lhsT=k.identb[:], rhs=TB[i][:, hh, cls, 0:n0], start=False, stop=True)),
             reads=[b_TB[i][hh], k.b_const], writes=[bS])
        if nlk > 512:
            S.op("pe", (lambda e: e.matmul(Sp[:, 512:nlk], lhsT=q_ap, rhs=kT[i][pb:pb + 64, kbase + 512:kbase + nlk], start=True, stop=False)),
                 reads=[b_qT[i], b_kT[i]], writes=[bS])
            S.op("pe", (lambda e: e.matmul(Sp[:, 512:nlk], lhsT=k.identb[:], rhs=TB[i][:, hh, cls, 512:nlk], start=False, stop=True)),
                 reads=[b_TB[i][hh], k.b_const], writes=[bS])
            c0 = nlk
        else:
            c0 = 512
        S.op("pe", (lambda e: e.matmul(Sp[:, c0:c0 + nctx], lhsT=q_ap, rhs=kT[i][pb:pb + 64, nlat:nlat + nctx], start=True, stop=True)),
             reads=[b_qT[i], b_kT[i]], writes=[bS])
        single = (c0 == nlk)
        s_t, bst = st[u % NB], b_st[u % NB]
        if single:
            S.op("dve", (lambda e: e.tensor_reduce(out=s_t[:, 0:1], in_=Sp[:, 0:nk], axis=AX.X, op=ALU.max)), reads=[bS], writes=[bst])
        else:
            S.op("dve", (lambda e: e.tensor_reduce(out=s_t[:, 0:1], in_=Sp[:, 0:nlk], axis=AX.X, op=ALU.max)), reads=[bS], writes=[bst])
            S.op("dve", (lambda e: e.tensor_reduce(out=s_t[:, 1:2], in_=Sp[:, c0:c0 + nctx], axis=AX.X, op=ALU.max)), reads=[bS], writes=[bst])
            S.op("dve", (lambda e: e.tensor_tensor(out=s_t[:, 0:1], in0=s_t[:, 0:1], in1=s_t[:, 1:2], op=ALU.max)), reads=[bst], writes=[bst])
        S.op("dve", (lambda e: e.tensor_scalar(out=s_t[:, 1:2], in0=s_t[:, 0:1], scalar1=-1.0, scalar2=None, op0=ALU.mult)), reads=[bst], writes=[bst])
        P_t, bP = P[u % NB], b_P[u % NB]
        if single:
            S.op("act", (lambda e: e.activation(out=P_t[:, 0:nk], in_=Sp[:, 0:nk], func=AF.Exp, bias=s_t[:, 1:2], scale=1.0, accum_out=s_t[:, 2:3])),
                 reads=[bS, bst], writes=[bP, bst])
        else:
            S.op("act", (lambda e: e.activation(out=P_t[:, 0:nlk], in_=Sp[:, 0:nlk], func=AF.Exp, bias=s_t[:, 1:2], scale=1.0, accum_out=s_t[:, 2:3])),
                 reads=[bS, bst], writes=[bP, bst])
            S.op("act", (lambda e: e.activation(out=P_t[:, nlk:nk], in_=Sp[:, c0:c0 + nctx], func=AF.Exp, bias=s_t[:, 1:2], scale=1.0, accum_out=s_t[:, 3:4])),
                 reads=[bS, bst], writes=[bP, bst])
            S.op("dve", (lambda e: e.tensor_tensor(out=s_t[:, 2:3], in0=s_t[:, 2:3], in1=s_t[:, 3:4], op=ALU.add)), reads=[bst], writes=[bst])

    def stageB(u, hp, hh, rp):
        s_t, bst = st[u % NB], b_st[u % NB]
        P_t, bP = P[u % NB], b_P[u % NB]
        S.op("dve", (lambda e: e.reciprocal(out=s_t[:, 3:4], in_=s_t[:, 2:3])), reads=[bst], writes=[bst])
        Tp, bT = psT[u % 2], b_T[u % 2]
        for j in range(nkc):
            S.op("pe", (lambda e, j=j: e.transpose(Tp[:, j * 128:(j + 1) * 128], P_t[:, j * 128:(j + 1) * 128], k.identb[:])),
                 reads=[bP, k.b_const], writes=[bT])
        PT_t, bPT = PT[u % NB], b_PT[u % NB]
        if u % 2 == 0:
            S.op("act", (lambda e: e.activation(out=PT_t[:].rearrange("p a b -> p (a b)"), in_=Tp[:, 0:nk], func=AF.Copy)), reads=[bT], writes=[bPT])
        else:
            S.op("dve", (lambda e: e.tensor_copy(out=PT_t[:].rearrange("p a b -> p (a b)"), in_=Tp[:, 0:nk])), reads=[bT], writes=[bPT])

    def stageC(u, hp, hh, rp):
        i = hp % 2
        pb = 64 * hh
        kbase = rp_lo[rp] * 64
        s_t, bst = st[u % NB], b_st[u % NB]
        PT_t, bPT = PT[u % NB], b_PT[u % NB]
        Opp, bO = psO[u % 2], b_O[u % 2]
        for j in range(nkc):
            tt = (kbase // 128 + j) if j < nch else (nlat // 128 + (j - nch))
            S.op("pe", (lambda e, j=j, tt=tt: e.matmul(Opp[:, 0:64], lhsT=PT_t[:, j, :], rhs=Vh[i][:, tt, pb:pb + 64], start=(j == 0), stop=(j == nkc - 1))),
                 reads=[bPT, b_Vh[i]], writes=[bO])
        S.op("dve", (lambda e: e.tensor_scalar(out=Op[i][:, rp, pb:pb + 64], in0=Opp[:, 0:64], scalar1=s_t[:, 3:4], scalar2=None, op0=ALU.mult)),
             reads=[bO, bst], writes=[b_Op[i]])
        if rp == nrp - 1 and hh == 1:
            S.dma("sp", (lambda e: e.dma_start(out=O1v[:, :, hp * 128:(hp + 1) * 128], in_=Op[i][:])), reads=[b_Op[i]], writes=[k.b_O1])

    units = [(hp, hh, rp) for hp in range(16) for rp in range(nrp) for hh in range(2)]
    loads(0)
    nU = len(units)
    for n in range(nU + 2):
        if n < nU:
            stageA(n, *units[n])
        if 0 <= n - 1 < nU:
            stageB(n - 1, *units[n - 1])
        if 0 <= n - 2 < nU:
            stageC(n - 2, *units[n - 2])
        if n < nU:
            hp, hh, rp = units[n]
            if n == hp * 2 * nrp + 1 and hp + 1 < 16:
                loads(hp + 1)
    S.end_phase()


def phase_l1out(k, tiles):
    S = k.S
    TAB, bTAB = k.TAB[1], k.b_TAB[1]
    S.begin_phase()
    XTv = k.XT.rearrange("(kc p) t -> p kc t", p=128)
    xt = [S.sbuf(f"xt{i}", [128, KC, 512], F32) for i in range(2)]
    b_xt = [Buf(), Buf()]
    otok = [S.sbuf(f"otok{i}", [128, D], BF16) for i in range(2)]
    b_otok = [Buf(), Buf()]
    mixin = [S.sbuf(f"mixin{i}", [128, KC, 512], BF16) for i in range(2)]
    b_mi = [Buf(), Buf()]
    ps_tr = [S.psum(f"ps_tr{i}", [128, 1024], BF16) for i in range(2)]
    b_tr = [Buf(), Buf()]
    ps_y = [S.psum(f"ps_y{i}", [128, 512]) for i in range(2)]
    b_y = [Buf(), Buf()]
    nT = len(tiles)
    ring_wo = Ring(S, "wo", [128, KC, 128], BF16, [k.wout1B[oc] for _ in range(nT) for oc in range(KC)], 8, q="sp", rbufs=[k.b_wout1B])
    cnt = [0]

    def prep(ti):
        t0, nt, cls = tiles[ti]
        i = ti % 2
        S.dma("pool", (lambda e: e.dma_start(out=xt[i][:, :, 0:nt], in_=XTv[:, :, t0:t0 + nt])), reads=[k.b_XT], writes=[b_xt[i]])
        for s_ in range(nt // 128):
            o_t, bo = otok[s_ % 2], b_otok[s_ % 2]
            tok0 = t0 + s_ * 128
            S.dma("pool", (lambda e, o_t=o_t, tok0=tok0: e.dma_start(out=o_t[:], in_=k.O1[tok0:tok0 + 128, :])), reads=[k.b_O1], writes=[bo])
            for kq in range(KC // 4):
                p_t, bp = ps_tr[cnt[0] % 2], b_tr[cnt[0] % 2]
                cnt[0] += 1
                for k4 in range(4):
                    kc = kq * 4 + k4
                    S.op("pe", (lambda e, p_t=p_t, o_t=o_t, kc=kc, k4=k4: e.transpose(p_t[:, k4 * 128:(k4 + 1) * 128], o_t[:, kc * 128:(kc + 1) * 128], k.identb[:])),
                         reads=[bo, k.b_const], writes=[bp])
                eng = "act" if kq % 2 == 0 else "dve"
                S.op(eng, (lambda e, p_t=p_t, kq=kq, s_=s_, eng=eng: (
                    e.activation(out=mixin[i][:, kq * 4:(kq + 1) * 4, s_ * 128:(s_ + 1) * 128], in_=p_t[:, 0:512].rearrange("p (a b) -> p a b", a=4), func=AF.Copy)
                    if eng == "act" else
                    e.tensor_copy(out=mixin[i][:, kq * 4:(kq + 1) * 4, s_ * 128:(s_ + 1) * 128], in_=p_t[:, 0:512].rearrange("p (a b) -> p a b", a=4)))),
                    reads=[bp], writes=[b_mi[i]])

    def outp(ti):
        t0, nt, cls = tiles[ti]
        i = ti % 2
        for oc in range(KC):
            py, by = ps_y[oc % 2], b_y[oc % 2]
            evn = ti * KC + oc
            ring_wo.prefetch(evn)
            wo, b_wo = ring_wo.get(evn)
            for kc in range(KC):
                S.op("pe", (lambda e, py=py, wo=wo, kc=kc: e.matmul(py[:, 0:nt], lhsT=wo[:, kc, :], rhs=mixin[i][:, kc, 0:nt],
                                                                   start=(kc == 0), stop=(kc == KC - 1))),
                     reads=[b_wo, b_mi[i]], writes=[by])
            S.op("dve", (lambda e, py=py, oc=oc: e.scalar_tensor_tensor(
                out=xt[i][:, oc, 0:nt], in0=py[:, 0:nt], scalar=TAB[:, cls, 5, oc:oc + 1], in1=xt[i][:, oc, 0:nt],
                op0=ALU.mult, op1=ALU.add)), reads=[by, b_xt[i], bTAB], writes=[b_xt[i]])
        S.dma("pool", (lambda e: e.dma_start(out=XTv[:, :, t0:t0 + nt], in_=xt[i][:, :, 0:nt])), reads=[b_xt[i]], writes=[k.b_XT])

    prep(0)
    for ti in range(nT):
        if ti + 1 < nT:
            prep(ti + 1)
        outp(ti)
    S.end_phase()


def tvec(v, n=KC):
    return np.ascontiguousarray(np.asarray(v, np.float32).reshape(n, 128).T)


def host_inputs(inp, b, nlat=NLAT, nctx=NCTX):
    m = {}
    m["x"] = np.ascontiguousarray(inp["x"][b, :nlat])
    m["ctx"] = np.ascontiguousarray(inp["ctx"][b, :nctx])
    cv = np.stack([tvec(inp["c"][b]), tvec(inp["c_ctx"])], axis=-1)
    m["cvec"] = np.ascontiguousarray(cv)
    for l in range(2):
        p = f"l{l}_"
        m[p + "ada_w"] = inp[p + "ada_w"]
        m[p + "ada_b_t"] = tvec(inp[p + "ada_b"], 144)
        m[p + "norms_t"] = np.ascontiguousarray(np.concatenate(
            [tvec(inp[p + "norm_ffn1"]), tvec(inp[p + "norm_mix"]), tvec(inp[p + "norm_ffn2"])], axis=1))
        for n in ["ffn1_w_gu", "ffn2_w_gu", "ffn1_w_down", "ffn2_w_down"]:
            m[p + n] = inp[p + n]
    m["norm_out_t"] = tvec(inp["norm_out"])
    m["l0_w_in"] = inp["l0_w_in"]
    gw = np.zeros((33, 2, 512), np.float32)
    gw[0:16, 0] = inp["l0_gla_gate_w_fwd"]; gw[32, 0] = inp["l0_gla_gate_b_fwd"]
    gw[16:32, 1] = inp["l0_gla_gate_w_bwd"]; gw[32, 1] = inp["l0_gla_gate_b_bwd"]
    m["l0_gw_aug"] = gw
    m["l0_gnorm_t"] = tvec(inp["l0_gla_norm"], 8)
    m["l0_w_out"] = inp["l0_w_out"]
    m["l1_w_qkv"] = inp["l1_w_qkv"]
    m["l1_w_out"] = inp["l1_w_out"]
    m["nabias"] = na_bias_tables(np.asarray(inp["l1_rpb"], np.float32), nlat // 64)
    m.update(host_consts(nlat, nctx))
    return m


def _bf16(a):
    import ml_dtypes
    return np.ascontiguousarray(a.astype(ml_dtypes.bfloat16))


_CONST = {}


def host_consts(nlat=NLAT, nctx=NCTX):
    key = (nlat, nctx)
    if key in _CONST:
        return _CONST[key]
    c = {}
    ntok = nlat + nctx
    t = np.arange(nlat)
    row, col = t // 64, t % 64
    inv = (10000.0 ** (-np.arange(32, dtype=np.float64) / 32.0))
    d = np.arange(128)
    pos = np.where(d[:, None] < 64, row[None, :], col[None, :]).astype(np.float64)
    ang = pos * inv[d % 32][:, None]
    cos, sin = np.cos(ang), np.sin(ang)
    sgn = np.where((d % 64) < 32, -1.0, 1.0)[:, None]
    sins = sin * sgn
    cosf = np.concatenate([cos, np.ones((128, nctx))], 1)
    sinf = np.concatenate([sins, np.zeros((128, nctx))], 1)
    sc = 128.0 ** -0.5
    c["rope"] = np.ascontiguousarray(np.stack([cosf * sc, sinf * sc, cosf, sinf]).astype(np.float32))
    i = np.arange(256)
    a = 2 * np.pi * ((i[:, None] * i[None, :]) % 256) / 256.0
    c["dftc"] = _bf16(np.stack([np.cos(a) / 16, np.sin(a) / 16, -np.sin(a) / 16]))
    i = np.arange(nlat, dtype=np.int64)
    a = 2 * np.pi * ((i[:, None] * i[None, :]) % nlat) / float(nlat)
    c["dftT"] = _bf16(np.stack([np.cos(a) / math.sqrt(nlat), -np.sin(a) / math.sqrt(nlat)]))
    _CONST[key] = c
    return c


_NC_CACHE = {}


def kernel(**inputs):
    inp = {k_: np.asarray(v) for k_, v in inputs.items()}
    B = inp["x"].shape[0]
    if "nc" not in _NC_CACHE:
        _NC_CACHE["nc"] = build(dict(nlat=NLAT, nctx=NCTX))
    nc = _NC_CACHE["nc"]
    shared = None
    in_maps = []
    for b in range(B):
        m = host_inputs(inp, b, NLAT, NCTX)
        if shared is None:
            shared = m
        else:
            for k_ in m:
                if k_ not in ("x", "ctx", "cvec"):
                    m[k_] = shared[k_]
        in_maps.append(m)
    res = run_bass_kernel_spmd(nc, in_maps, core_ids=list(range(B)))
    out = np.stack([np.asarray(res.results[b]["out"]) for b in range(B)], axis=0)
    return out.astype(np.float32)
```
